# Optimizing a Trainium2 kernel written in Bass

```python
import jax, jax.numpy as jnp
from jax import lax
import numpy as np

D_MODEL = 1024
BATCH = 32
SEQ = 256
DEPTH = 4
DEC_BATCH = 2
DEC_SEQ = 2048
PAST_LEN = 512

GRID_W = 64
Q_BLOCK = 128
ROPE_BASE = 10000.0
EPS = 1e-6
N_BRANCH = 4
BRANCH_DIM = 256
RWKV_HEADS = 4
RWKV_HEAD_DIM = 64
RWKV_DIM = RWKV_HEADS * RWKV_HEAD_DIM
RWKV_DECAY_RANK = 64
RWKV_ICLR_RANK = 64
RWKV_GATE_RANK = 128
RWKV_GN_EPS = 64e-5
MLA_HEADS = 4
MLA_NOPE = 64
MLA_ROPE = 32
MLA_VDIM = 64
MLA_Q_RANK = 256
MLA_KV_RANK = 128
GQA_HEADS = 4
GQA_KV_HEADS = 2
GQA_HEAD_DIM = 64
NA_HEADS = 4
NA_HEAD_DIM = 64
NA_DIM = NA_HEADS * NA_HEAD_DIM
NA_WIN_H = 8
NA_WIN_W = 16
FFN_DIM = ((8 * D_MODEL + 3 * 256 - 1) // (3 * 256)) * 256
IN_WIDTHS = (RWKV_DIM, RWKV_DIM, RWKV_DIM, 2 * RWKV_DECAY_RANK, 2 * RWKV_ICLR_RANK, RWKV_GATE_RANK,
             MLA_Q_RANK, MLA_KV_RANK, MLA_ROPE,
             GQA_HEADS * GQA_HEAD_DIM, GQA_KV_HEADS * GQA_HEAD_DIM, GQA_KV_HEADS * GQA_HEAD_DIM,
             NA_DIM, NA_DIM, NA_DIM,
             N_BRANCH * D_MODEL)
N_IN = sum(IN_WIDTHS)

kernel_name = 'hybrid_dit_prefix_context_step'


def rms_norm(x, g):
    xf = x.astype(jnp.float32)
    y = xf * lax.rsqrt(jnp.mean(xf * xf, axis=-1, keepdims=True) + EPS)
    return y.astype(x.dtype) * g


def split_columns(z):
    offsets = np.cumsum(IN_WIDTHS)[:-1].tolist()
    return jnp.split(z, offsets, axis=-1)


def modulation(c_vec, w_mod, b_mod):
    m = (jax.nn.silu(c_vec) @ w_mod + b_mod)[:, None, :]
    return jnp.split(m, 6, axis=-1)


def axial_rope(n_tokens, rot_dim, dtype):
    t = jnp.arange(n_tokens)
    pos = jnp.stack([t // GRID_W, t % GRID_W], axis=-1).astype(jnp.float32)
    n_freq = rot_dim // 4
    inv = ROPE_BASE ** (-jnp.arange(n_freq, dtype=jnp.float32) / n_freq)
    ang = (pos[:, :, None] * inv).reshape(n_tokens, rot_dim // 2)
    return jnp.cos(ang).astype(dtype), jnp.sin(ang).astype(dtype)


def apply_rope(x, cos, sin):
    half = x.shape[-1] // 2
    x1, x2 = x[..., :half], x[..., half:]
    c = cos[None, :, None, :]
    s = sin[None, :, None, :]
    return jnp.concatenate([x1 * c - x2 * s, x1 * s + x2 * c], axis=-1)


def blocked_attention(q, kv_sets, scale):
    Bt, T, KVH, G, Dk = q.shape
    nb = T // Q_BLOCK
    offs = np.cumsum([k.shape[1] for k, _ in kv_sets])[:-1].tolist()
    q_blocks = q.reshape(Bt, nb, Q_BLOCK, KVH, G, Dk).swapaxes(0, 1)

    def one_block(qb):
        s = jnp.concatenate([jnp.einsum('bqkgd,bskd->bkgqs', qb, k) for k, _ in kv_sets], axis=-1)
        p = jax.nn.softmax(s.astype(jnp.float32) * scale, axis=-1).astype(q.dtype)
        o = None
        for pi, (_, v) in zip(jnp.split(p, offs, axis=-1), kv_sets):
            term = jnp.einsum('bkgqs,bskd->bqkgd', pi, v)
            o = term if o is None else o + term
        return o

    o = lax.map(one_block, q_blocks)
    return o.swapaxes(0, 1).reshape(Bt, T, -1)


def neighbourhood_attention(q, k, v, k_ctx, v_ctx, rel_bias):
    Bt, T, H, Dh = q.shape
    rows = T // GRID_W
    win_h = min(NA_WIN_H, rows)
    win_w = NA_WIN_W
    n_win = win_h * win_w
    scale = Dh ** -0.5
    k_grid = k.reshape(Bt, rows, GRID_W, H, Dh)
    v_grid = v.reshape(Bt, rows, GRID_W, H, Dh)
    cols = jnp.arange(GRID_W)
    c0 = jnp.clip(cols - win_w // 2, 0, GRID_W - win_w)
    col_idx = c0[:, None] + jnp.arange(win_w)[None, :]
    bias_cols = rel_bias[:, :, col_idx - cols[:, None] + NA_WIN_W - 1]
    q_rows = q.reshape(Bt, rows, GRID_W, H, Dh).swapaxes(0, 1)

    def one_row(args):
        r, q_r = args
        r0 = jnp.clip(r - win_h // 2, 0, rows - win_h)
        k_win = lax.dynamic_slice_in_dim(k_grid, r0, win_h, axis=1)[:, :, col_idx]
        v_win = lax.dynamic_slice_in_dim(v_grid, r0, win_h, axis=1)[:, :, col_idx]
        bias = jnp.take(bias_cols, r0 + jnp.arange(win_h) - r + NA_WIN_H - 1, axis=1)
        s_win = (jnp.einsum('bqhd,brqwhd->bhqrw', q_r, k_win).astype(jnp.float32) * scale
                 + bias.transpose(0, 2, 1, 3)[None].astype(jnp.float32))
        s_ctx = jnp.einsum('bqhd,blhd->bhql', q_r, k_ctx).astype(jnp.float32) * scale
        s = jnp.concatenate([s_win.reshape(Bt, H, GRID_W, n_win), s_ctx], axis=-1)
        p = jax.nn.softmax(s, axis=-1).astype(q.dtype)
        p_win = p[..., :n_win].reshape(Bt, H, GRID_W, win_h, win_w)
        return (jnp.einsum('bhqrw,brqwhd->bqhd', p_win, v_win)
                + jnp.einsum('bhql,blhd->bqhd', p[..., n_win:], v_ctx))

    o = lax.map(one_row, (jnp.arange(rows), q_rows))
    return o.swapaxes(0, 1).reshape(Bt, T, H * Dh)


def rwkv7_bidirectional(r, k, v, w_lo, a_lo, g_lo, S0, w0, w2, a0, a2, g2, k_k, k_a, r_k, ln_g, ln_b):
    Bt, T, _ = r.shape
    H, N = RWKV_HEADS, RWKV_HEAD_DIM
    w_lo = w_lo.reshape(Bt, T, 2, RWKV_DECAY_RANK)
    a_lo = a_lo.reshape(Bt, T, 2, RWKV_ICLR_RANK)
    w = -jax.nn.softplus(-(w0 + jnp.einsum('btdr,dre->btde', jnp.tanh(w_lo), w2))) - 0.5
    decay = jnp.exp(-jnp.exp(w))
    a = jax.nn.sigmoid(a0 + jnp.einsum('btdr,dre->btde', a_lo, a2))
    g = jax.nn.sigmoid(g_lo) @ g2
    kkf = (k * k_k).reshape(Bt, T, H, N).astype(jnp.float32)
    kk = (kkf * lax.rsqrt(jnp.sum(kkf * kkf, axis=-1, keepdims=True) + 1e-12)).astype(k.dtype).reshape(Bt, T, RWKV_DIM)
    k_dir = k[:, :, None, :] * (1.0 + (a - 1.0) * k_a)

    def scan_layout(u_fwd, u_bwd):
        u = jnp.stack([u_fwd, u_bwd[:, ::-1]], axis=0)
        return u.reshape(2, Bt, T, H, N).transpose(2, 0, 1, 3, 4)

    xs = (scan_layout(r, r), scan_layout(decay[:, :, 0], decay[:, :, 1]),
          scan_layout(k_dir[:, :, 0], k_dir[:, :, 1]), scan_layout(v, v),
          scan_layout(kk, kk), scan_layout(a[:, :, 0], a[:, :, 1]))

    def step(S, inp):
        r_t, w_t, k_t, v_t, kk_t, a_t = inp
        s_a = jnp.einsum('dbhvk,dbhk->dbhv', S, -kk_t)
        S = (S * w_t[..., None, :] + s_a[..., :, None] * (kk_t * a_t)[..., None, :]
             + v_t[..., :, None] * k_t[..., None, :])
        return S, jnp.einsum('dbhvk,dbhk->dbhv', S, r_t)

    S_fin, ys = lax.scan(step, S0, xs)
    y = (ys[:, 0] + ys[::-1, 1]).transpose(1, 0, 2, 3)
    yf = y.astype(jnp.float32)
    mu = jnp.mean(yf, axis=-1, keepdims=True)
    var = jnp.mean(jnp.square(yf - mu), axis=-1, keepdims=True)
    y = ((yf - mu) * lax.rsqrt(var + RWKV_GN_EPS)).astype(r.dtype).reshape(Bt, T, RWKV_DIM) * ln_g + ln_b
    k_mean = k_dir.mean(axis=2)
    bonus = jnp.sum((r * k_mean).reshape(Bt, T, H, N) * r_k, axis=-1, keepdims=True) * v.reshape(Bt, T, H, N)
    return (y + bonus.reshape(Bt, T, RWKV_DIM)) * g, S_fin


def mla_expand(ckv, kpe, w_ukv):
    Bt, S, _ = ckv.shape
    kv = (ckv @ w_ukv).reshape(Bt, S, MLA_HEADS, MLA_NOPE + MLA_VDIM)
    k = jnp.concatenate([kv[..., :MLA_NOPE], jnp.broadcast_to(kpe, (Bt, S, MLA_HEADS, MLA_ROPE))], axis=-1)
    return k, kv[..., MLA_NOPE:]


def mixer_sublayer(x, shift, scale, gate, p, ctx):
    Bt, T, _ = x.shape
    latent = ctx is not None
    h = rms_norm(x, p['norm_mix_pre']) * (1.0 + scale) + shift
    (a_r, a_k, a_v, a_wlo, a_alo, a_glo, m_cq, m_ckv, m_kpe,
     g_q, g_k, g_v, n_q, n_k, n_v, gate_logits) = split_columns(h @ p['w_in'])
    if latent:
        st_rwkv, c_ckv, c_kpe, c_gk, c_gv, c_nk, c_nv = ctx
        S0 = jnp.swapaxes(st_rwkv, 0, 1).astype(x.dtype)
        cos_m, sin_m = axial_rope(T, MLA_ROPE, x.dtype)
        cos_g, sin_g = axial_rope(T, GQA_HEAD_DIM, x.dtype)
    else:
        S0 = jnp.zeros((2, Bt, RWKV_HEADS, RWKV_HEAD_DIM, RWKV_HEAD_DIM), x.dtype)
    y_a, S_fin = rwkv7_bidirectional(a_r, a_k, a_v, a_wlo, a_alo, a_glo, S0, p['rwkv_w0'], p['rwkv_w2'],
                                     p['rwkv_a0'], p['rwkv_a2'], p['rwkv_g2'], p['rwkv_k_k'], p['rwkv_k_a'],
                                     p['rwkv_r_k'], p['rwkv_ln_g'], p['rwkv_ln_b'])
    ckv = rms_norm(m_ckv, p['mla_kv_norm'])
    q_m = (rms_norm(m_cq, p['mla_q_norm']) @ p['mla_w_uq']).reshape(Bt, T, MLA_HEADS, MLA_NOPE + MLA_ROPE)
    kpe = m_kpe[:, :, None, :]
    if latent:
        q_m = jnp.concatenate([q_m[..., :MLA_NOPE], apply_rope(q_m[..., MLA_NOPE:], cos_m, sin_m)], axis=-1)
        kpe = apply_rope(kpe, cos_m, sin_m)
    kv_m = [mla_expand(ckv, kpe, p['mla_w_ukv'])]
    if latent:
        kv_m.append(mla_expand(c_ckv, c_kpe[:, :, None, :], p['mla_w_ukv']))
    y_b = blocked_attention(q_m[:, :, :, None, :], kv_m, (MLA_NOPE + MLA_ROPE) ** -0.5)
    q_g = rms_norm(g_q.reshape(Bt, T, GQA_HEADS, GQA_HEAD_DIM), p['gqa_q_norm'])
    k_g = rms_norm(g_k.reshape(Bt, T, GQA_KV_HEADS, GQA_HEAD_DIM), p['gqa_k_norm'])
    v_g = g_v.reshape(Bt, T, GQA_KV_HEADS, GQA_HEAD_DIM)
    if latent:
        q_g = apply_rope(q_g, cos_g, sin_g)
        kv_g = [(apply_rope(k_g, cos_g, sin_g), v_g), (c_gk, c_gv)]
    else:
        kv_g = [(k_g, v_g)]
    y_c = blocked_attention(q_g.reshape(Bt, T, GQA_KV_HEADS, GQA_HEADS // GQA_KV_HEADS, GQA_HEAD_DIM),
                            kv_g, GQA_HEAD_DIM ** -0.5)
    q_n = n_q.reshape(Bt, T, NA_HEADS, NA_HEAD_DIM)
    k_n = n_k.reshape(Bt, T, NA_HEADS, NA_HEAD_DIM)
    v_n = n_v.reshape(Bt, T, NA_HEADS, NA_HEAD_DIM)
    if latent:
        y_d = neighbourhood_attention(q_n, k_n, v_n, c_nk, c_nv, p['na_rel_bias'])
    else:
        y_d = blocked_attention(q_n[:, :, :, None, :], [(k_n, v_n)], NA_HEAD_DIM ** -0.5)
    ys = jnp.stack([y_a, y_b, y_c, y_d], axis=2)
    proj = jnp.einsum('btnd,nde->btne', ys, p['w_branch'])
    gates = jax.nn.sigmoid(gate_logits).reshape(Bt, T, N_BRANCH, D_MODEL)
    merged = jnp.sum(gates * proj, axis=2) @ p['w_out']
    x = x + gate * rms_norm(merged, p['norm_mix_post'])
    if latent:
        return x, None
    return x, (jnp.swapaxes(S_fin, 0, 1), ckv, m_kpe, k_g, v_g, k_n, v_n)


def ffn_sublayer(x, shift, scale, gate, p):
    h = rms_norm(x, p['norm_ffn_pre']) * (1.0 + scale) + shift
    f = (jax.nn.silu(h @ p['ffn_w_gate']) * (h @ p['ffn_w_up'])) @ p['ffn_w_down']
    return x + gate * rms_norm(f, p['norm_ffn_post'])


def setup_inputs(seed: int = 0) -> dict:
    key = jax.random.key(seed)
    ks = iter(jax.random.split(key, 64))

    def nrm(shape, s=1.0):
        return s * jax.random.normal(next(ks), shape, jnp.float32)

    def gain(shape):
        return 1.0 + 0.01 * jax.random.normal(next(ks), shape, jnp.float32)

    L = DEPTH
    return {
        'x_prompt': nrm((BATCH, SEQ, D_MODEL)),
        'x_sample': nrm((DEC_BATCH, DEC_SEQ, D_MODEL)),
        'c': nrm((DEC_BATCH, D_MODEL)),
        'state_rwkv': nrm((DEC_BATCH, L, 2, RWKV_HEADS, RWKV_HEAD_DIM, RWKV_HEAD_DIM), 0.5),
        'cache_mla_ckv': nrm((DEC_BATCH, L, PAST_LEN, MLA_KV_RANK)),
        'cache_mla_kpe': nrm((DEC_BATCH, L, PAST_LEN, MLA_ROPE)),
        'cache_gqa_k': nrm((DEC_BATCH, L, PAST_LEN, GQA_KV_HEADS, GQA_HEAD_DIM)),
        'cache_gqa_v': nrm((DEC_BATCH, L, PAST_LEN, GQA_KV_HEADS, GQA_HEAD_DIM)),
        'cache_na_k': nrm((DEC_BATCH, L, PAST_LEN, NA_HEADS, NA_HEAD_DIM)),
        'cache_na_v': nrm((DEC_BATCH, L, PAST_LEN, NA_HEADS, NA_HEAD_DIM)),
        'c_ctx': nrm((D_MODEL,)),
        'w_mod': nrm((L, D_MODEL, 6 * D_MODEL), 0.5 * D_MODEL ** -0.5),
        'b_mod': nrm((L, 6 * D_MODEL), 0.01),
        'norm_mix_pre': gain((L, D_MODEL)),
        'norm_mix_post': gain((L, D_MODEL)),
        'norm_ffn_pre': gain((L, D_MODEL)),
        'norm_ffn_post': gain((L, D_MODEL)),
        'w_in': nrm((L, D_MODEL, N_IN), D_MODEL ** -0.5),
        'rwkv_w0': jax.random.uniform(next(ks), (L, 2, RWKV_DIM), jnp.float32, -6.0, 1.0),
        'rwkv_w2': nrm((L, 2, RWKV_DECAY_RANK, RWKV_DIM), 0.5 * RWKV_DECAY_RANK ** -0.5),
        'rwkv_a0': nrm((L, 2, RWKV_DIM), 0.1),
        'rwkv_a2': nrm((L, 2, RWKV_ICLR_RANK, RWKV_DIM), RWKV_ICLR_RANK ** -0.5),
        'rwkv_g2': nrm((L, RWKV_GATE_RANK, RWKV_DIM), RWKV_GATE_RANK ** -0.5),
        'rwkv_k_k': 0.85 + nrm((L, RWKV_DIM), 0.05),
        'rwkv_k_a': 1.0 + nrm((L, RWKV_DIM), 0.05),
        'rwkv_r_k': nrm((L, RWKV_HEADS, RWKV_HEAD_DIM), 0.1),
        'rwkv_ln_g': gain((L, RWKV_DIM)),
        'rwkv_ln_b': nrm((L, RWKV_DIM), 0.01),
        'mla_q_norm': gain((L, MLA_Q_RANK)),
        'mla_w_uq': nrm((L, MLA_Q_RANK, MLA_HEADS * (MLA_NOPE + MLA_ROPE)), MLA_Q_RANK ** -0.5),
        'mla_kv_norm': gain((L, MLA_KV_RANK)),
        'mla_w_ukv': nrm((L, MLA_KV_RANK, MLA_HEADS * (MLA_NOPE + MLA_VDIM)), MLA_KV_RANK ** -0.5),
        'gqa_q_norm': gain((L, GQA_HEAD_DIM)),
        'gqa_k_norm': gain((L, GQA_HEAD_DIM)),
        'na_rel_bias': nrm((L, NA_HEADS, 2 * NA_WIN_H - 1, 2 * NA_WIN_W - 1), 0.1),
        'w_branch': nrm((L, N_BRANCH, BRANCH_DIM, D_MODEL), BRANCH_DIM ** -0.5),
        'w_out': nrm((L, D_MODEL, D_MODEL), D_MODEL ** -0.5),
        'ffn_w_gate': nrm((L, D_MODEL, FFN_DIM), D_MODEL ** -0.5),
        'ffn_w_up': nrm((L, D_MODEL, FFN_DIM), D_MODEL ** -0.5),
        'ffn_w_down': nrm((L, FFN_DIM, D_MODEL), FFN_DIM ** -0.5),
    }


def reference(x_prompt, x_sample, c, state_rwkv, cache_mla_ckv, cache_mla_kpe, cache_gqa_k, cache_gqa_v,
              cache_na_k, cache_na_v, c_ctx, w_mod, b_mod, norm_mix_pre, norm_mix_post, norm_ffn_pre,
              norm_ffn_post, w_in, rwkv_w0, rwkv_w2, rwkv_a0, rwkv_a2, rwkv_g2, rwkv_k_k, rwkv_k_a, rwkv_r_k,
              rwkv_ln_g, rwkv_ln_b, mla_q_norm, mla_w_uq, mla_kv_norm, mla_w_ukv, gqa_q_norm, gqa_k_norm,
              na_rel_bias, w_branch, w_out, ffn_w_gate, ffn_w_up, ffn_w_down):
    P = {'norm_mix_pre': norm_mix_pre, 'norm_mix_post': norm_mix_post, 'norm_ffn_pre': norm_ffn_pre,
         'norm_ffn_post': norm_ffn_post, 'w_in': w_in, 'rwkv_w0': rwkv_w0, 'rwkv_w2': rwkv_w2,
         'rwkv_a0': rwkv_a0, 'rwkv_a2': rwkv_a2, 'rwkv_g2': rwkv_g2, 'rwkv_k_k': rwkv_k_k, 'rwkv_k_a': rwkv_k_a,
         'rwkv_r_k': rwkv_r_k, 'rwkv_ln_g': rwkv_ln_g, 'rwkv_ln_b': rwkv_ln_b, 'mla_q_norm': mla_q_norm,
         'mla_w_uq': mla_w_uq, 'mla_kv_norm': mla_kv_norm, 'mla_w_ukv': mla_w_ukv, 'gqa_q_norm': gqa_q_norm,
         'gqa_k_norm': gqa_k_norm, 'na_rel_bias': na_rel_bias, 'w_branch': w_branch, 'w_out': w_out,
         'ffn_w_gate': ffn_w_gate, 'ffn_w_up': ffn_w_up, 'ffn_w_down': ffn_w_down}

    x = x_prompt
    entries = []
    for l in range(DEPTH):
        p = {name: arr[l] for name, arr in P.items()}
        sh_m, sc_m, gt_m, sh_f, sc_f, gt_f = modulation(c_ctx[None, :], w_mod[l], b_mod[l])
        x, ent = mixer_sublayer(x, sh_m, sc_m, gt_m, p, None)
        entries.append(ent)
        x = ffn_sublayer(x, sh_f, sc_f, gt_f, p)
    y_prompt = x
    new_state_rwkv = jnp.stack([e[0] for e in entries], axis=1)
    new_cache_mla_ckv = jnp.stack([e[1] for e in entries], axis=1)
    new_cache_mla_kpe = jnp.stack([e[2] for e in entries], axis=1)
    new_cache_gqa_k = jnp.stack([e[3] for e in entries], axis=1)
    new_cache_gqa_v = jnp.stack([e[4] for e in entries], axis=1)
    new_cache_na_k = jnp.stack([e[5] for e in entries], axis=1)
    new_cache_na_v = jnp.stack([e[6] for e in entries], axis=1)

    x = x_sample
    for l in range(DEPTH):
        p = {name: arr[l] for name, arr in P.items()}
        sh_m, sc_m, gt_m, sh_f, sc_f, gt_f = modulation(c, w_mod[l], b_mod[l])
        ctx = (state_rwkv[:, l], cache_mla_ckv[:, l], cache_mla_kpe[:, l], cache_gqa_k[:, l],
               cache_gqa_v[:, l], cache_na_k[:, l], cache_na_v[:, l])
        x, _ = mixer_sublayer(x, sh_m, sc_m, gt_m, p, ctx)
        x = ffn_sublayer(x, sh_f, sc_f, gt_f, p)
    y_sample = x
    return (y_prompt, y_sample, new_state_rwkv, new_cache_mla_ckv, new_cache_mla_kpe,
            new_cache_gqa_k, new_cache_gqa_v, new_cache_na_k, new_cache_na_v)
```

```python
import math
from contextlib import ExitStack
import numpy as np
import concourse.bass as bass
import concourse.mybir as mybir
from concourse.bass_utils import run_bass_kernel_spmd

F32 = mybir.dt.float32
BF16 = mybir.dt.bfloat16
AF = mybir.ActivationFunctionType
ALU = mybir.AluOpType
AX = mybir.AxisListType

L = 4
D = 1024
FF = 2816
NF = FF // 128
EPS = 1e-6
N_IN = 6944

ENGS = ['pe', 'act', 'dve', 'pool', 'sp']
SEG = 4000
DMA_K = 8
DMA_USES = 200
SAME_ENGINE_SYNC = ('dve', 'act')


class Op:
    __slots__ = ('eng', 'fn', 'deps', 'dma', 'signal', 'sig', 'dj', 'idx')

    def __init__(self, eng, fn, dma):
        self.eng = eng
        self.fn = fn
        self.deps = []
        self.dma = dma
        self.signal = False
        self.sig = None
        self.dj = None


class Sched:
    def __init__(self, nc, stack):
        self.nc = nc
        self.stack = stack
        self.ops = {e: [] for e in ENGS}
        self.buf = {}
        self.sems = {}
        self.bar = {e: [] for e in ENGS}

    def sem(self, key):
        s = self.sems.get(key)
        if s is None:
            s = self.stack.enter_context(self.nc.semaphore("s_" + "_".join(str(k) for k in key)))
            self.sems[key] = s
        return s

    def barrier(self):
        lasts = [self.ops[e][-1] for e in ENGS if self.ops[e]]
        for e in ENGS:
            lasts += [o for o in self.ops[e][-4 * DMA_K:] if o.dma][-DMA_K:]
        for e in ENGS:
            self.bar[e] = list(lasts)

    def op(self, eng, fn, reads=(), writes=(), dma=False):
        o = Op(eng, fn, dma)
        o.idx = len(self.ops[eng])
        deps = {}
        psr_ = [k for k in reads if isinstance(k, tuple) and k[0] == 'ps']
        if psr_:
            reads = [k for k in reads if not (isinstance(k, tuple) and k[0] == 'ps')]
            writes = list(writes) + [k for k in psr_ if k not in writes]
        if self.bar[eng]:
            for d in self.bar[eng]:
                deps[id(d)] = d
            self.bar[eng] = []
        for k in reads:
            st = self.buf.get(k)
            if st is None:
                st = self.buf[k] = [None, []]
            if st[0] is not None:
                deps[id(st[0])] = st[0]
            st[1].append(o)
        for k in writes:
            st = self.buf.get(k)
            if st is None:
                st = self.buf[k] = [None, []]
            if st[0] is not None:
                deps[id(st[0])] = st[0]
            for r in st[1]:
                if r is not o:
                    deps[id(r)] = r
            st[0] = o
            st[1] = []
        o.deps = [d for d in deps.values() if d.dma or d.eng != eng or eng in SAME_ENGINE_SYNC]
        self.ops[eng].append(o)
        return o

    def emit(self):
        nc = self.nc
        for e in ENGS:
            for o in self.ops[e]:
                for d in o.deps:
                    if not d.dma:
                        d.signal = True
        for e in ENGS:
            c = 0
            j = 0
            for o in self.ops[e]:
                if o.dma:
                    o.dj = j
                    j += 1
                elif o.signal:
                    o.sig = c
                    c += 1
        last = {}
        for e in ENGS:
            for o in self.ops[e]:
                if o.dma:
                    slot = o.dj % DMA_K
                    n = o.dj // DMA_K
                    key = ('d', e, slot, n // DMA_USES)
                    self.sem(key)
                    last[key] = 16 * (n % DMA_USES + 1)
                elif o.signal:
                    self.sem(('e', e, o.sig // SEG))
        sched = self
        self.n_instr = {e: 0 for e in ENGS}

        def run_engine(e, h):
            waited = {}

            def wait(key, val):
                if waited.get(key, 0) >= val:
                    return
                waited[key] = val
                h.wait_ge(sched.sems[key], val)
                sched.n_instr[e] += 1

            for o in sched.ops[e]:
                for d in o.deps:
                    if d.dma:
                        slot = d.dj % DMA_K
                        n = d.dj // DMA_K
                        wait(('d', d.eng, slot, n // DMA_USES), 16 * (n % DMA_USES + 1))
                    else:
                        wait(('e', d.eng, d.sig // SEG), d.sig % SEG + 1)
                if o.dma:
                    slot = o.dj % DMA_K
                    n = o.dj // DMA_K
                    if n > 0:
                        pn = n - 1
                        wait(('d', e, slot, pn // DMA_USES), 16 * (pn % DMA_USES + 1))
                ins = o.fn(h)
                sched.n_instr[e] += 1
                if o.dma:
                    slot = o.dj % DMA_K
                    n = o.dj // DMA_K
                    ins.then_inc(sched.sems[('d', e, slot, n // DMA_USES)], 16)
                elif o.signal:
                    ins.then_inc(sched.sems[('e', e, o.sig // SEG)], 1)
            if e == 'sp':
                for key, val in last.items():
                    wait(key, val)

        with nc.Block() as block:
            @block.tensor
            def _(h):
                run_engine('pe', h)

            @block.scalar
            def _(h):
                run_engine('act', h)

            @block.vector
            def _(h):
                run_engine('dve', h)

            @block.gpsimd
            def _(h):
                run_engine('pool', h)

            @block.sync
            def _(h):
                run_engine('sp', h)


IN_OFF = {}
_o = 0
for _n, _w in [('a_r', 256), ('a_k', 256), ('a_v', 256), ('a_wlo', 128), ('a_alo', 128), ('a_glo', 128),
               ('m_cq', 256), ('m_ckv', 128), ('m_kpe', 32), ('g_q', 256), ('g_k', 128), ('g_v', 128),
               ('n_q', 256), ('n_k', 256), ('n_v', 256), ('gate', 4096)]:
    IN_OFF[_n] = _o
    _o += _w

VEC = {}
NVEC = 0
for _n, _w in [('b_mod', 48), ('g_mpre', 8), ('g_mpost', 8), ('g_fpre', 8), ('g_fpost', 8), ('a0', 4), ('k_k', 2),
               ('k_a', 2), ('r_k', 2), ('qn', 2), ('kvn', 1), ('gq', 1), ('gq_sw', 1), ('gk', 1), ('gk_sw', 1), ('ln_g', 2), ('ln_b', 2)]:
    VEC[_n] = NVEC
    NVEC += _w

CST = {}
NCST = 0
for _n, _w in [('ident', 128), ('ones', 128), ('bones', 128), ('eps_rms', 1), ('eps_kk', 1), ('eps_gn', 1), ('one', 1)]:
    CST[_n] = NCST
    NCST += _w
CSR = {}
NCSR = 0
for _n, _w in [('TI0', 128), ('TS0', 128), ('TI1', 128), ('TS1', 128),
               ('MS0', 128), ('MI0', 128), ('MS1', 128), ('MI1', 128), ('cm0', 128), ('cm1', 128)]:
    CSR[_n] = NCSR
    NCSR += _w


def _fm(v):
    return np.ascontiguousarray(v.reshape(L, -1, 128).transpose(0, 2, 1))


def _swap_halves(a, width):
    sh = a.shape
    b = a.reshape(sh[:-1] + (sh[-1] // width, 2, width // 2))
    return np.ascontiguousarray(b[..., ::-1, :].reshape(sh))


def _rope_tables(T, rot_dim):
    t = np.arange(T)
    pos = np.stack([t // 64, t % 64], axis=-1).astype(np.float32)
    n_freq = rot_dim // 4
    inv = (10000.0 ** (-np.arange(n_freq, dtype=np.float32) / n_freq)).astype(np.float32)
    ang = (pos[:, :, None] * inv).reshape(T, rot_dim // 2)
    c = np.cos(ang).astype(np.float32).T
    s = np.sin(ang).astype(np.float32).T
    C = np.concatenate([c, c], axis=0)
    S_ = np.concatenate([-s, s], axis=0)
    return C, S_


def _consts():
    c = np.zeros((128, NCST), np.float32)
    r = np.zeros((128, NCSR), np.float32)
    i = np.arange(128)
    c[:, CST['ident']:CST['ident'] + 128] = np.eye(128)
    c[:, CST['ones']:CST['ones'] + 128] = 1.0
    c[:, CST['bones']:CST['bones'] + 128] = (i[:, None] // 64 == i[None, :] // 64)
    same = (i[:, None] // 64 == i[None, :] // 64)
    su = same & (i[:, None] < i[None, :])
    iu = same & (i[:, None] <= i[None, :])
    sl = same & (i[:, None] > i[None, :])
    il = same & (i[:, None] >= i[None, :])
    ch = -math.exp(-0.5)
    for name, val in (('MS0', su), ('MI0', iu), ('MS1', sl), ('MI1', il), ('TI0', ch * iu), ('TS0', ch * su),
                      ('TI1', ch * il), ('TS1', ch * sl), ('cm0', np.broadcast_to(i[None, :] < 64, (128, 128))),
                      ('cm1', np.broadcast_to(i[None, :] >= 64, (128, 128)))):
        r[:, CSR[name]:CSR[name] + 128] = val
    c[:, CST['eps_rms']] = EPS
    c[:, CST['eps_kk']] = 1e-12
    c[:, CST['eps_gn']] = 64e-5
    c[:, CST['one']] = 1.0
    return c, r


def _na_bias_table(rel_bias):
    p = np.arange(128)
    st = np.arange(4)
    dr = np.arange(8)
    cq = np.arange(64)
    off = st[None, :] * 128 + p[:, None]
    kr = (off // 64)[:, None, :, None]
    kc = (off % 64)[:, None, :, None]
    ri = np.broadcast_to(kr - dr[None, :, None, None] + 7, (128, 8, 4, 64))
    c0 = np.clip(cq - 8, 0, 48)
    valid = np.broadcast_to((kc >= c0) & (kc < c0 + 16), (128, 8, 4, 64))
    ci = np.broadcast_to(np.clip(kc - cq + 15, 0, 30), (128, 8, 4, 64))
    g = rel_bias[:, :, ri, ci]
    g = np.where(valid[None, None], g, np.float32(-30000.0)).astype(np.float32)
    return np.ascontiguousarray(g.transpose(0, 1, 2, 3, 4, 5))


class Path:
    def __init__(self, name, T, nseq, lat, m):
        self.name = name
        self.T = T
        self.NT = T // 512
        self.NB = T // 128
        self.nseq = nseq
        self.Ts = T // nseq
        self.lat = lat
        self.m = m
        self.SC = 512 if lat else 0


class Prog:
    def __init__(self, nc, cfg):
        self.nc = nc
        self.cfg = cfg
        self.root = ExitStack()
        self.S = Sched(nc, self.root)
        self.scopes = [self.root]
        self.q = 0
        self.uid = 0
        self.io = {}

    def din(self, name, shape):
        t = self.nc.dram_tensor(name, list(shape), F32, kind="ExternalInput").ap()
        self.io[name] = t
        return t

    def dout(self, name, shape):
        t = self.nc.dram_tensor(name, list(shape), F32, kind="ExternalOutput").ap()
        self.io[name] = t
        return t

    def T(self, name, shape, dt):
        self.uid += 1
        return self.scopes[-1].enter_context(self.nc.sbuf_tensor(f"{name}_{self.uid}", list(shape), dt))

    def push(self):
        st = ExitStack()
        self.scopes.append(st)
        return st

    def pop(self):
        self.S.barrier()
        st = self.scopes.pop()
        st.close()
        self.wp = list(self.wp_root)

    def more_wp(self, n):
        self.wp = list(self.wp_root) + [self.T(f"wpx{i}", [128, 8, 128], BF16) for i in range(n)]

    def mm(self, out, lhsT, rhs, start, stop, r, w):
        self.S.op('pe', lambda h: h.matmul(out, lhsT, rhs, start=start, stop=stop), r, w)

    def tr(self, out, in_, ident, r, w):
        self.S.op('pe', lambda h: h.transpose(out, in_, ident), r, w)

    def act(self, out, in_, func, r, w, bias=None, scale=None, accum=None):
        kw = {}
        if bias is not None:
            kw['bias'] = bias
        if scale is not None:
            kw['scale'] = scale
        if accum is not None:
            kw['accum_out'] = accum
        self.S.op('act', lambda h: h.activation(out, in_, func, **kw), r, w)

    def tt(self, out, a, b, op, r, w):
        self.S.op('dve', lambda h: h.tensor_tensor(out, a, b, op=op), r, w)

    def ts(self, out, a, s1, s2, op0, op1, r, w):
        if s2 is None:
            self.S.op('dve', lambda h: h.tensor_scalar(out, a, s1, None, op0=op0), r, w)
        else:
            self.S.op('dve', lambda h: h.tensor_scalar(out, a, s1, s2, op0=op0, op1=op1), r, w)

    def stt(self, out, in0, scalar, in1, op0, op1, r, w):
        self.S.op('dve', lambda h: h.scalar_tensor_tensor(out, in0, scalar, in1, op0=op0, op1=op1), r, w)

    def vcopy(self, out, in_, r, w):
        self.S.op('dve', lambda h: h.tensor_copy(out, in_), r, w)

    def acopy(self, out, in_, r, w):
        self.S.op('act', lambda h: h.copy(out, in_), r, w)

    def recip(self, out, in_, r, w):
        self.S.op('dve', lambda h: h.reciprocal(out, in_), r, w)

    def memset(self, out, val, w, eng='dve'):
        self.S.op(eng, lambda h: h.memset(out, val), (), w)

    def dma(self, eng, out, in_, r, w):
        self.S.op(eng, lambda h: h.dma_start(out=out, in_=in_), r, w, dma=True)

    def psq(self, ncols=512, parts=128):
        b = self.q % 5
        self.q += 1
        return self.ps[b][0:parts, 0:ncols], [('ps', b)]

    def psr(self, b, ncols=512, parts=128, off=0):
        return self.ps[b][0:parts, off:off + ncols], [('ps', b)]

    def load_w(self, wap, n):
        i = self.wi % len(self.wp)
        self.wi += 1
        wt = self.wp[i]
        key = ('wp', i)
        self.dma('pool', wt[:, :, 0:n], wap.rearrange("(kc p) n -> p kc n", p=128), [], [key])
        return wt, key

    def cc(self, name, n=128, rows=slice(0, 128)):
        o = CST[name]
        return self.cst[rows, o:o + n]

    def vv(self, l, name, i=0, n=1):
        o = VEC[name] + i
        return self.vecs[:, l, o:o + n]

    def build(self):
        nc, S, cfg = self.nc, self.S, self.cfg
        io = self.io
        xc = self.din("xc", [1024, D])
        xl = self.din("xl", [2048, D])
        cvec = self.din("cvec", [128, 8, 2])
        vecs_d = self.din("vecs", [128, L, NVEC])
        cst_d = self.din("cst", [128, NCST])
        self.csr_d = self.din("csr", [128, NCSR])
        w_mod = self.din("w_mod_t", [L, 12, 128, 8 * 512])
        w_in = self.din("w_in", [L, D, N_IN])
        w_in_sw = self.din("w_in_sw", [L, D, 416])
        w_fg = self.din("ffn_gu_t", [L, NF // 2, 128, 8 * 512])
        w_fu = None
        w_fd = self.din("ffn_d_t", [L, 4, 128, NF * 256])
        self.gate_t = self.din("gate_t", [L, 8, 128, 4 * 8 * 128])
        self.wbr_t = self.din("wbr_t", [L, 8, 128, 8 * 128])
        self.wout_t = self.din("wout_t", [L, 8, 128, 8 * 128])
        yc = self.dout("yc", [1024, D])
        yl = self.dout("yl", [2048, D])
        self.w_in, self.w_in_sw = w_in, w_in_sw
        self.declare_branch_io()
        if cfg.get('dbg'):
            self.dbg = [self.dout('dbg_c', [128, 2, 1024]), self.dout('dbg_l', [128, 2, 2048])]
            self.dbg2 = [self.dout('dbg2_c', [128, 2, 8, 128]), self.dout('dbg2_l', [128, 2, 16, 128])]
            self.dbg3 = [[self.dout(f'dbg3_{n}{i}', [128, 2, T_]) for i in range(3)] for n, T_ in (('c', 1024), ('l', 2048))]

        self.ps = [self.root.enter_context(nc.psum_tensor(f"psb{i}", [128, 512], F32)) if i != 5 else None for i in range(8)]
        self.psbf = self.root.enter_context(nc.psum_tensor("psbf", [128, 1024], BF16))
        self.cst = self.T("cst", [128, NCST], F32)
        self.identb = self.T("identb", [128, 128], BF16)
        self.vecs = self.T("vecs", [128, L, NVEC], F32)
        self.AB = self.T("AB", [128, L, 2, 6, 8], F32)
        self.xT = self.T("xT", [128, 8, 2048], F32)
        self.hT = self.T("hT", [128, 8, 2048], BF16)
        self.wp = []
        self.wp_root = []
        self.wi = 0
        self.sq = [self.T(f"sq{i}", [128, 512], F32) for i in range(2)]
        self.rs = [self.T(f"rs{i}", [128, 512], F32) for i in range(2)]
        self.tmpf = [self.T(f"tmpf{i}", [128, 512], F32) for i in range(2)]
        self.cnt = 0

        self.dma('sp', self.cst[:], cst_d[:, :], [], ['cst'])
        self.dma('pool', self.identb[:], cst_d[:, CST['ident']:CST['ident'] + 128], [], ['identb'])
        self.dma('sp', self.vecs[:], vecs_d[:, :, :], [], ['vecs'])

        ms = self.push()
        self.modall = self.T("modall", [128, L, 48, 2], F32)
        cv = self.T("cv", [128, 8, 2], F32)
        scb = self.T("scb", [128, 8, 2], BF16)
        wm = [self.T(f"wm{i}", [128, 8, 512], BF16) for i in range(2)]
        self.dma('sp', cv[:], cvec[:, :, :], [], ['cv'])
        self.act(scb[:], cv[:], AF.Silu, ['cv'], ['scb'])
        for l in range(L if cfg.get('do_mod', True) else 0):
            for g in range(12):
                i = (l * 12 + g) % 2
                self.dma('pool', wm[i][:].rearrange("p a b -> p (a b)"), w_mod[l, g], [], [('wm', i)])
                ps, pk = self.psq(512)
                for q4 in range(4):
                    for kc in range(8):
                        self.mm(ps[:, q4 * 2:q4 * 2 + 2], wm[i][:, kc, q4 * 128:(q4 + 1) * 128], scb[:, kc, :],
                                kc == 0, kc == 7, [('wm', i), 'scb'], pk)
                for q4 in range(4):
                    n = g * 4 + q4
                    self.act(self.modall[:, l, n, :], ps[:, q4 * 2:q4 * 2 + 2], AF.Identity, pk + ['vecs'], ['modall'],
                             bias=self.vv(l, 'b_mod', n))
            mv = self.modall[:, l, :, :].rearrange("p (s k) m -> p s k m", s=6)
            for m in range(2 if cfg.get('mod_stage', 2) >= 2 else 0):
                for (si, gi, oi) in ((1, 'g_mpre', 0), (4, 'g_fpre', 3)):
                    self.stt(self.AB[:, l, m, oi, :], mv[:, si, :, m], 1.0, self.vv(l, gi, 0, 8), ALU.add, ALU.mult,
                             ['modall', 'vecs'], ['AB'])
                for (si, oi) in ((0, 1), (3, 4)):
                    self.vcopy(self.AB[:, l, m, oi, :], mv[:, si, :, m], ['modall'], ['AB'])
                for (si, gi, oi) in ((2, 'g_mpost', 2), (5, 'g_fpost', 5)):
                    self.tt(self.AB[:, l, m, oi, :], mv[:, si, :, m], self.vv(l, gi, 0, 8), ALU.mult,
                            ['modall', 'vecs'], ['AB'])
        self.pop()

        paths = []
        if cfg.get('do_ctx', True):
            paths.append((Path('ctx', 1024, 4, False, 0), xc, yc))
        if cfg.get('do_lat', True):
            paths.append((Path('lat', 2048, 1, True, 1), xl, yl))
        for path, xi, yo in paths:
            self.load_x(path, xi)
            for l in range(cfg.get('nlayers', L)):
                if cfg.get('do_mixer', True):
                    self.mixer(l, path)
                if cfg.get('do_ffn', True):
                    self.ffn(l, path, w_fg, w_fu, w_fd)
            self.store_x(path, yo)
        S.emit()

    def load_x(self, path, xi):
        self.push()
        self.xin = [self.T(f"xin{i}", [128, D], F32) for i in range(2)]
        for blk in range(path.NB):
            i = blk % 2
            self.dma('sp', self.xin[i][:], xi[blk * 128:(blk + 1) * 128, :], [], [('xin', i)])
            for half in range(2):
                ps, pk = self.psq(512)
                for c in range(4):
                    kc = half * 4 + c
                    self.tr(ps[:, c * 128:(c + 1) * 128], self.xin[i][:, kc * 128:(kc + 1) * 128], self.cc('ident'),
                            [('xin', i), 'cst'], pk)
                if True:
                    for c in range(4):
                        kc = half * 4 + c
                        f = self.vcopy if c % 2 == 0 else self.acopy
                        f(self.xT[:, kc, blk * 128:(blk + 1) * 128], ps[:, c * 128:(c + 1) * 128], pk, [('xT', blk // 4)])
                    continue
        self.pop()

    def store_x(self, path, yo):
        self.push()
        self.xin = [self.T(f"xin{i}", [128, D], F32) for i in range(2)]
        for blk in range(path.NB):
            i = blk % 2
            for half in range(2):
                ps, pk = self.psq(512)
                for c in range(4):
                    kc = half * 4 + c
                    self.tr(ps[:, c * 128:(c + 1) * 128], self.xT[:, kc, blk * 128:(blk + 1) * 128], self.cc('ident'),
                            [('xT', blk // 4), 'cst'], pk)
                if half == 0:
                    self.vcopy(self.xin[i][:, 0:512], ps, pk, [('xin', i)])
                else:
                    self.acopy(self.xin[i][:, 512:1024], ps, pk, [('xin', i)])
            self.dma('sp', yo[blk * 128:(blk + 1) * 128, :], self.xin[i][:], [('xin', i)], [])
        self.pop()

    def rstd(self, rs, ps, pk, rk, n, eps_name):
        self.act(rs, ps, AF.Sqrt, pk + ['cst'], [rk], bias=self.cc(eps_name, 1), scale=1.0 / n)
        self.recip(rs, rs, [rk], [rk])

    def norm_h(self, l, path, ai):
        for j in range(path.NT):
            sl = slice(j * 512, (j + 1) * 512)
            ps, pk = self.psq(512)
            for kc in range(8):
                i = self.cnt % 2
                self.cnt += 1
                self.act(self.sq[i][:], self.xT[:, kc, sl], AF.Square, [('xT', j)], [('sq', i)])
                self.mm(ps, self.cc('ones'), self.sq[i][:], kc == 0, kc == 7, [('sq', i), 'cst'], pk)
            ri = j % 2
            self.rstd(self.rs[ri][:], ps, pk, ('rs', ri), D, 'eps_rms')
            for kc in range(8):
                i = self.cnt % 2
                self.cnt += 1
                self.tt(self.tmpf[i][:], self.xT[:, kc, sl], self.rs[ri][:], ALU.mult, [('xT', j), ('rs', ri)],
                        [('tmpf', i)])
                self.act(self.hT[:, kc, sl], self.tmpf[i][:], AF.Identity, [('tmpf', i), 'AB'], [('hT', j)],
                         bias=self.AB[:, l, path.m, ai + 1, kc:kc + 1], scale=self.AB[:, l, path.m, ai, kc:kc + 1])

    def evac_oT(self, kc, ps, pk, ps_n, pnk):
        self.acopy(self.oT[:, kc, :], ps, pk, [('oT', kc)])
        i = self.cnt % 2
        self.cnt += 1
        self.act(self.sq[i][:], ps, AF.Square, pk, [('sq', i)])
        self.mm(ps_n, self.cc('ones'), self.sq[i][:], kc == 0, kc == 7, [('sq', i), 'cst'], pnk)

    def post_residual(self, l, path, j, gi, ps_n, pnk):
        sl = slice(j * 512, (j + 1) * 512)
        ri = j % 2
        self.rstd(self.rs[ri][:], ps_n, pnk, ('rs', ri), D, 'eps_rms')
        for kc in range(8):
            i = self.cnt % 2
            self.cnt += 1
            self.tt(self.tmpf[i][:], self.oT[:, kc, :], self.rs[ri][:], ALU.mult, [('oT', kc), ('rs', ri)],
                    [('tmpf', i)])
            self.stt(self.xT[:, kc, sl], self.tmpf[i][:], self.AB[:, l, path.m, gi, kc:kc + 1], self.xT[:, kc, sl],
                     ALU.mult, ALU.add, [('tmpf', i), 'AB', ('xT', j)], [('xT', j)])

    def ffn(self, l, path, w_fg, w_fu, w_fd):
        self.norm_h(l, path, 3)
        self.push()
        G = 2
        actT = self.T("actT", [128, NF, G * 512], BF16)
        self.oT = self.T("oT", [128, 8, 512], F32)
        wgu = [self.T(f"wgu{i}", [128, 8, 512], BF16) for i in range(2)]
        wd = self.T("wd", [128, NF, 256], BF16)
        sg = self.tmpf
        cnt = 0
        for g in range(path.NT // G):
            for fp in range(NF // 2):
                i = fp % 2
                self.dma('pool', wgu[i][:].rearrange("p a b -> p (a b)"), w_fg[l, fp], [], [('wgu', i)])
                for f2 in range(2):
                    f = fp * 2 + f2
                    for t in range(G):
                        j = g * G + t
                        sl = slice(j * 512, (j + 1) * 512)
                        pg, pgk = self.psq(512)
                        for kc in range(8):
                            self.mm(pg, wgu[i][:, kc, f2 * 128:(f2 + 1) * 128], self.hT[:, kc, sl], kc == 0, kc == 7,
                                    [('wgu', i), ('hT', j)], pgk)
                        pu, puk = self.psq(512)
                        for kc in range(8):
                            self.mm(pu, wgu[i][:, kc, 256 + f2 * 128:256 + (f2 + 1) * 128], self.hT[:, kc, sl], kc == 0, kc == 7,
                                    [('wgu', i), ('hT', j)], puk)
                        k = cnt % 2
                        cnt += 1
                        self.act(sg[k][:], pg, AF.Silu, pgk, [('tmpf', k)])
                        self.tt(actT[:, f, t * 512:(t + 1) * 512], sg[k][:], pu, ALU.mult, [('tmpf', k)] + puk, [('actT', f, t)])
            for t in range(G):
                j = g * G + t
                ps_n, pnk = self.psr(6, 512)
                for q in range(4):
                    for part in range(2):
                        f0, f1 = part * 11, (part + 1) * 11
                        self.dma('pool', wd[:, f0:f1, :].rearrange("p a b -> p (a b)"), w_fd[l, q][:, f0 * 256:f1 * 256], [], [('wd', part)])
                    pss = [self.psq(512) for _ in range(2)]
                    for oc in range(2):
                        for f in range(NF):
                            self.mm(pss[oc][0], wd[:, f, oc * 128:(oc + 1) * 128], actT[:, f, t * 512:(t + 1) * 512], f == 0, f == NF - 1,
                                    [('wd', f // 11), ('actT', f, t)], pss[oc][1])
                    for oc in range(2):
                        self.evac_oT(q * 2 + oc, pss[oc][0], pss[oc][1], ps_n, pnk)
                self.post_residual(l, path, j, 5, ps_n, pnk)
        self.pop()

    def declare_branch_io(self):
        d = self.din
        self.st_in = d("st_in", [L, 2, 4, 64, 64])
        self.c_ckv = d("c_ckv", [L, 512, 128])
        self.c_kpe = d("c_kpe", [L, 512, 32])
        self.c_gk = d("c_gk", [L, 512, 128])
        self.c_gv = d("c_gv", [L, 512, 128])
        self.c_nk = d("c_nk", [L, 512, 256])
        self.c_nv = d("c_nv", [L, 512, 256])
        self.rows_d = d("rows", [128, L, 1024])
        self.w2_d = d("rwkv_w2", [L, 128, 256])
        self.a2_d = d("rwkv_a2", [L, 128, 256])
        self.g2_d = d("rwkv_g2", [L, 128, 256])
        self.w_uq = d("mla_w_uq", [L, 256, 384])
        self.w_uq_sw = d("mla_w_uq_sw", [L, 256, 128])
        self.w_ukv = d("mla_w_ukv", [L, 128, 512])
        self.rope_d = d("rope", [128, 4, 2048])
        self.nab_d = d("nab", [L, 4, 128, 8 * 4 * 64])
        o = self.dout
        self.st_out = o("st_out", [4, L, 2, 4, 64, 64])
        self.o_ckv = o("o_ckv", [4, L, 256, 128])
        self.o_kpe = o("o_kpe", [4, L, 256, 32])
        self.o_gk = o("o_gk", [4, L, 256, 128])
        self.o_gv = o("o_gv", [4, L, 256, 128])
        self.o_nk = o("o_nk", [4, L, 256, 256])
        self.o_nv = o("o_nv", [4, L, 256, 256])

    def mixer(self, l, path):
        cfg = self.cfg
        self.norm_h(l, path, 0)
        self.push()
        self.yT = self.T("yT", [128, 8, path.T], BF16)
        br = cfg.get('branches', 'ABCD')
        if 'A' in br:
            self.branch_rwkv(l, path)
        for n, name in enumerate('ABCD'):
            if name not in br:
                for c in range(2):
                    self.memset(self.yT[:, 2 * n + c, :], 0.0, [('yT', 2 * n + c)])
        if 'B' in br:
            self.branch_mla(l, path)
        if 'C' in br:
            self.branch_gqa(l, path)
        if 'D' in br:
            self.branch_na(l, path)
        self.pass2(l, path)
        self.pop()

    def project_group(self, path, wspecs, evac, fin=None):
        assert len(wspecs) <= len(self.wp), (len(wspecs), len(self.wp))
        wts = []
        for tag, pieces in wspecs:
            i = self.wi % len(self.wp)
            self.wi += 1
            wt = self.wp[i]
            key = ('wp', i)
            c0 = 0
            for (wap, n) in pieces:
                self.dma('pool', wt[:, :, c0:c0 + n], wap.rearrange("(kc p) n -> p kc n", p=128), [], [key])
                c0 += n
            wts.append((tag, wt, key, c0))
        for j in range(path.NT):
            sl = slice(j * 512, (j + 1) * 512)
            for tag, wt, key, n in wts:
                ps, pk = self.psq(512, parts=n)
                for kc in range(8):
                    self.mm(ps, wt[:, kc, 0:n], self.hT[:, kc, sl], kc == 0, kc == 7, [key, ('hT', j)], pk)
                evac(tag, j, ps, pk)
            if fin is not None:
                fin(j)

    def win(self, l, name, c0, n):
        o = IN_OFF[name] + c0
        return (self.w_in[l][:, o:o + n], n)

    def attn_head(self, path, pieces, vaug, vkey, ydst, ykey, scale):
        N = min(512, path.Ts)
        LOOK = 2
        steps = []
        for seq in range(path.nseq):
            sts = list(range(path.NB + 4)) if path.lat else [seq * 2, seq * 2 + 1]
            for qt in range(path.Ts // N):
                q0 = seq * path.Ts + qt * N
                for si, st in enumerate(sts):
                    steps.append((q0, st, si == 0, si == len(sts) - 1))
        inflight = {}

        def emit_qk(k):
            q0, st, _, _ = steps[k]
            ps, pk = self.psq(N)
            for pi, (qf, kf, keys) in enumerate(pieces):
                self.mm(ps, kf(st), qf(q0, N), pi == 0, pi == len(pieces) - 1, keys, pk)
            inflight[k] = (ps, pk)

        for k in range(min(LOOK, len(steps))):
            emit_qk(k)
        po = pok = None
        for k, (q0, st, first, last) in enumerate(steps):
            if k + LOOK < len(steps):
                emit_qk(k + LOOK)
            if first:
                po, pok = self.psr(6 + self.ocnt % 2, N, parts=65)
                self.ocnt += 1
            ps, pk = inflight.pop(k)
            i = self.pcnt % 3
            self.pcnt += 1
            self.act(self.pT[i][:, 0:N], ps, AF.Exp, pk, [('pT', i)], scale=scale)
            self.mm(po, vaug(st), self.pT[i][:, 0:N], first, last, [('pT', i), vkey], pok)
            if last:
                self.attn_norm(po, pok, N, ydst(q0, N), ykey)

    def attn_norm(self, po, pok, N, ydst, ykey):
        osb = self.osb
        self.acopy(osb[0:65, 0:N], po, pok, ['osb'])
        pb, pbk = self.psq(N, parts=64)
        self.mm(pb, self.cst[64:65, CST['ones']:CST['ones'] + 64], osb[64:65, 0:N], True, True, ['osb', 'cst'], pbk)
        ap, shift = ydst
        self.recip(self.rcp[0:64, 0:N], pb, pbk, ['rcp'])
        if not shift:
            self.tt(ap, osb[0:64, 0:N], self.rcp[0:64, 0:N], ALU.mult, ['osb', 'rcp'], [ykey])
        else:
            self.tt(self.ytmp[0:64, 0:N], osb[0:64, 0:N], self.rcp[0:64, 0:N], ALU.mult, ['osb', 'rcp'], ['ytmp'])
            self.acopy(ap, self.ytmp[0:64, 0:N], ['ytmp'], [ykey])

    def attn_tiles(self):
        self.pT = [self.T(f"pT{i}", [128, 512], BF16) for i in range(3)]
        self.rcp = self.T("rcp", [64, 512], F32)
        self.osb = self.T("osb", [65, 512], F32)
        self.rec = self.osb
        self.ytmp = self.T("ytmp", [64, 512], BF16)
        self.pcnt = 0
        self.ocnt = 0

    def out_tok(self, src_fn, nparts, ncols, path, l, dst, srckey):
        for blk in range(path.NB):
            ps, pk = self.psq(nparts, parts=128)
            self.tr(ps, src_fn(blk), self.cst[0:nparts, CST['ident']:CST['ident'] + nparts], [srckey, 'cst'], pk)
            i = self.stc % 2
            self.stc += 1
            self.acopy(self.stage[i][:, 0:nparts], ps, pk, [('stage', i)])
            seq, t0 = blk // 2, (blk % 2) * 128
            self.dma('sp', dst(seq, t0), self.stage[i][:, 0:nparts], [('stage', i)], [])

    def pass2(self, l, path):
        self.push()
        G = 2
        mT = self.T("mT", [128, 8, G * 512], BF16)
        self.oT = self.T("oT", [128, 8, 512], F32)
        wg = [self.T(f"wgt{i}", [128, 4, 8, 128], BF16) for i in range(2)]
        wb = [self.T(f"wbt{i}", [128, 8, 128], BF16) for i in range(2)]
        wo = [self.T(f"wot{i}", [128, 8, 128], BF16) for i in range(3)]
        acc, sgm, tmp = self.tmpf, self.sq, self.rs
        cnt = 0
        woc = 0
        for g in range(path.NT // G):
            for e in range(8):
                i = e % 2
                self.dma('pool', wg[i][:].rearrange("p a b c -> p (a b c)"), self.gate_t[l, e], [], [('wgt', i)])
                self.dma('pool', wb[i][:].rearrange("p a b -> p (a b)"), self.wbr_t[l, e], [], [('wbt', i)])
                for t in range(G):
                    j = g * G + t
                    sl = slice(j * 512, (j + 1) * 512)
                    ai = t % 2
                    for n in range(4):
                        pa, pak = self.psq(512)
                        for kc in range(8):
                            self.mm(pa, wg[i][:, n, kc, :], self.hT[:, kc, sl], kc == 0, kc == 7, [('wgt', i), ('hT', j)], pak)
                        si = cnt % 2
                        cnt += 1
                        self.act(sgm[si][:], pa, AF.Sigmoid, pak, [('sq', si)])
                        pb, pbk = self.psq(512)
                        for k2 in range(2):
                            self.mm(pb, wb[i][:, 2 * n + k2, :], self.yT[:, 2 * n + k2, sl], k2 == 0, k2 == 1,
                                    [('wbt', i), ('yT', 2 * n + k2)], pbk)
                        if n == 0:
                            self.tt(acc[ai][:], sgm[si][:], pb, ALU.mult, [('sq', si)] + pbk, [('tmpf', ai)])
                        else:
                            self.tt(tmp[si][:], sgm[si][:], pb, ALU.mult, [('sq', si)] + pbk, [('rs', si)])
                            if n < 3:
                                self.tt(acc[ai][:], acc[ai][:], tmp[si][:], ALU.add, [('tmpf', ai), ('rs', si)], [('tmpf', ai)])
                            else:
                                self.tt(mT[:, e, t * 512:(t + 1) * 512], acc[ai][:], tmp[si][:], ALU.add, [('tmpf', ai), ('rs', si)],
                                        [('mT', t)])
            for t in range(G):
                j = g * G + t
                ps_n, pnk = self.psr(6, 512)
                for oc in range(8):
                    wi_ = woc % 3
                    woc += 1
                    self.dma('pool', wo[wi_][:].rearrange("p a b -> p (a b)"), self.wout_t[l, oc], [], [('wot', wi_)])
                    ps, pk = self.psq(512)
                    for kc in range(8):
                        self.mm(ps, wo[wi_][:, kc, :], mT[:, kc, t * 512:(t + 1) * 512], kc == 0, kc == 7, [('wot', wi_), ('mT', t)], pk)
                    self.evac_oT(oc, ps, pk, ps_n, pnk)
                self.post_residual(l, path, j, 2, ps_n, pnk)
        self.pop()

    def branch_mla(self, l, path):
        T_, SC, lat = path.T, path.SC, path.lat
        TK = T_ + SC
        NS = TK // 128
        self.push()
        self.more_wp(5)
        self.attn_tiles()
        cqn = self.T("cqn", [128, 2, T_], BF16)
        ckb = self.T("ckb", [128, TK], BF16)
        Qh = self.T("Qh", [96, T_], BF16)
        Kh = self.T("Kh", [96, TK], BF16)
        va = self.T("va", [128, NS, 65], BF16)
        wuq = self.T("wuq", [128, 2, 384], BF16)
        wukv = self.T("wukv", [128, 512], BF16)
        cqf = [self.T(f"cqf{i}", [128, 512], F32) for i in range(2)]
        ckf = self.T("ckf", [128, 512], F32)
        ktb = self.T("ktb", [32, 512], BF16)
        self.dma('pool', wuq[:], self.w_uq[l].rearrange("(kc p) n -> p kc n", p=128), [], ['wuq'])
        self.dma('pool', wukv[:], self.w_ukv[l], [], ['wukv'])
        if lat:
            wuqs = self.T("wuqs", [128, 2, 128], BF16)
            self.dma('pool', wuqs[:], self.w_uq_sw[l].rearrange("(kc p) n -> p kc n", p=128), [], ['wuqs'])
            rC = self.T("rC", [32, 2048], BF16)
            rS = self.T("rS", [32, 2048], BF16)
            self.dma('pool', rC[:], self.rope_d[0:32, 2, :], [], ['rC'])
            self.dma('pool', rS[:], self.rope_d[0:32, 3, :], [], ['rS'])
        else:
            ckn_f = self.T("ckn_f", [128, T_], F32)
            kpe_f = self.T("kpe_f", [32, T_], F32)
            self.stage = [self.T(f"stage{i}", [128, 128], F32) for i in range(2)]
            self.stc = 0
        self.memset(va[:, :, 64:65], 1.0, ['va'])
        st8 = {}

        def evac(tag, j, ps, pk):
            sl = slice(j * 512, (j + 1) * 512)
            if tag in ('cq0', 'cq1'):
                c = int(tag[2])
                self.acopy(cqf[c][:], ps, pk, [('cqf', c)])
                i = self.cnt % 2
                self.cnt += 1
                self.act(self.sq[i][:], ps, AF.Square, pk, [('sq', i)])
                if c == 0:
                    st8['pn'] = self.psr(7, 512)
                self.mm(st8['pn'][0], self.cc('ones'), self.sq[i][:], c == 0, c == 1, [('sq', i), 'cst'], st8['pn'][1])
                if c == 1:
                    ri = j % 2
                    self.rstd(self.rs[ri][:], st8['pn'][0], st8['pn'][1], ('rs', ri), 256, 'eps_rms')
                    for c2 in range(2):
                        self.stt(cqn[:, c2, sl], cqf[c2][:], self.vv(l, 'qn', c2), self.rs[ri][:], ALU.mult, ALU.mult,
                                 [('cqf', c2), ('rs', ri), 'vecs'], [('cqn', j)])
            elif tag == 'ckv':
                self.acopy(ckf[:], ps, pk, ['ckf'])
                i = self.cnt % 2
                self.cnt += 1
                self.act(self.sq[i][:], ps, AF.Square, pk, [('sq', i)])
                pn, pnk = self.psq(512)
                self.mm(pn, self.cc('ones'), self.sq[i][:], True, True, [('sq', i), 'cst'], pnk)
                i2 = self.cnt % 2
                self.cnt += 1
                self.rstd(self.tmpf[i2][:], pn, pnk, ('tmpf', i2), 128, 'eps_rms')
                if lat:
                    self.stt(ckb[:, sl], ckf[:], self.vv(l, 'kvn'), self.tmpf[i2][:], ALU.mult, ALU.mult,
                             ['ckf', ('tmpf', i2), 'vecs'], [('ckb', j)])
                else:
                    self.stt(ckn_f[:, sl], ckf[:], self.vv(l, 'kvn'), self.tmpf[i2][:], ALU.mult, ALU.mult,
                             ['ckf', ('tmpf', i2), 'vecs'], ['ckn_f'])
                    self.acopy(ckb[:, sl], ckn_f[:, sl], ['ckn_f'], [('ckb', j)])
            elif tag == 'kpe':
                if lat:
                    self.tt(cqf[0][0:32, :], ps, rC[:, sl], ALU.mult, pk + ['rC'], [('cqf', 0)])
                else:
                    self.acopy(kpe_f[:, sl], ps, pk, ['kpe_f'])
                    self.acopy(Kh[64:96, sl], ps, pk, [('Khr', j)])
            elif tag == 'kpes':
                self.tt(cqf[1][0:32, :], ps, rS[:, sl], ALU.mult, pk + ['rS'], [('cqf', 1)])
                self.tt(ktb[:], cqf[0][0:32, :], cqf[1][0:32, :], ALU.add, [('cqf', 0), ('cqf', 1)], ['ktb'])
                self.acopy(Kh[64:96, sl], ktb[:], ['ktb'], [('Khr', j)])

        specs = [('cq0', [self.win(l, 'm_cq', 0, 128)]), ('cq1', [self.win(l, 'm_cq', 128, 128)]),
                 ('ckv', [self.win(l, 'm_ckv', 0, 128)]), ('kpe', [self.win(l, 'm_kpe', 0, 32)])]
        if lat:
            specs.append(('kpes', [(self.w_in_sw[l][:, 384:416], 32)]))
        self.project_group(path, specs, evac)

        if lat:
            cst_ = [self.T(f"cst_{i}", [128, 160], F32) for i in range(2)]
            for st in range(4):
                i = st % 2
                self.dma('sp', cst_[i][:, 0:128], self.c_ckv[l][st * 128:(st + 1) * 128, :], [], [('cst_', i)])
                self.dma('sp', cst_[i][:, 128:160], self.c_kpe[l][st * 128:(st + 1) * 128, :], [], [('cst_', i)])
                ps, pk = self.psq(128)
                self.tr(ps, cst_[i][:, 0:128], self.cc('ident'), [('cst_', i), 'cst'], pk)
                self.acopy(ckb[:, T_ + st * 128:T_ + (st + 1) * 128], ps, pk, [('ckb', 'c')])
                ps, pk = self.psq(128, parts=32)
                self.tr(ps, cst_[i][:, 128:160], self.cc('ident'), [('cst_', i), 'cst'], pk)
                self.acopy(Kh[64:96, T_ + st * 128:T_ + (st + 1) * 128], ps, pk, [('Khr', 'c')])
        else:
            self.out_tok(lambda blk: ckn_f[:, blk * 128:(blk + 1) * 128], 128, 128, path, l,
                         lambda seq, t0: self.o_ckv[seq, l, t0:t0 + 128, :], 'ckn_f')
            self.out_tok(lambda blk: kpe_f[:, blk * 128:(blk + 1) * 128], 32, 32, path, l,
                         lambda seq, t0: self.o_kpe[seq, l, t0:t0 + 128, :], 'kpe_f')

        ckb_keys = [('ckb', j) for j in range(path.NT)] + ([('ckb', 'c')] if lat else [])
        khr_keys = [('Khr', j) for j in range(path.NT)] + ([('Khr', 'c')] if lat else [])
        sc = 96 ** -0.5
        for h in range(4):
            for j in range(path.NT):
                sl = slice(j * 512, (j + 1) * 512)
                ps, pk = self.psq(512, parts=64)
                for c2 in range(2):
                    self.mm(ps, wuq[:, c2, h * 96:h * 96 + 64], cqn[:, c2, sl], c2 == 0, c2 == 1, ['wuq', ('cqn', j)], pk)
                self.acopy(Qh[0:64, sl], ps, pk, ['Qh'])
                ps, pk = self.psq(512, parts=32)
                for c2 in range(2):
                    self.mm(ps, wuq[:, c2, h * 96 + 64:h * 96 + 96], cqn[:, c2, sl], c2 == 0, c2 == 1, ['wuq', ('cqn', j)], pk)
                if lat:
                    self.tt(cqf[0][0:32, :], ps, rC[:, sl], ALU.mult, pk + ['rC'], [('cqf', 0)])
                    ps, pk = self.psq(512, parts=32)
                    for c2 in range(2):
                        self.mm(ps, wuqs[:, c2, h * 32:(h + 1) * 32], cqn[:, c2, sl], c2 == 0, c2 == 1, ['wuqs', ('cqn', j)], pk)
                    self.tt(cqf[1][0:32, :], ps, rS[:, sl], ALU.mult, pk + ['rS'], [('cqf', 1)])
                    self.tt(ktb[:], cqf[0][0:32, :], cqf[1][0:32, :], ALU.add, [('cqf', 0), ('cqf', 1)], ['ktb'])
                    self.acopy(Qh[64:96, sl], ktb[:], ['ktb'], ['Qh'])
                else:
                    self.acopy(Qh[64:96, sl], ps, pk, ['Qh'])
            for jt in range(TK // 512):
                sl = slice(jt * 512, (jt + 1) * 512)
                ps, pk = self.psq(512, parts=64)
                self.mm(ps, wukv[:, h * 128:h * 128 + 64], ckb[:, sl], True, True, ['wukv'] + ckb_keys, pk)
                self.acopy(Kh[0:64, sl], ps, pk, ['Kh'])
            for s8 in range(0, NS, 8):
                ns = min(8, NS - s8)
                ps, pk = self.psq(512)
                for s in range(ns):
                    st = s8 + s
                    self.mm(ps[:, s * 64:(s + 1) * 64], ckb[:, st * 128:(st + 1) * 128], wukv[:, h * 128 + 64:h * 128 + 128],
                            True, True, ['wukv'] + ckb_keys, pk)
                self.acopy(va[:, s8:s8 + ns, 0:64], ps[:, 0:ns * 64].rearrange("p (a b) -> p a b", b=64), pk, ['va'])
            pieces = [(lambda q0, n: Qh[0:96, q0:q0 + n], lambda st: Kh[0:96, st * 128:(st + 1) * 128],
                       ['Qh', 'Kh'] + khr_keys)]
            rows = slice((h % 2) * 64, (h % 2) * 64 + 64)
            self.attn_head(path, pieces, lambda st: va[:, st, :], 'va',
                           lambda q0, n, h=h, rows=rows: (self.yT[rows, 2 + h // 2, q0:q0 + n], h % 2 == 1),
                           ('yT', 2 + h // 2), sc)
        self.pop()

    def branch_gqa(self, l, path):
        T_, SC, lat = path.T, path.SC, path.lat
        TK = T_ + SC
        NS = TK // 128
        self.push()
        self.more_wp(3)
        self.attn_tiles()
        qT = self.T("gq", [128, 2, T_], BF16)
        kT = self.T("gk", [128, TK], BF16)
        va = self.T("gva", [128, NS, 2, 65], BF16)
        f32 = [self.T(f"gf{i}", [128, 512], F32) for i in range(4)]
        self.stage = [self.T(f"stage{i}", [128, 128], F32) for i in range(2)]
        self.stc = 0
        if lat:
            rC = self.T("rC", [128, 2048], BF16)
            rS = self.T("rS", [128, 2048], BF16)
            self.dma('pool', rC[:], self.rope_d[:, 0, :], [], ['rC'])
            self.dma('pool', rS[:], self.rope_d[:, 1, :], [], ['rS'])
        else:
            kn_f = self.T("kn_f", [128, T_], F32)
        self.memset(va[:, :, :, 64:65], 1.0, ['va'])
        oq = IN_OFF['g_q']

        def normed(tag, j, ps, pk, gname, gsw, dst, dkey, fdst=None):
            sl = slice(j * 512, (j + 1) * 512)
            self.acopy(f32[0][:], ps, pk, [('gf', 0)])
            i = self.cnt % 2
            self.cnt += 1
            self.act(self.sq[i][:], ps, AF.Square, pk, [('sq', i)])
            pn, pnk = self.psq(512)
            self.mm(pn, self.cc('bones'), self.sq[i][:], True, True, [('sq', i), 'cst'], pnk)
            ri = self.cnt % 2
            self.cnt += 1
            self.rstd(self.rs[ri][:], pn, pnk, ('rs', ri), 64, 'eps_rms')
            self.cur_rs = ri
            if not lat:
                if fdst is not None:
                    self.stt(fdst[:, sl], f32[0][:], self.vv(l, gname), self.rs[ri][:], ALU.mult, ALU.mult,
                             [('gf', 0), ('rs', ri), 'vecs'], ['kn_f'])
                    self.acopy(dst[:, sl], fdst[:, sl], ['kn_f'], [dkey])
                else:
                    self.stt(dst[:, sl], f32[0][:], self.vv(l, gname), self.rs[ri][:], ALU.mult, ALU.mult,
                             [('gf', 0), ('rs', ri), 'vecs'], [dkey])
            else:
                self.stt(f32[1][:], f32[0][:], self.vv(l, gname), self.rs[ri][:], ALU.mult, ALU.mult,
                         [('gf', 0), ('rs', ri), 'vecs'], [('gf', 1)])
                self.tt(f32[1][:], f32[1][:], rC[:, sl], ALU.mult, [('gf', 1), 'rC'], [('gf', 1)])

        def swapped(j, ps, pk, gsw, dst, dkey):
            sl = slice(j * 512, (j + 1) * 512)
            ri = self.cur_rs
            self.stt(f32[2][:], ps, self.vv(l, gsw), self.rs[ri][:], ALU.mult, ALU.mult, pk + [('rs', ri), 'vecs'], [('gf', 2)])
            self.tt(f32[2][:], f32[2][:], rS[:, sl], ALU.mult, [('gf', 2), 'rS'], [('gf', 2)])
            self.tt(dst[:, sl], f32[1][:], f32[2][:], ALU.add, [('gf', 1), ('gf', 2)], [dkey])

        def evac(tag, j, ps, pk):
            sl = slice(j * 512, (j + 1) * 512)
            if tag in ('q0', 'q1'):
                X = int(tag[1])
                normed(tag, j, ps, pk, 'gq', 'gq_sw', qT[:, X, :], ('gq', X))
            elif tag in ('q0s', 'q1s'):
                X = int(tag[1])
                swapped(j, ps, pk, 'gq_sw', qT[:, X, :], ('gq', X))
            elif tag == 'k':
                normed(tag, j, ps, pk, 'gk', 'gk_sw', kT, 'gk', None if lat else kn_f)
            elif tag == 'ks':
                swapped(j, ps, pk, 'gk_sw', kT, 'gk')
            elif tag == 'v':
                self.acopy(f32[3][:], ps, pk, [('gf', 3)])
                for b4 in range(4):
                    blk = j * 4 + b4
                    pt, ptk = self.psq(128)
                    self.tr(pt, f32[3][:, b4 * 128:(b4 + 1) * 128], self.cc('ident'), [('gf', 3), 'cst'], ptk)
                    self.acopy(va[:, blk, :, 0:64], pt.rearrange("p (a b) -> p a b", b=64), ptk, ['va'])
                    if not lat:
                        i = self.stc % 2
                        self.stc += 1
                        self.vcopy(self.stage[i][:], pt, ptk, [('stage', i)])
                        seq, t0 = blk // 2, (blk % 2) * 128
                        self.dma('sp', self.o_gv[seq, l, t0:t0 + 128, :], self.stage[i][:], [('stage', i)], [])

        def qspec(X, src, off):
            return [(src[l][:, off + X * 64: off + X * 64 + 64], 64), (src[l][:, off + (X + 2) * 64: off + (X + 2) * 64 + 64], 64)]

        for X in range(2):
            specs = [(f'q{X}', qspec(X, self.w_in, oq))]
            if lat:
                specs.append((f'q{X}s', qspec(X, self.w_in_sw, 0)))
            self.project_group(path, specs, evac)
        specs = [('k', [self.win(l, 'g_k', 0, 128)])]
        if lat:
            specs.append(('ks', [(self.w_in_sw[l][:, 256:384], 128)]))
        specs.append(('v', [self.win(l, 'g_v', 0, 128)]))
        self.project_group(path, specs, evac)

        if lat:
            cst_ = [self.T(f"cst_{i}", [128, 128], F32) for i in range(2)]
            for st in range(4):
                i = st % 2
                self.dma('sp', cst_[i][:], self.c_gk[l][st * 128:(st + 1) * 128, :], [], [('cst_', i)])
                ps, pk = self.psq(128)
                self.tr(ps, cst_[i][:], self.cc('ident'), [('cst_', i), 'cst'], pk)
                self.acopy(kT[:, T_ + st * 128:T_ + (st + 1) * 128], ps, pk, ['gk'])
                self.dma('pool', va[:, T_ // 128 + st, :, 0:64],
                         self.c_gv[l][st * 128:(st + 1) * 128, :].rearrange("p (a b) -> p a b", b=64), [], ['va'])
        else:
            self.out_tok(lambda blk: kn_f[:, blk * 128:(blk + 1) * 128], 128, 128, path, l,
                         lambda seq, t0: self.o_gk[seq, l, t0:t0 + 128, :], 'kn_f')

        for h in range(4):
            X, kvh = h % 2, h // 2
            rows = slice(kvh * 64, kvh * 64 + 64)
            pieces = [(lambda q0, n, X=X, rows=rows: qT[rows, X, q0:q0 + n],
                       lambda st, rows=rows: kT[rows, st * 128:(st + 1) * 128], [('gq', X), 'gk'])]
            orow = slice((h % 2) * 64, (h % 2) * 64 + 64)
            self.attn_head(path, pieces, lambda st, kvh=kvh: va[:, st, kvh, :], 'va',
                           lambda q0, n, h=h, orow=orow: (self.yT[orow, 4 + h // 2, q0:q0 + n], h % 2 == 1),
                           ('yT', 4 + h // 2), 0.125)
        self.pop()

    def branch_na(self, l, path):
        T_, SC, lat, NB = path.T, path.SC, path.lat, path.NB
        TK = T_ + SC
        self.push()
        self.attn_tiles()
        va_e = self.T("nva_e", [128, NB, 4, 65], BF16)
        self.memset(va_e[:, :, :, 64:65], 1.0, ['va_e'])
        if lat:
            va_o = self.T("nva_o", [128, NB - 1, 4, 65], BF16)
            va_c = self.T("nva_c", [128, 4, 4, 65], BF16)
            self.memset(va_o[:, :, :, 64:65], 1.0, ['va_o'])
            self.memset(va_c[:, :, :, 64:65], 1.0, ['va_c'])
        self.push()
        self.more_wp(2)
        vfull = self.T("vfull", [128, 2, T_], F32)
        self.stage = [self.T(f"stage{i}", [128, 128], F32) for i in range(2)]
        self.stc = 0

        def evac_v(tag, j, ps, pk):
            c = int(tag[1])
            self.acopy(vfull[:, c, j * 512:(j + 1) * 512], ps, pk, [('vfull', c)])

        self.project_group(path, [('v0', [self.win(l, 'n_v', 0, 128)]), ('v1', [self.win(l, 'n_v', 128, 128)])], evac_v)
        for blk in range(NB):
            for c in range(2):
                pt, ptk = self.psq(128)
                self.tr(pt, vfull[:, c, blk * 128:(blk + 1) * 128], self.cc('ident'), [('vfull', c), 'cst'], ptk)
                self.acopy(va_e[:, blk, 2 * c:2 * c + 2, 0:64], pt.rearrange("p (a b) -> p a b", b=64), ptk, ['va_e'])
                if not lat:
                    i = self.stc % 2
                    self.stc += 1
                    self.vcopy(self.stage[i][:], pt, ptk, [('stage', i)])
                    seq, t0 = blk // 2, (blk % 2) * 128
                    self.dma('sp', self.o_nv[seq, l, t0:t0 + 128, c * 128:(c + 1) * 128], self.stage[i][:], [('stage', i)], [])
        if lat:
            for s in range(NB - 1):
                for c in range(2):
                    pt, ptk = self.psq(128)
                    self.tr(pt, vfull[:, c, 64 + s * 128:192 + s * 128], self.cc('ident'), [('vfull', c), 'cst'], ptk)
                    self.acopy(va_o[:, s, 2 * c:2 * c + 2, 0:64], pt.rearrange("p (a b) -> p a b", b=64), ptk, ['va_o'])
            for st in range(4):
                self.dma('pool', va_c[:, st, :, 0:64],
                         self.c_nv[l][st * 128:(st + 1) * 128, :].rearrange("p (a b) -> p a b", b=64), [], ['va_c'])
        self.pop()
        self.more_wp(4)
        qT = self.T("nq", [128, 2, T_], BF16)
        kT = self.T("nk", [128, 2, TK], BF16)
        if not lat:
            kf = self.T("nkf", [128, 2, T_], F32)
            self.stage = [self.T(f"stage{i}", [128, 128], F32) for i in range(2)]

        def evac(tag, j, ps, pk):
            sl = slice(j * 512, (j + 1) * 512)
            c = int(tag[1])
            if tag[0] == 'q':
                self.act(qT[:, c, sl], ps, AF.Identity, pk, [('nq', c)], scale=0.125)
            else:
                if lat:
                    self.acopy(kT[:, c, sl], ps, pk, [('nk', c)])
                else:
                    self.acopy(kf[:, c, sl], ps, pk, [('nkf', c)])
                    self.vcopy(kT[:, c, sl], kf[:, c, sl], [('nkf', c)], [('nk', c)])

        self.project_group(path, [('q0', [self.win(l, 'n_q', 0, 128)]), ('q1', [self.win(l, 'n_q', 128, 128)]),
                                  ('k0', [self.win(l, 'n_k', 0, 128)]), ('k1', [self.win(l, 'n_k', 128, 128)])], evac)
        if lat:
            cst_ = [self.T(f"cst_{i}", [128, 256], F32) for i in range(2)]
            for st in range(4):
                i = st % 2
                self.dma('sp', cst_[i][:], self.c_nk[l][st * 128:(st + 1) * 128, :], [], [('cst_', i)])
                for c in range(2):
                    ps, pk = self.psq(128)
                    self.tr(ps, cst_[i][:, c * 128:(c + 1) * 128], self.cc('ident'), [('cst_', i), 'cst'], pk)
                    self.acopy(kT[:, c, T_ + st * 128:T_ + (st + 1) * 128], ps, pk, [('nk', c)])
        else:
            for c in range(2):
                self.out_tok(lambda blk, c=c: kf[:, c, blk * 128:(blk + 1) * 128], 128, 128, path, l,
                             lambda seq, t0, c=c: self.o_nk[seq, l, t0:t0 + 128, c * 128:(c + 1) * 128], ('nkf', c))

        if not lat:
            for h in range(4):
                c = h // 2
                rows = slice((h % 2) * 64, (h % 2) * 64 + 64)
                pieces = [(lambda q0, n, c=c, rows=rows: qT[rows, c, q0:q0 + n],
                           lambda st, c=c, rows=rows: kT[rows, c, st * 128:(st + 1) * 128], [('nq', c), ('nk', c)])]
                self.attn_head(path, pieces, lambda st, h=h: va_e[:, st, h, :], 'va_e',
                               lambda q0, n, h=h, rows=rows: (self.yT[rows, 6 + h // 2, q0:q0 + n], h % 2 == 1),
                               ('yT', 6 + h // 2), 1.0)
        else:
            BT = self.T("nBT", [128, 8, 4, 64], BF16)
            for h in range(4):
                c = h // 2
                rows = slice((h % 2) * 64, (h % 2) * 64 + 64)
                self.dma('pool', BT[:].rearrange("p d s q -> p (d s q)"), self.nab_d[l][h], [], ['nBT'])
                inflight = {}

                def emit_scores(r, h=h, c=c, rows=rows):
                    r0 = min(max(r - 4, 0), 24)
                    dr = r - r0
                    k0 = r0 * 64
                    qs = qT[rows, c, r * 64:(r + 1) * 64]
                    ps, pk = self.psq(512)
                    for i in range(4):
                        self.mm(ps[:, i * 64:(i + 1) * 64], kT[rows, c, k0 + i * 128:k0 + (i + 1) * 128], qs, True, False,
                                [('nq', c), ('nk', c)], pk)
                        self.mm(ps[:, i * 64:(i + 1) * 64], self.identb[:], BT[:, dr, i, :], False, True, ['identb', 'nBT'], pk)
                    for i in range(4):
                        self.mm(ps[:, (4 + i) * 64:(5 + i) * 64], kT[rows, c, T_ + i * 128:T_ + (i + 1) * 128], qs, True, True,
                                [('nq', c), ('nk', c)], pk)
                    inflight[r] = (ps, pk)

                NLA = 2
                for r_ in range(NLA):
                    emit_scores(r_)
                for r in range(32):
                    if r + NLA < 32:
                        emit_scores(r + NLA)
                    r0 = min(max(r - 4, 0), 24)
                    ps, pk = inflight.pop(r)
                    pi = self.pcnt % 3
                    self.pcnt += 1
                    self.act(self.pT[pi][:], ps, AF.Exp, pk, [('pT', pi)])
                    po, pok = self.psr(6 + self.ocnt % 2, 64, parts=65)
                    self.ocnt += 1
                    for i in range(8):
                        if i < 4:
                            if r0 % 2 == 0:
                                v, vk = va_e[:, r0 // 2 + i, h, :], 'va_e'
                            else:
                                v, vk = va_o[:, (r0 - 1) // 2 + i, h, :], 'va_o'
                        else:
                            v, vk = va_c[:, i - 4, h, :], 'va_c'
                        self.mm(po, v, self.pT[pi][:, i * 64:(i + 1) * 64], i == 0, i == 7, [('pT', pi), vk], pok)
                    self.attn_norm(po, pok, 64, (self.yT[rows, 6 + h // 2, r * 64:(r + 1) * 64], h % 2 == 1), ('yT', 6 + h // 2))
        self.pop()

    def branch_rwkv(self, l, path):
        T_, lat, NB, nseq = path.T, path.lat, path.NB, path.nseq
        NBS = NB // nseq
        self.S.barrier()
        self.push()
        self.more_wp(3)
        bigs = [self.sq[0], self.sq[1], self.rs[0], self.rs[1], self.tmpf[0], self.tmpf[1]]
        rw = [t[:, q * 128:(q + 1) * 128] for t in bigs for q in range(4)]
        RK = lambda n: ('rw', n)
        csr = self.T("csr", [128, 512], F32)
        csrb = self.T("csrb", [128, NCSR - 512], BF16)
        self.dma('sp', csr[:], self.csr_d[:, 0:512], [], ['csr'])
        self.dma('pool', csrb[:], self.csr_d[:, 512:NCSR], [], ['csr'])

        def cr(name, n=128):
            o = CSR[name]
            return csr[:, o:o + n] if o < 512 else csrb[:, o - 512:o - 512 + n]

        w2t = self.T("w2t", [128, 256], BF16)
        a2t = self.T("a2t", [128, 256], BF16)
        g2t = self.T("g2t", [128, 256], BF16)
        self.dma('pool', w2t[:], self.w2_d[l], [], ['w2t'])
        self.dma('pool', a2t[:], self.a2_d[l], [], ['a2t'])
        self.dma('pool', g2t[:], self.g2_d[l], [], ['g2t'])
        tanhT = self.T("tanhT", [128, T_], BF16)
        aloT = self.T("aloT", [128, T_], BF16)
        sgT = self.T("sgT", [128, T_], BF16)
        psbf = self.psbf

        def evac0(tag, j, ps, pk):
            sl = slice(j * 512, (j + 1) * 512)
            if tag == 'wlo':
                self.act(tanhT[:, sl], ps, AF.Tanh, pk, ['tanhT'])
            elif tag == 'alo':
                self.acopy(aloT[:, sl], ps, pk, ['aloT'])
            else:
                self.act(sgT[:, sl], ps, AF.Sigmoid, pk, ['sgT'])

        self.project_group(path, [('wlo', [self.win(l, 'a_wlo', 0, 128)]), ('alo', [self.win(l, 'a_alo', 0, 128)]),
                                  ('glo', [self.win(l, 'a_glo', 0, 128)])], evac0)
        rT = self.T("rT", [128, T_], BF16)
        kT = self.T("kT", [128, T_], F32)
        vT = self.T("vT", [128, T_], BF16)
        yacc = self.T("yacc", [128, NB, 128], F32)
        bonv = self.T("bonv", [128, T_], BF16)
        nch = nseq * 2
        ST32 = self.T("ST32", [128, nch, 64], F32)
        STb = self.T("STb", [128, nch, 64], BF16)
        rows = self.T("rows", [1, 256], F32)
        ones1 = self.cst[0:1, CST['ones']:CST['ones'] + 128]
        carve = {'c': 2, 'o': 0}

        def balloc(name, cols):
            if carve['c'] < 8 and carve['o'] + cols > T_:
                carve['c'] += 1
                carve['o'] = 0
            if carve['c'] < 8:
                ap = self.yT[:, carve['c'], carve['o']:carve['o'] + cols]
                carve['o'] += cols
                return ap
            assert not lat
            return self.T(name, [128, cols], BF16)[:]

        NSLOT = 2 if lat else 4
        rwx = [self.T(f"rwx{i}", [128, 128], F32)[:] for i in range(22)] if not lat else []
        slots = []
        for s in range(NSLOT):
            sl_ = dict(idx=s)
            for nm, cols in (('AR', 256), ('Bh', 128), ('Kh', 128), ('BKt', 256), ('R01', 256), ('TM', 512), ('App', 128),
                             ('UT', 128), ('S0k', 128)):
                sl_[nm] = balloc(f"{nm}{s}", cols)
            for hh in range(2):
                for nm, cols in (('NP', 256), ('MP', 256), ('NpT0', 128), ('NpT1', 128), ('Npw0', 128), ('Npw1', 128),
                                 ('Tac0', 128), ('Tac1', 128), ('Xt', 64)):
                    sl_[(nm, hh)] = balloc(f"{nm}{s}{hh}", cols)
            sl_['tmpS'] = self.T(f"tmpS{s}", [128, 64], F32)
            sl_['lwc'] = self.T(f"lwc{s}", [128, 2], F32)
            if s < 2:
                sl_['rw'] = rw[s * 11:(s + 1) * 11]
            else:
                sl_['rw'] = rwx[(s - 2) * 11:(s - 1) * 11]
            sl_['rwk'] = [RK(s * 11 + i if s < 2 else 100 + s * 11 + i) for i in range(11)]
            slots.append(sl_)

        def visit_gen(S_, c2, seq, d, blk):
            s = S_['idx']
            K_ = lambda nm: (nm, s)
            tok = slice(blk * 128, (blk + 1) * 128)
            ch = seq * 2 + d
            cN = slice(c2 * 128, (c2 + 1) * 128)
            R, RKs = S_['rw'], S_['rwk']
            kkraw, sqk, rn, an, tmp, b_ = R[0], R[1], R[2], R[3], R[4], R[5]
            kkraw_k, sqk_k, rn_k, an_k, tmp_k, b_k = RKs[0], RKs[1], RKs[2], RKs[3], RKs[4], RKs[5]
            alr, alr_k = [R[6], R[7]], [RKs[6], RKs[7]]
            kd, kd_k = [R[8], R[9]], [RKs[8], RKs[9]]
            eWp, eWp_k = R[10], RKs[10]
            sg, sg_k = R[0], RKs[0]
            rk, rk_k = R[1], RKs[1]
            ksum, ksum_k = R[2], RKs[2]
            eWt, eWt_k = R[4], RKs[4]
            eW, eW_k = R[6], RKs[6]
            eWi, eWi_k = R[7], RKs[7]
            AR, Bh, Kh, BKt, R01, TM, App, UT, S0k = (S_[n] for n in ('AR', 'Bh', 'Kh', 'BKt', 'R01', 'TM', 'App', 'UT', 'S0k'))
            tmpS, lwc = S_['tmpS'], S_['lwc']
            self.ts(kkraw, kT[:, tok], self.vv(l, 'k_k', c2), None, ALU.mult, None, ['kT', 'vecs'], [kkraw_k])
            self.act(sqk, kkraw, AF.Square, [kkraw_k], [sqk_k])
            ps, pk = self.psq(128)
            self.mm(ps, self.cc('bones'), sqk, True, True, [sqk_k, 'cst'], pk)
            self.act(rn, ps, AF.Sqrt, pk + ['cst'], [rn_k], bias=self.cc('eps_kk', 1), scale=1.0)
            self.recip(rn, rn, [rn_k], [rn_k])
            self.stt(an, kkraw, -1.0, rn, ALU.mult, ALU.mult, [kkraw_k, rn_k], [an_k])
            yield
            for dd in ([0, 1] if d == 0 else [1]):
                ps, pk = self.psq(128)
                self.mm(ps, a2t[64 * dd:64 * dd + 64, cN], aloT[64 * dd:64 * dd + 64, tok], True, True, ['a2t', 'aloT'], pk)
                self.act(alr[dd], ps, AF.Sigmoid, pk + ['vecs'], [alr_k[dd]], bias=self.vv(l, 'a0', dd * 2 + c2))
                self.ts(tmp, alr[dd], -1.0, self.vv(l, 'k_a', c2), ALU.add, ALU.mult, [alr_k[dd], 'vecs'], [tmp_k])
                self.stt(kd[dd], tmp, 1.0, kT[:, tok], ALU.add, ALU.mult, [tmp_k, 'kT'], [kd_k[dd]])
            self.stt(b_, an, -1.0, alr[d], ALU.mult, ALU.mult, [an_k, alr_k[d]], [b_k])
            yield
            if d == 0:
                self.ts(rk, rT[:, tok], self.vv(l, 'r_k', c2), None, ALU.mult, None, ['rT', 'vecs'], [rk_k])
                self.tt(ksum, kd[0], kd[1], ALU.add, [kd_k[0], kd_k[1]], [ksum_k])
                self.tt(rk, rk, ksum, ALU.mult, [rk_k, ksum_k], [rk_k])
                ps, pk = self.psq(128)
                self.mm(ps, self.cc('bones'), rk, True, True, [rk_k, 'cst'], pk)
                self.stt(bonv[:, tok], ps, 0.5, vT[:, tok], ALU.mult, ALU.mult, pk + ['vT'], ['bonv'])
                yield
            ps, pk = self.psq(128)
            self.mm(ps, tanhT[64 * d:64 * d + 64, tok], w2t[64 * d:64 * d + 64, cN], True, False, ['tanhT', 'w2t'], pk)
            self.mm(ps, ones1, rows[0:1, d * 128:(d + 1) * 128], False, True, ['rows', 'cst'], pk)
            self.act(sg, ps, AF.Sigmoid, pk, [sg_k])
            plw, plwk = self.psq(128)
            self.mm(plw, sg, cr(f'TI{d}'), True, True, [sg_k, 'csr'], plwk)
            plp, plpk = self.psq(128)
            self.mm(plp, sg, cr(f'TS{d}'), True, True, [sg_k, 'csr'], plpk)
            lcs = [63, 127] if d == 0 else [0, 64]
            self.act(eW, plw, AF.Exp, plwk, [eW_k])
            self.act(eWi, plw, AF.Exp, plwk, [eWi_k], scale=-1.0)
            for cc in range(2):
                self.vcopy(lwc[:, cc:cc + 1], plw[:, lcs[cc]:lcs[cc] + 1], plwk, [K_('lwc')])
            for cc in range(2):
                self.act(eWt[:, cc * 64:(cc + 1) * 64], plw[:, cc * 64:(cc + 1) * 64], AF.Exp, plwk + [K_('lwc')], [eWt_k],
                         bias=lwc[:, cc:cc + 1], scale=-1.0)
            self.act(eWp, plp, AF.Exp, plpk, [eWp_k])
            yield
            self.tt(AR[:, 0:128], an, eWp, ALU.mult, [an_k, eWp_k], [K_('AR')])
            self.tt(AR[:, 128:256], rT[:, tok], eW, ALU.mult, ['rT', eW_k], [K_('AR')])
            self.tt(Bh, b_, eWi, ALU.mult, [b_k, eWi_k], [K_('Bh')])
            self.tt(Kh, kd[d], eWi, ALU.mult, [kd_k[d], eWi_k], [K_('Kh')])
            self.tt(BKt[:, 0:128], b_, eWt, ALU.mult, [b_k, eWt_k], [K_('BKt')])
            self.tt(BKt[:, 128:256], kd[d], eWt, ALU.mult, [kd_k[d], eWt_k], [K_('BKt')])
            self.tt(R01[:, 0:128], AR[:, 128:256], cr('cm0'), ALU.mult, [K_('AR'), 'csr'], [K_('R01')])
            self.tt(R01[:, 128:256], AR[:, 128:256], cr('cm1'), ALU.mult, [K_('AR'), 'csr'], [K_('R01')])
            yield
            pbk = [('ps', 5)]
            for i, (src, k) in enumerate(((AR[:, 0:128], K_('AR')), (BKt[:, 0:128], K_('BKt')), (BKt[:, 128:256], K_('BKt')),
                                          (vT[:, tok], 'vT'))):
                self.tr(psbf[:, i * 128:(i + 1) * 128], src, self.identb[:], [k, 'identb'], pbk)
            self.acopy(TM, psbf[:, 0:512], pbk, [K_('TM')])
            yield
            WCs = [eW[:, lcs[cc]:lcs[cc] + 1] for cc in range(2)]
            msi = cr(f'MS{d}', 256)
            mst = cr(f'MS{1 - d}')
            py, pyk = self.psr(6 + s // 2, 128, off=(s % 2) * 128)

            def head_gen(hh):
                fr = slice(64 * hh, 64 * hh + 64)
                hc = slice(64 * hh, 64 * hh + 64)
                sk = ('ST', ch, hh)
                H_ = lambda nm: (nm, s, hh)
                NP, MP, Xt = S_[('NP', hh)], S_[('MP', hh)], S_[('Xt', hh)]
                NpT = [S_[('NpT0', hh)], S_[('NpT1', hh)]]
                Npw = [S_[('Npw0', hh)], S_[('Npw1', hh)]]
                Tac = [S_[('Tac0', hh)], S_[('Tac1', hh)]]
                pa, pak = self.psq(256)
                self.mm(pa, Bh[fr, :], AR[fr, :], True, True, [K_('Bh'), K_('AR')], pak)
                self.tt(NP, pa, msi, ALU.mult, pak + ['csr'], [H_('NP')])
                yield
                pb2, pb2k = self.psq(256)
                self.mm(pb2, Kh[fr, :], AR[fr, :], True, True, [K_('Kh'), K_('AR')], pb2k)
                self.tt(MP, pb2, msi, ALU.mult, pb2k + ['csr'], [H_('MP')])
                yield
                pc, pck = self.psq(128)
                self.mm(pc, AR[fr, 0:128], Bh[fr, :], True, True, [K_('AR'), K_('Bh')], pck)
                self.tt(NpT[0], pc, mst, ALU.mult, pck + ['csr'], [H_('NpT0')])
                self.tt(Tac[0], NP[:, 0:128], self.identb[:], ALU.add, [H_('NP'), 'identb'], [H_('Tac0')])
                yield
                Ncur, Nk = NP[:, 0:128], H_('NP')
                ti = 0
                ni = 0
                for p in (2, 4, 8, 16, 32):
                    pT_, pTk = self.psq(128)
                    self.mm(pT_, Ncur, NpT[ni], True, True, [Nk, H_(f'NpT{ni}')], pTk)
                    self.acopy(NpT[1 - ni], pT_, pTk, [H_(f'NpT{1 - ni}')])
                    if p < 32:
                        pN, pNk = self.psq(128)
                        self.mm(pN, NpT[ni], Ncur, True, True, [Nk, H_(f'NpT{ni}')], pNk)
                        wi_ = (p.bit_length()) % 2
                        self.vcopy(Npw[wi_], pN, pNk, [H_(f'Npw{wi_}')])
                    yield
                    pU, pUk = self.psq(128)
                    self.mm(pU, NpT[1 - ni], Tac[ti], True, True, [H_(f'NpT{1 - ni}'), H_(f'Tac{ti}')], pUk)
                    self.tt(Tac[1 - ti], pU, Tac[ti], ALU.add, pUk + [H_(f'Tac{ti}')], [H_(f'Tac{1 - ti}')])
                    ti = 1 - ti
                    ni = 1 - ni
                    if p < 32:
                        Ncur, Nk = Npw[wi_], H_(f'Npw{wi_}')
                    yield
                Tinv, Tk = Tac[ti], H_(f'Tac{ti}')
                px, pxk = self.psq(64)
                self.mm(px, MP[:, 0:128], TM[:, 384 + 64 * hh:384 + 64 * hh + 64], True, True, [H_('MP'), K_('TM')], pxk)
                self.vcopy(Xt, px, pxk, [H_('Xt')])
                pp, ppk = self.psq(128, parts=64)
                self.mm(pp, TM[:, 64 * hh:64 * hh + 64], Tinv, True, True, [K_('TM'), Tk], ppk)
                self.acopy(App[fr, :], pp, ppk, [H_('App')])
                yield
                for ci, cc in enumerate((0, 1) if d == 0 else (1, 0)):
                    tr_ = slice(64 * cc, 64 * cc + 64)
                    self.vcopy(S0k[fr, cc * 64:(cc + 1) * 64], STb[fr, ch, :], [sk], [H_('S0k')])
                    pu, puk = self.psq(64, parts=64)
                    self.mm(pu, Tinv[:, tr_], Xt, True, False, [Tk, H_('Xt')], puk)
                    self.mm(pu, App[fr, tr_], STb[fr, ch, :], False, True, [H_('App'), sk], puk)
                    self.acopy(UT[tr_, hc], pu, puk, [H_('UT')])
                    yield
                    pS, pSk = self.psq(64, parts=64)
                    self.mm(pS, TM[tr_, 128 + 64 * hh:128 + 64 * hh + 64], UT[tr_, hc], True, False, [K_('TM'), H_('UT')], pSk)
                    self.mm(pS, TM[tr_, 256 + 64 * hh:256 + 64 * hh + 64], TM[tr_, 384 + 64 * hh:384 + 64 * hh + 64], False, True,
                            [K_('TM')], pSk)
                    if hh == 0:
                        self.stt(ST32[fr, ch, :], ST32[fr, ch, :], WCs[cc][fr, :], pS, ALU.mult, ALU.add, [sk, eW_k] + pSk, [sk])
                    else:
                        self.acopy(tmpS[fr, :], pS, pSk, [K_('tmpS')])
                        self.stt(ST32[fr, ch, :], ST32[fr, ch, :], WCs[cc][fr, :], tmpS[fr, :], ALU.mult, ALU.add,
                                 [sk, eW_k, K_('tmpS')], [sk])
                    self.acopy(STb[fr, ch, :], ST32[fr, ch, :], [sk], [sk])
                    yield
                self.mm(py[:, hc], R01[fr, 0:128], S0k[fr, 0:64], True, False, [K_('R01'), H_('S0k')], pyk)
                self.mm(py[:, hc], R01[fr, 128:256], S0k[fr, 64:128], False, False, [K_('R01'), H_('S0k')], pyk)
                self.mm(py[:, hc], NP[:, 128:256], UT[:, hc], False, False, [H_('NP'), H_('UT')], pyk)
                self.mm(py[:, hc], MP[:, 128:256], TM[:, 384 + 64 * hh:384 + 64 * hh + 64], False, True, [H_('MP'), K_('TM')], pyk)

            hg = [head_gen(0), head_gen(1)]
            while hg:
                for g in list(hg):
                    try:
                        next(g)
                    except StopIteration:
                        hg.remove(g)
                yield
            self.tt(yacc[:, blk, :], yacc[:, blk, :], py, ALU.add, [('yacc', blk)] + pyk, [('yacc', blk)])

        def stream_gen(S_, c2, seq, d):
            for i in range(NBS):
                blk = seq * NBS + (i if d == 0 else NBS - 1 - i)
                yield from visit_gen(S_, c2, seq, d, blk)

        for c2 in range(2):
            cN = slice(c2 * 128, (c2 + 1) * 128)

            def evac1(tag, j, ps, pk):
                sl = slice(j * 512, (j + 1) * 512)
                dst, key = {'r': (rT, 'rT'), 'k': (kT, 'kT'), 'v': (vT, 'vT')}[tag]
                self.acopy(dst[:, sl], ps, pk, [key])

            self.project_group(path, [('r', [self.win(l, 'a_r', c2 * 128, 128)]), ('k', [self.win(l, 'a_k', c2 * 128, 128)]),
                                      ('v', [self.win(l, 'a_v', c2 * 128, 128)])], evac1)
            for d in range(2):
                self.dma('sp', rows[0:1, d * 128:(d + 1) * 128], self.rows_d[0:1, l, d * 256 + c2 * 128:d * 256 + (c2 + 1) * 128], [], ['rows'])
            for blk in range(NB):
                self.memset(yacc[:, blk, :], 0.0, [('yacc', blk)])
            stin = rw[22][0:64, :]
            for seq in range(nseq):
                for d in range(2):
                    ch = seq * 2 + d
                    keys = [('ST', ch, 0), ('ST', ch, 1)]
                    if lat:
                        self.dma('sp', stin.rearrange("v (h k) -> v h k", h=2),
                                 self.st_in[l, d, 2 * c2:2 * c2 + 2].rearrange("h v k -> v h k"), [], [RK(22)])
                        ps, pk = self.psq(64)
                        self.tr(ps, stin, self.cst[0:64, CST['ident']:CST['ident'] + 64], [RK(22), 'cst'], pk)
                        self.acopy(ST32[:, ch, :], ps, pk, keys)
                        self.vcopy(STb[:, ch, :], ST32[:, ch, :], keys, keys)
                    else:
                        self.memset(ST32[:, ch, :], 0.0, keys)
                        self.memset(STb[:, ch, :], 0.0, keys)
                if NSLOT == 4 and seq % 2 == 0:
                    pending = [stream_gen(slots[0], c2, seq, 0), stream_gen(slots[1], c2, seq, 1)]
                    continue
                if NSLOT == 4:
                    gens = pending + [stream_gen(slots[2], c2, seq, 0), stream_gen(slots[3], c2, seq, 1)]
                else:
                    gens = [stream_gen(slots[0], c2, seq, 0), stream_gen(slots[1], c2, seq, 1)]
                while gens:
                    for g in list(gens):
                        try:
                            next(g)
                        except StopIteration:
                            gens.remove(g)
                if not lat:
                    for sq_ in ([seq - 1, seq] if NSLOT == 4 else [seq]):
                        for d in range(2):
                            ch = sq_ * 2 + d
                            keys = [('ST', ch, 0), ('ST', ch, 1)]
                            ps, pk = self.psq(128, parts=64)
                            self.tr(ps, ST32[:, ch, :], self.cc('ident'), keys + ['cst'], pk)
                            self.acopy(stin, ps, pk, [RK(22)])
                            self.dma('sp', self.st_out[sq_, l, d, 2 * c2:2 * c2 + 2].rearrange("h v k -> v h k"),
                                     stin.rearrange("v (h k) -> v h k", h=2), [RK(22)], [])
            for blk in range(NB):
                tok = slice(blk * 128, (blk + 1) * 128)
                y = yacc[:, blk, :]
                i4 = (blk % 2) * 4
                ysq, yn, r19, r23 = rw[i4 + 0], rw[i4 + 1], rw[i4 + 2], rw[i4 + 3]
                k17, k18, k19, k23 = RK(i4 + 0), RK(i4 + 1), RK(i4 + 2), RK(i4 + 3)
                pt, ptk = self.psq(128)
                self.tr(pt, y, self.cc('ident'), [('yacc', blk), 'cst'], ptk)
                self.acopy(ysq, pt, ptk, [k17])
                pm, pmk = self.psq(128)
                self.mm(pm, self.cc('bones'), ysq, True, True, [k17, 'cst'], pmk)
                self.stt(yn, pm, -1.0 / 64, ysq, ALU.mult, ALU.add, pmk + [k17], [k18])
                self.act(ysq, yn, AF.Square, [k18], [k17])
                pv, pvk = self.psq(128)
                self.mm(pv, self.cc('bones'), ysq, True, True, [k17, 'cst'], pvk)
                self.act(r23, pv, AF.Sqrt, pvk + ['cst'], [k23], bias=self.cc('eps_gn', 1), scale=1.0 / 64)
                self.recip(r23, r23, [k23], [k23])
                self.tt(yn, yn, r23, ALU.mult, [k18, k23], [k18])
                self.act(r19, yn, AF.Identity, [k18, 'vecs'], [k19], bias=self.vv(l, 'ln_b', c2), scale=self.vv(l, 'ln_g', c2))
                pg, pgk = self.psq(128)
                self.mm(pg, g2t[:, cN], sgT[:, tok], True, True, ['g2t', 'sgT'], pgk)
                self.tt(r19, r19, bonv[:, tok], ALU.add, [k19, 'bonv'], [k19])
                self.tt(self.yT[:, c2, tok], r19, pg, ALU.mult, [k19] + pgk, [('yT', c2)])
        self.pop()


_CACHE = {}


def _get_nc(cfg_key, cfg):
    if cfg_key not in _CACHE:
        nc = bass.Bass("TRN2", target_bir_lowering=False)
        p = Prog(nc, cfg)
        p.build()
        _CACHE[cfg_key] = (nc, p)
    return _CACHE[cfg_key]


def _host_inputs(inp):
    f = lambda a: np.ascontiguousarray(np.asarray(a, dtype=np.float32))
    g = {k: f(v) for k, v in inp.items()}
    shared = {}
    vec = np.zeros((L, 128, NVEC), np.float32)

    def put(name, arr):
        vec[:, :, VEC[name]:VEC[name] + arr.shape[2]] = arr
    put('b_mod', _fm(g['b_mod']))
    put('g_mpre', _fm(g['norm_mix_pre']))
    put('g_mpost', _fm(g['norm_mix_post']))
    put('g_fpre', _fm(g['norm_ffn_pre']))
    put('g_fpost', _fm(g['norm_ffn_post']))
    put('a0', _fm(g['rwkv_a0'].reshape(L, 512)))
    put('k_k', _fm(g['rwkv_k_k']))
    put('k_a', _fm(g['rwkv_k_a']))
    put('r_k', _fm(g['rwkv_r_k'].reshape(L, 256)))
    put('qn', _fm(g['mla_q_norm']))
    put('kvn', _fm(g['mla_kv_norm']))
    put('gq', _fm(np.tile(g['gqa_q_norm'], (1, 2))))
    put('gq_sw', _fm(np.tile(_swap_halves(g['gqa_q_norm'], 64), (1, 2))))
    put('gk', _fm(np.tile(g['gqa_k_norm'], (1, 2))))
    put('gk_sw', _fm(np.tile(_swap_halves(g['gqa_k_norm'], 64), (1, 2))))
    put('ln_g', _fm(g['rwkv_ln_g']))
    put('ln_b', _fm(g['rwkv_ln_b']))
    shared['vecs'] = np.ascontiguousarray(vec.transpose(1, 0, 2))
    shared['cst'], shared['csr'] = _consts()
    for k in ('w_in', 'mla_w_uq', 'mla_w_ukv'):
        shared[k] = g[k]
    ca = np.ascontiguousarray
    shared['w_mod_t'] = ca(g['w_mod'].reshape(L, 8, 128, 12, 512).transpose(0, 3, 2, 1, 4)).reshape(L, 12, 128, 8 * 512)
    gu = np.concatenate([g['ffn_w_gate'].reshape(L, 8, 128, NF // 2, 256), g['ffn_w_up'].reshape(L, 8, 128, NF // 2, 256)], axis=-1)
    shared['ffn_gu_t'] = ca(gu.transpose(0, 3, 2, 1, 4)).reshape(L, NF // 2, 128, 8 * 512)
    shared['ffn_d_t'] = ca(g['ffn_w_down'].reshape(L, NF, 128, 4, 256).transpose(0, 3, 2, 1, 4)).reshape(L, 4, 128, NF * 256)
    gt = g['w_in'][:, :, IN_OFF['gate']:].reshape(L, 8, 128, 4, 8, 128)
    shared['gate_t'] = ca(gt.transpose(0, 4, 2, 3, 1, 5)).reshape(L, 8, 128, 4 * 8 * 128)
    wb = g['w_branch'].reshape(L, 4, 2, 128, 8, 128)
    shared['wbr_t'] = ca(wb.transpose(0, 4, 3, 1, 2, 5)).reshape(L, 8, 128, 8 * 128)
    wo = g['w_out'].reshape(L, 8, 128, 8, 128)
    shared['wout_t'] = ca(wo.transpose(0, 3, 2, 1, 4)).reshape(L, 8, 128, 8 * 128)
    wi = g['w_in']
    o = IN_OFF
    shared['w_in_sw'] = np.concatenate([
        _swap_halves(wi[:, :, o['g_q']:o['g_q'] + 256], 64),
        _swap_halves(wi[:, :, o['g_k']:o['g_k'] + 128], 64),
        _swap_halves(wi[:, :, o['m_kpe']:o['m_kpe'] + 32], 32)], axis=2)
    uq = g['mla_w_uq'].reshape(L, 256, 4, 96)
    shared['mla_w_uq_sw'] = np.ascontiguousarray(_swap_halves(uq[:, :, :, 64:96], 32).reshape(L, 256, 128))
    rows = np.concatenate([g['rwkv_w0'].reshape(L, 512), g['rwkv_ln_g'], g['rwkv_ln_b']], axis=1)
    shared['rows'] = np.ascontiguousarray(np.broadcast_to(rows[None], (128, L, 1024)))
    shared['rwkv_w2'] = g['rwkv_w2'].reshape(L, 128, 256)
    shared['rwkv_a2'] = g['rwkv_a2'].reshape(L, 128, 256)
    shared['rwkv_g2'] = g['rwkv_g2']
    rope = np.zeros((128, 4, 2048), np.float32)
    C, S_ = _rope_tables(2048, 64)
    rope[:, 0] = np.concatenate([C, C], axis=0)
    rope[:, 1] = np.concatenate([S_, S_], axis=0)
    C, S_ = _rope_tables(2048, 32)
    rope[0:32, 2] = C
    rope[0:32, 3] = S_
    shared['rope'] = rope
    shared['nab'] = _na_bias_table(g['na_rel_bias']).reshape(L, 4, 128, -1)
    in_maps = []
    for i in range(8):
        b = i % 2
        m = dict(shared)
        m['xc'] = np.ascontiguousarray(g['x_prompt'][4 * i:4 * i + 4].reshape(1024, D))
        m['xl'] = g['x_sample'][b]
        cv = np.stack([g['c_ctx'], g['c'][b]], axis=-1)
        m['cvec'] = np.ascontiguousarray(cv.reshape(8, 128, 2).transpose(1, 0, 2))
        m['st_in'] = g['state_rwkv'][b]
        m['c_ckv'] = g['cache_mla_ckv'][b]
        m['c_kpe'] = g['cache_mla_kpe'][b]
        m['c_gk'] = g['cache_gqa_k'][b].reshape(L, 512, 128)
        m['c_gv'] = g['cache_gqa_v'][b].reshape(L, 512, 128)
        m['c_nk'] = g['cache_na_k'][b].reshape(L, 512, 256)
        m['c_nv'] = g['cache_na_v'][b].reshape(L, 512, 256)
        in_maps.append(m)
    return in_maps


def run(inputs, cfg=None, cores=8):
    cfg = dict(cfg or {})
    key = tuple(sorted(cfg.items()))
    nc, prog = _get_nc(key, cfg)
    in_maps = _host_inputs(inputs)[:cores]
    in_maps = [{k: v for k, v in m.items() if k in prog.io} for m in in_maps]
    res = run_bass_kernel_spmd(nc, in_maps, core_ids=list(range(cores)))
    return res.results


def kernel(**inputs):
    r = run(inputs)
    y_prompt = np.concatenate([r[i]['yc'].reshape(4, 256, D) for i in range(8)], axis=0)
    y_sample = np.stack([r[0]['yl'], r[1]['yl']], axis=0)
    cat = lambda k, shp: np.concatenate([r[i][k].reshape((4,) + shp) for i in range(8)], axis=0)
    return (y_prompt, y_sample,
            cat('st_out', (L, 2, 4, 64, 64)),
            cat('o_ckv', (L, 256, 128)),
            cat('o_kpe', (L, 256, 32)),
            cat('o_gk', (L, 256, 2, 64)),
            cat('o_gv', (L, 256, 2, 64)),
            cat('o_nk', (L, 256, 4, 64)),
            cat('o_nv', (L, 256, 4, 64)))
```

```python
import math
from contextlib import ExitStack
import numpy as np
import concourse.bass as bass
import concourse.mybir as mybir
from concourse.bass_utils import run_bass_kernel_spmd

F32 = mybir.dt.float32
BF16 = mybir.dt.bfloat16
AF = mybir.ActivationFunctionType
ALU = mybir.AluOpType
AX = mybir.AxisListType

L = 4
D = 1024
FF = 2816
NF = FF // 128
EPS = 1e-6
N_IN = 6944

ENGS = ['pe', 'act', 'dve', 'pool', 'sp']
SEG = 4000
DMA_K = 8
DMA_USES = 200
SAME_ENGINE_SYNC = ('dve', 'act')


class Op:
    __slots__ = ('eng', 'fn', 'deps', 'dma', 'signal', 'sig', 'dj', 'idx')

    def __init__(self, eng, fn, dma):
        self.eng = eng
        self.fn = fn
        self.deps = []
        self.dma = dma
        self.signal = False
        self.sig = None
        self.dj = None


class Sched:
    def __init__(self, nc, stack):
        self.nc = nc
        self.stack = stack
        self.ops = {e: [] for e in ENGS}
        self.buf = {}
        self.sems = {}
        self.bar = {e: [] for e in ENGS}

    def sem(self, key):
        s = self.sems.get(key)
        if s is None:
            s = self.stack.enter_context(self.nc.semaphore("s_" + "_".join(str(k) for k in key)))
            self.sems[key] = s
        return s

    def barrier(self):
        lasts = [self.ops[e][-1] for e in ENGS if self.ops[e]]
        for e in ENGS:
            lasts += [o for o in self.ops[e][-4 * DMA_K:] if o.dma][-DMA_K:]
        for e in ENGS:
            self.bar[e] = list(lasts)

    def op(self, eng, fn, reads=(), writes=(), dma=False):
        o = Op(eng, fn, dma)
        o.idx = len(self.ops[eng])
        deps = {}
        psr_ = [k for k in reads if isinstance(k, tuple) and k[0] == 'ps']
        if psr_:
            reads = [k for k in reads if not (isinstance(k, tuple) and k[0] == 'ps')]
            writes = list(writes) + [k for k in psr_ if k not in writes]
        if self.bar[eng]:
            for d in self.bar[eng]:
                deps[id(d)] = d
            self.bar[eng] = []
        for k in reads:
            st = self.buf.get(k)
            if st is None:
                st = self.buf[k] = [None, []]
            if st[0] is not None:
                deps[id(st[0])] = st[0]
            st[1].append(o)
        for k in writes:
            st = self.buf.get(k)
            if st is None:
                st = self.buf[k] = [None, []]
            if st[0] is not None:
                deps[id(st[0])] = st[0]
            for r in st[1]:
                if r is not o:
                    deps[id(r)] = r
            st[0] = o
            st[1] = []
        o.deps = [d for d in deps.values() if d.dma or d.eng != eng or eng in SAME_ENGINE_SYNC]
        self.ops[eng].append(o)
        return o

    def emit(self):
        nc = self.nc
        for e in ENGS:
            for o in self.ops[e]:
                for d in o.deps:
                    if not d.dma:
                        d.signal = True
        for e in ENGS:
            c = 0
            j = 0
            for o in self.ops[e]:
                if o.dma:
                    o.dj = j
                    j += 1
                elif o.signal:
                    o.sig = c
                    c += 1
        last = {}
        for e in ENGS:
            for o in self.ops[e]:
                if o.dma:
                    slot = o.dj % DMA_K
                    n = o.dj // DMA_K
                    key = ('d', e, slot, n // DMA_USES)
                    self.sem(key)
                    last[key] = 16 * (n % DMA_USES + 1)
                elif o.signal:
                    self.sem(('e', e, o.sig // SEG))
        sched = self
        self.n_instr = {e: 0 for e in ENGS}

        def run_engine(e, h):
            waited = {}

            def wait(key, val):
                if waited.get(key, 0) >= val:
                    return
                waited[key] = val
                h.wait_ge(sched.sems[key], val)
                sched.n_instr[e] += 1

            for o in sched.ops[e]:
                for d in o.deps:
                    if d.dma:
                        slot = d.dj % DMA_K
                        n = d.dj // DMA_K
                        wait(('d', d.eng, slot, n // DMA_USES), 16 * (n % DMA_USES + 1))
                    else:
                        wait(('e', d.eng, d.sig // SEG), d.sig % SEG + 1)
                if o.dma:
                    slot = o.dj % DMA_K
                    n = o.dj // DMA_K
                    if n > 0:
                        pn = n - 1
                        wait(('d', e, slot, pn // DMA_USES), 16 * (pn % DMA_USES + 1))
                ins = o.fn(h)
                sched.n_instr[e] += 1
                if o.dma:
                    slot = o.dj % DMA_K
                    n = o.dj // DMA_K
                    ins.then_inc(sched.sems[('d', e, slot, n // DMA_USES)], 16)
                elif o.signal:
                    ins.then_inc(sched.sems[('e', e, o.sig // SEG)], 1)
            if e == 'sp':
                for key, val in last.items():
                    wait(key, val)

        with nc.Block() as block:
            @block.tensor
            def _(h):
                run_engine('pe', h)

            @block.scalar
            def _(h):
                run_engine('act', h)

            @block.vector
            def _(h):
                run_engine('dve', h)

            @block.gpsimd
            def _(h):
                run_engine('pool', h)

            @block.sync
            def _(h):
                run_engine('sp', h)


IN_OFF = {}
_o = 0
for _n, _w in [('a_r', 256), ('a_k', 256), ('a_v', 256), ('a_wlo', 128), ('a_alo', 128), ('a_glo', 128),
               ('m_cq', 256), ('m_ckv', 128), ('m_kpe', 32), ('g_q', 256), ('g_k', 128), ('g_v', 128),
               ('n_q', 256), ('n_k', 256), ('n_v', 256), ('gate', 4096)]:
    IN_OFF[_n] = _o
    _o += _w

VEC = {}
NVEC = 0
for _n, _w in [('b_mod', 48), ('g_mpre', 8), ('g_mpost', 8), ('g_fpre', 8), ('g_fpost', 8), ('a0', 4), ('k_k', 2),
               ('k_a', 2), ('r_k', 2), ('qn', 2), ('kvn', 1), ('gq', 1), ('gq_sw', 1), ('gk', 1), ('gk_sw', 1), ('ln_g', 2), ('ln_b', 2)]:
    VEC[_n] = NVEC
    NVEC += _w

CST = {}
NCST = 0
for _n, _w in [('ident', 128), ('ones', 128), ('bones', 128), ('eps_rms', 1), ('eps_kk', 1), ('eps_gn', 1), ('one', 1)]:
    CST[_n] = NCST
    NCST += _w
CSR = {}
NCSR = 0
for _n, _w in [('TI0', 128), ('TS0', 128), ('TI1', 128), ('TS1', 128),
               ('MS0', 128), ('MI0', 128), ('MS1', 128), ('MI1', 128), ('cm0', 128), ('cm1', 128)]:
    CSR[_n] = NCSR
    NCSR += _w


def _fm(v):
    return np.ascontiguousarray(v.reshape(L, -1, 128).transpose(0, 2, 1))


def _swap_halves(a, width):
    sh = a.shape
    b = a.reshape(sh[:-1] + (sh[-1] // width, 2, width // 2))
    return np.ascontiguousarray(b[..., ::-1, :].reshape(sh))


def _rope_tables(T, rot_dim):
    t = np.arange(T)
    pos = np.stack([t // 64, t % 64], axis=-1).astype(np.float32)
    n_freq = rot_dim // 4
    inv = (10000.0 ** (-np.arange(n_freq, dtype=np.float32) / n_freq)).astype(np.float32)
    ang = (pos[:, :, None] * inv).reshape(T, rot_dim // 2)
    c = np.cos(ang).astype(np.float32).T
    s = np.sin(ang).astype(np.float32).T
    C = np.concatenate([c, c], axis=0)
    S_ = np.concatenate([-s, s], axis=0)
    return C, S_


def _consts():
    c = np.zeros((128, NCST), np.float32)
    r = np.zeros((128, NCSR), np.float32)
    i = np.arange(128)
    c[:, CST['ident']:CST['ident'] + 128] = np.eye(128)
    c[:, CST['ones']:CST['ones'] + 128] = 1.0
    c[:, CST['bones']:CST['bones'] + 128] = (i[:, None] // 64 == i[None, :] // 64)
    same = (i[:, None] // 64 == i[None, :] // 64)
    su = same & (i[:, None] < i[None, :])
    iu = same & (i[:, None] <= i[None, :])
    sl = same & (i[:, None] > i[None, :])
    il = same & (i[:, None] >= i[None, :])
    ch = -math.exp(-0.5)
    for name, val in (('MS0', su), ('MI0', iu), ('MS1', sl), ('MI1', il), ('TI0', ch * iu), ('TS0', ch * su),
                      ('TI1', ch * il), ('TS1', ch * sl), ('cm0', np.broadcast_to(i[None, :] < 64, (128, 128))),
                      ('cm1', np.broadcast_to(i[None, :] >= 64, (128, 128)))):
        r[:, CSR[name]:CSR[name] + 128] = val
    c[:, CST['eps_rms']] = EPS
    c[:, CST['eps_kk']] = 1e-12
    c[:, CST['eps_gn']] = 64e-5
    c[:, CST['one']] = 1.0
    return c, r


def _na_bias_table(rel_bias):
    p = np.arange(128)
    st = np.arange(4)
    dr = np.arange(8)
    cq = np.arange(64)
    off = st[None, :] * 128 + p[:, None]
    kr = (off // 64)[:, None, :, None]
    kc = (off % 64)[:, None, :, None]
    ri = np.broadcast_to(kr - dr[None, :, None, None] + 7, (128, 8, 4, 64))
    c0 = np.clip(cq - 8, 0, 48)
    valid = np.broadcast_to((kc >= c0) & (kc < c0 + 16), (128, 8, 4, 64))
    ci = np.broadcast_to(np.clip(kc - cq + 15, 0, 30), (128, 8, 4, 64))
    g = rel_bias[:, :, ri, ci]
    g = np.where(valid[None, None], g, np.float32(-30000.0)).astype(np.float32)
    return np.ascontiguousarray(g.transpose(0, 1, 2, 3, 4, 5))


class Path:
    def __init__(self, name, T, nseq, lat, m):
        self.name = name
        self.T = T
        self.NT = T // 512
        self.NB = T // 128
        self.nseq = nseq
        self.Ts = T // nseq
        self.lat = lat
        self.m = m
        self.SC = 512 if lat else 0


class Prog:
    def __init__(self, nc, cfg):
        self.nc = nc
        self.cfg = cfg
        self.root = ExitStack()
        self.S = Sched(nc, self.root)
        self.scopes = [self.root]
        self.q = 0
        self.uid = 0
        self.io = {}

    def din(self, name, shape):
        t = self.nc.dram_tensor(name, list(shape), F32, kind="ExternalInput").ap()
        self.io[name] = t
        return t

    def dout(self, name, shape):
        t = self.nc.dram_tensor(name, list(shape), F32, kind="ExternalOutput").ap()
        self.io[name] = t
        return t

    def T(self, name, shape, dt):
        self.uid += 1
        return self.scopes[-1].enter_context(self.nc.sbuf_tensor(f"{name}_{self.uid}", list(shape), dt))

    def push(self):
        st = ExitStack()
        self.scopes.append(st)
        return st

    def pop(self):
        self.S.barrier()
        st = self.scopes.pop()
        st.close()
        self.wp = list(self.wp_root)

    def more_wp(self, n):
        self.wp = list(self.wp_root) + [self.T(f"wpx{i}", [128, 8, 128], BF16) for i in range(n)]

    def mm(self, out, lhsT, rhs, start, stop, r, w):
        self.S.op('pe', lambda h: h.matmul(out, lhsT, rhs, start=start, stop=stop), r, w)

    def tr(self, out, in_, ident, r, w):
        self.S.op('pe', lambda h: h.transpose(out, in_, ident), r, w)

    def act(self, out, in_, func, r, w, bias=None, scale=None, accum=None):
        kw = {}
        if bias is not None:
            kw['bias'] = bias
        if scale is not None:
            kw['scale'] = scale
        if accum is not None:
            kw['accum_out'] = accum
        self.S.op('act', lambda h: h.activation(out, in_, func, **kw), r, w)

    def tt(self, out, a, b, op, r, w):
        self.S.op('dve', lambda h: h.tensor_tensor(out, a, b, op=op), r, w)

    def ts(self, out, a, s1, s2, op0, op1, r, w):
        if s2 is None:
            self.S.op('dve', lambda h: h.tensor_scalar(out, a, s1, None, op0=op0), r, w)
        else:
            self.S.op('dve', lambda h: h.tensor_scalar(out, a, s1, s2, op0=op0, op1=op1), r, w)

    def stt(self, out, in0, scalar, in1, op0, op1, r, w):
        self.S.op('dve', lambda h: h.scalar_tensor_tensor(out, in0, scalar, in1, op0=op0, op1=op1), r, w)

    def vcopy(self, out, in_, r, w):
        self.S.op('dve', lambda h: h.tensor_copy(out, in_), r, w)

    def acopy(self, out, in_, r, w):
        self.S.op('act', lambda h: h.copy(out, in_), r, w)

    def recip(self, out, in_, r, w):
        self.S.op('dve', lambda h: h.reciprocal(out, in_), r, w)

    def memset(self, out, val, w, eng='dve'):
        self.S.op(eng, lambda h: h.memset(out, val), (), w)

    def dma(self, eng, out, in_, r, w):
        self.S.op(eng, lambda h: h.dma_start(out=out, in_=in_), r, w, dma=True)

    def psq(self, ncols=512, parts=128):
        b = self.q % 5
        self.q += 1
        return self.ps[b][0:parts, 0:ncols], [('ps', b)]

    def psr(self, b, ncols=512, parts=128, off=0):
        return self.ps[b][0:parts, off:off + ncols], [('ps', b)]

    def load_w(self, wap, n):
        i = self.wi % len(self.wp)
        self.wi += 1
        wt = self.wp[i]
        key = ('wp', i)
        self.dma('pool', wt[:, :, 0:n], wap.rearrange("(kc p) n -> p kc n", p=128), [], [key])
        return wt, key

    def cc(self, name, n=128, rows=slice(0, 128)):
        o = CST[name]
        return self.cst[rows, o:o + n]

    def vv(self, l, name, i=0, n=1):
        o = VEC[name] + i
        return self.vecs[:, l, o:o + n]

    def build(self):
        nc, S, cfg = self.nc, self.S, self.cfg
        io = self.io
        xc = self.din("xc", [1024, D])
        xl = self.din("xl", [2048, D])
        cvec = self.din("cvec", [128, 8, 2])
        vecs_d = self.din("vecs", [128, L, NVEC])
        cst_d = self.din("cst", [128, NCST])
        self.csr_d = self.din("csr", [128, NCSR])
        w_mod = self.din("w_mod_t", [L, 12, 128, 8 * 512])
        w_in = self.din("w_in", [L, D, N_IN])
        w_in_sw = self.din("w_in_sw", [L, D, 416])
        w_fg = self.din("ffn_gu_t", [L, NF // 2, 128, 8 * 512])
        w_fu = None
        w_fd = self.din("ffn_d_t", [L, 4, 128, NF * 256])
        self.gate_t = self.din("gate_t", [L, 8, 128, 4 * 8 * 128])
        self.wbr_t = self.din("wbr_t", [L, 8, 128, 8 * 128])
        self.wout_t = self.din("wout_t", [L, 8, 128, 8 * 128])
        yc = self.dout("yc", [1024, D])
        yl = self.dout("yl", [2048, D])
        self.w_in, self.w_in_sw = w_in, w_in_sw
        self.declare_branch_io()
        if cfg.get('dbg'):
            self.dbg = [self.dout('dbg_c', [128, 2, 1024]), self.dout('dbg_l', [128, 2, 2048])]
            self.dbg2 = [self.dout('dbg2_c', [128, 2, 8, 128]), self.dout('dbg2_l', [128, 2, 16, 128])]
            self.dbg3 = [[self.dout(f'dbg3_{n}{i}', [128, 2, T_]) for i in range(3)] for n, T_ in (('c', 1024), ('l', 2048))]

        self.ps = [self.root.enter_context(nc.psum_tensor(f"psb{i}", [128, 512], F32)) if i != 5 else None for i in range(8)]
        self.psbf = self.root.enter_context(nc.psum_tensor("psbf", [128, 1024], BF16))
        self.cst = self.T("cst", [128, NCST], F32)
        self.identb = self.T("identb", [128, 128], BF16)
        self.vecs = self.T("vecs", [128, L, NVEC], F32)
        self.AB = self.T("AB", [128, L, 2, 6, 8], F32)
        self.xT = self.T("xT", [128, 8, 2048], F32)
        self.hT = self.T("hT", [128, 8, 2048], BF16)
        self.wp = []
        self.wp_root = []
        self.wi = 0
        self.sq = [self.T(f"sq{i}", [128, 512], F32) for i in range(2)]
        self.rs = [self.T(f"rs{i}", [128, 512], F32) for i in range(2)]
        self.tmpf = [self.T(f"tmpf{i}", [128, 512], F32) for i in range(2)]
        self.cnt = 0

        self.dma('sp', self.cst[:], cst_d[:, :], [], ['cst'])
        self.dma('pool', self.identb[:], cst_d[:, CST['ident']:CST['ident'] + 128], [], ['identb'])
        self.dma('sp', self.vecs[:], vecs_d[:, :, :], [], ['vecs'])

        ms = self.push()
        self.modall = self.T("modall", [128, L, 48, 2], F32)
        cv = self.T("cv", [128, 8, 2], F32)
        scb = self.T("scb", [128, 8, 2], BF16)
        wm = [self.T(f"wm{i}", [128, 8, 512], BF16) for i in range(2)]
        self.dma('sp', cv[:], cvec[:, :, :], [], ['cv'])
        self.act(scb[:], cv[:], AF.Silu, ['cv'], ['scb'])
        for l in range(L if cfg.get('do_mod', True) else 0):
            for g in range(12):
                i = (l * 12 + g) % 2
                self.dma('pool', wm[i][:].rearrange("p a b -> p (a b)"), w_mod[l, g], [], [('wm', i)])
                ps, pk = self.psq(512)
                for q4 in range(4):
                    for kc in range(8):
                        self.mm(ps[:, q4 * 2:q4 * 2 + 2], wm[i][:, kc, q4 * 128:(q4 + 1) * 128], scb[:, kc, :],
                                kc == 0, kc == 7, [('wm', i), 'scb'], pk)
                for q4 in range(4):
                    n = g * 4 + q4
                    self.act(self.modall[:, l, n, :], ps[:, q4 * 2:q4 * 2 + 2], AF.Identity, pk + ['vecs'], ['modall'],
                             bias=self.vv(l, 'b_mod', n))
            mv = self.modall[:, l, :, :].rearrange("p (s k) m -> p s k m", s=6)
            for m in range(2 if cfg.get('mod_stage', 2) >= 2 else 0):
                for (si, gi, oi) in ((1, 'g_mpre', 0), (4, 'g_fpre', 3)):
                    self.stt(self.AB[:, l, m, oi, :], mv[:, si, :, m], 1.0, self.vv(l, gi, 0, 8), ALU.add, ALU.mult,
                             ['modall', 'vecs'], ['AB'])
                for (si, oi) in ((0, 1), (3, 4)):
                    self.vcopy(self.AB[:, l, m, oi, :], mv[:, si, :, m], ['modall'], ['AB'])
                for (si, gi, oi) in ((2, 'g_mpost', 2), (5, 'g_fpost', 5)):
                    self.tt(self.AB[:, l, m, oi, :], mv[:, si, :, m], self.vv(l, gi, 0, 8), ALU.mult,
                            ['modall', 'vecs'], ['AB'])
        self.pop()

        paths = []
        if cfg.get('do_ctx', True):
            paths.append((Path('ctx', 1024, 4, False, 0), xc, yc))
        if cfg.get('do_lat', True):
            paths.append((Path('lat', 2048, 1, True, 1), xl, yl))
        for path, xi, yo in paths:
            self.load_x(path, xi)
            for l in range(cfg.get('nlayers', L)):
                if cfg.get('do_mixer', True):
                    self.mixer(l, path)
                if cfg.get('do_ffn', True):
                    self.ffn(l, path, w_fg, w_fu, w_fd)
            self.store_x(path, yo)
        S.emit()

    def load_x(self, path, xi):
        self.push()
        self.xin = [self.T(f"xin{i}", [128, D], F32) for i in range(2)]
        for blk in range(path.NB):
            i = blk % 2
            self.dma('sp', self.xin[i][:], xi[blk * 128:(blk + 1) * 128, :], [], [('xin', i)])
            for half in range(2):
                ps, pk = self.psq(512)
                for c in range(4):
                    kc = half * 4 + c
                    self.tr(ps[:, c * 128:(c + 1) * 128], self.xin[i][:, kc * 128:(kc + 1) * 128], self.cc('ident'),
                            [('xin', i), 'cst'], pk)
                if True:
                    for c in range(4):
                        kc = half * 4 + c
                        f = self.vcopy if c % 2 == 0 else self.acopy
                        f(self.xT[:, kc, blk * 128:(blk + 1) * 128], ps[:, c * 128:(c + 1) * 128], pk, [('xT', blk // 4)])
                    continue
        self.pop()

    def store_x(self, path, yo):
        self.push()
        self.xin = [self.T(f"xin{i}", [128, D], F32) for i in range(2)]
        for blk in range(path.NB):
            i = blk % 2
            for half in range(2):
                ps, pk = self.psq(512)
                for c in range(4):
                    kc = half * 4 + c
                    self.tr(ps[:, c * 128:(c + 1) * 128], self.xT[:, kc, blk * 128:(blk + 1) * 128], self.cc('ident'),
                            [('xT', blk // 4), 'cst'], pk)
                if half == 0:
                    self.vcopy(self.xin[i][:, 0:512], ps, pk, [('xin', i)])
                else:
                    self.acopy(self.xin[i][:, 512:1024], ps, pk, [('xin', i)])
            self.dma('sp', yo[blk * 128:(blk + 1) * 128, :], self.xin[i][:], [('xin', i)], [])
        self.pop()

    def rstd(self, rs, ps, pk, rk, n, eps_name):
        self.act(rs, ps, AF.Sqrt, pk + ['cst'], [rk], bias=self.cc(eps_name, 1), scale=1.0 / n)
        self.recip(rs, rs, [rk], [rk])

    def norm_h(self, l, path, ai):
        for j in range(path.NT):
            sl = slice(j * 512, (j + 1) * 512)
            ps, pk = self.psq(512)
            for kc in range(8):
                i = self.cnt % 2
                self.cnt += 1
                self.act(self.sq[i][:], self.xT[:, kc, sl], AF.Square, [('xT', j)], [('sq', i)])
                self.mm(ps, self.cc('ones'), self.sq[i][:], kc == 0, kc == 7, [('sq', i), 'cst'], pk)
            ri = j % 2
            self.rstd(self.rs[ri][:], ps, pk, ('rs', ri), D, 'eps_rms')
            for kc in range(8):
                i = self.cnt % 2
                self.cnt += 1
                self.tt(self.tmpf[i][:], self.xT[:, kc, sl], self.rs[ri][:], ALU.mult, [('xT', j), ('rs', ri)],
                        [('tmpf', i)])
                self.act(self.hT[:, kc, sl], self.tmpf[i][:], AF.Identity, [('tmpf', i), 'AB'], [('hT', j)],
                         bias=self.AB[:, l, path.m, ai + 1, kc:kc + 1], scale=self.AB[:, l, path.m, ai, kc:kc + 1])

    def evac_oT(self, kc, ps, pk, ps_n, pnk):
        self.acopy(self.oT[:, kc, :], ps, pk, [('oT', kc)])
        i = self.cnt % 2
        self.cnt += 1
        self.act(self.sq[i][:], ps, AF.Square, pk, [('sq', i)])
        self.mm(ps_n, self.cc('ones'), self.sq[i][:], kc == 0, kc == 7, [('sq', i), 'cst'], pnk)

    def post_residual(self, l, path, j, gi, ps_n, pnk):
        sl = slice(j * 512, (j + 1) * 512)
        ri = j % 2
        self.rstd(self.rs[ri][:], ps_n, pnk, ('rs', ri), D, 'eps_rms')
        for kc in range(8):
            i = self.cnt % 2
            self.cnt += 1
            self.tt(self.tmpf[i][:], self.oT[:, kc, :], self.rs[ri][:], ALU.mult, [('oT', kc), ('rs', ri)],
                    [('tmpf', i)])
            self.stt(self.xT[:, kc, sl], self.tmpf[i][:], self.AB[:, l, path.m, gi, kc:kc + 1], self.xT[:, kc, sl],
                     ALU.mult, ALU.add, [('tmpf', i), 'AB', ('xT', j)], [('xT', j)])

    def ffn(self, l, path, w_fg, w_fu, w_fd):
        self.norm_h(l, path, 3)
        self.push()
        G = 2
        actT = self.T("actT", [128, NF, G * 512], BF16)
        self.oT = self.T("oT", [128, 8, 512], F32)
        wgu = [self.T(f"wgu{i}", [128, 8, 512], BF16) for i in range(2)]
        wd = self.T("wd", [128, NF, 256], BF16)
        sg = self.tmpf
        cnt = 0
        for g in range(path.NT // G):
            for fp in range(NF // 2):
                i = fp % 2
                self.dma('pool', wgu[i][:].rearrange("p a b -> p (a b)"), w_fg[l, fp], [], [('wgu', i)])
                for f2 in range(2):
                    f = fp * 2 + f2
                    for t in range(G):
                        j = g * G + t
                        sl = slice(j * 512, (j + 1) * 512)
                        pg, pgk = self.psq(512)
                        for kc in range(8):
                            self.mm(pg, wgu[i][:, kc, f2 * 128:(f2 + 1) * 128], self.hT[:, kc, sl], kc == 0, kc == 7,
                                    [('wgu', i), ('hT', j)], pgk)
                        pu, puk = self.psq(512)
                        for kc in range(8):
                            self.mm(pu, wgu[i][:, kc, 256 + f2 * 128:256 + (f2 + 1) * 128], self.hT[:, kc, sl], kc == 0, kc == 7,
                                    [('wgu', i), ('hT', j)], puk)
                        k = cnt % 2
                        cnt += 1
                        self.act(sg[k][:], pg, AF.Silu, pgk, [('tmpf', k)])
                        self.tt(actT[:, f, t * 512:(t + 1) * 512], sg[k][:], pu, ALU.mult, [('tmpf', k)] + puk, [('actT', f, t)])
            for t in range(G):
                j = g * G + t
                ps_n, pnk = self.psr(6, 512)
                for q in range(4):
                    for part in range(2):
                        f0, f1 = part * 11, (part + 1) * 11
                        self.dma('pool', wd[:, f0:f1, :].rearrange("p a b -> p (a b)"), w_fd[l, q][:, f0 * 256:f1 * 256], [], [('wd', part)])
                    pss = [self.psq(512) for _ in range(2)]
                    for f in range(NF):
                        for oc in range(2):
                            self.mm(pss[oc][0], wd[:, f, oc * 128:(oc + 1) * 128], actT[:, f, t * 512:(t + 1) * 512], f == 0, f == NF - 1,
                                    [('wd', f // 11), ('actT', f, t)], pss[oc][1])
                    for oc in range(2):
                        self.evac_oT(q * 2 + oc, pss[oc][0], pss[oc][1], ps_n, pnk)
                self.post_residual(l, path, j, 5, ps_n, pnk)
        self.pop()

    def declare_branch_io(self):
        d = self.din
        self.st_in = d("st_in", [L, 2, 4, 64, 64])
        self.c_ckv = d("c_ckv", [L, 512, 128])
        self.c_kpe = d("c_kpe", [L, 512, 32])
        self.c_gk = d("c_gk", [L, 512, 128])
        self.c_gv = d("c_gv", [L, 512, 128])
        self.c_nk = d("c_nk", [L, 512, 256])
        self.c_nv = d("c_nv", [L, 512, 256])
        self.rows_d = d("rows", [128, L, 1024])
        self.w2_d = d("rwkv_w2", [L, 128, 256])
        self.a2_d = d("rwkv_a2", [L, 128, 256])
        self.g2_d = d("rwkv_g2", [L, 128, 256])
        self.w_uq = d("mla_w_uq", [L, 256, 384])
        self.w_uq_sw = d("mla_w_uq_sw", [L, 256, 128])
        self.w_ukv = d("mla_w_ukv", [L, 128, 512])
        self.rope_d = d("rope", [128, 4, 2048])
        self.nab_d = d("nab", [L, 4, 128, 8 * 4 * 64])
        o = self.dout
        self.st_out = o("st_out", [4, L, 2, 4, 64, 64])
        self.o_ckv = o("o_ckv", [4, L, 256, 128])
        self.o_kpe = o("o_kpe", [4, L, 256, 32])
        self.o_gk = o("o_gk", [4, L, 256, 128])
        self.o_gv = o("o_gv", [4, L, 256, 128])
        self.o_nk = o("o_nk", [4, L, 256, 256])
        self.o_nv = o("o_nv", [4, L, 256, 256])

    def mixer(self, l, path):
        cfg = self.cfg
        self.norm_h(l, path, 0)
        self.push()
        self.yT = self.T("yT", [128, 8, path.T], BF16)
        br = cfg.get('branches', 'ABCD')
        if 'A' in br:
            self.branch_rwkv(l, path)
        for n, name in enumerate('ABCD'):
            if name not in br:
                for c in range(2):
                    self.memset(self.yT[:, 2 * n + c, :], 0.0, [('yT', 2 * n + c)])
        if 'B' in br:
            self.branch_mla(l, path)
        if 'C' in br:
            self.branch_gqa(l, path)
        if 'D' in br:
            self.branch_na(l, path)
        self.pass2(l, path)
        self.pop()

    def project_group(self, path, wspecs, evac, fin=None):
        assert len(wspecs) <= len(self.wp), (len(wspecs), len(self.wp))
        wts = []
        for tag, pieces in wspecs:
            i = self.wi % len(self.wp)
            self.wi += 1
            wt = self.wp[i]
            key = ('wp', i)
            c0 = 0
            for (wap, n) in pieces:
                self.dma('pool', wt[:, :, c0:c0 + n], wap.rearrange("(kc p) n -> p kc n", p=128), [], [key])
                c0 += n
            wts.append((tag, wt, key, c0))
        for j in range(path.NT):
            sl = slice(j * 512, (j + 1) * 512)
            for tag, wt, key, n in wts:
                ps, pk = self.psq(512, parts=n)
                for kc in range(8):
                    self.mm(ps, wt[:, kc, 0:n], self.hT[:, kc, sl], kc == 0, kc == 7, [key, ('hT', j)], pk)
                evac(tag, j, ps, pk)
            if fin is not None:
                fin(j)

    def win(self, l, name, c0, n):
        o = IN_OFF[name] + c0
        return (self.w_in[l][:, o:o + n], n)

    def attn_head(self, path, pieces, vaug, vkey, ydst, ykey, scale):
        N = min(512, path.Ts)
        LOOK = 2
        steps = []
        for seq in range(path.nseq):
            sts = list(range(path.NB + 4)) if path.lat else [seq * 2, seq * 2 + 1]
            for qt in range(path.Ts // N):
                q0 = seq * path.Ts + qt * N
                for si, st in enumerate(sts):
                    steps.append((q0, st, si == 0, si == len(sts) - 1))
        inflight = {}

        def emit_qk(k):
            q0, st, _, _ = steps[k]
            ps, pk = self.psq(N)
            for pi, (qf, kf, keys) in enumerate(pieces):
                self.mm(ps, kf(st), qf(q0, N), pi == 0, pi == len(pieces) - 1, keys, pk)
            inflight[k] = (ps, pk)

        for k in range(min(LOOK, len(steps))):
            emit_qk(k)
        po = pok = None
        for k, (q0, st, first, last) in enumerate(steps):
            if k + LOOK < len(steps):
                emit_qk(k + LOOK)
            if first:
                po, pok = self.psr(6 + self.ocnt % 2, N, parts=65)
                self.ocnt += 1
            ps, pk = inflight.pop(k)
            i = self.pcnt % 3
            self.pcnt += 1
            self.act(self.pT[i][:, 0:N], ps, AF.Exp, pk, [('pT', i)], scale=scale)
            self.mm(po, vaug(st), self.pT[i][:, 0:N], first, last, [('pT', i), vkey], pok)
            if last:
                self.attn_norm(po, pok, N, ydst(q0, N), ykey)

    def attn_norm(self, po, pok, N, ydst, ykey):
        osb = self.osb
        self.acopy(osb[0:65, 0:N], po, pok, ['osb'])
        pb, pbk = self.psq(N, parts=64)
        self.mm(pb, self.cst[64:65, CST['ones']:CST['ones'] + 64], osb[64:65, 0:N], True, True, ['osb', 'cst'], pbk)
        ap, shift = ydst
        self.recip(self.rcp[0:64, 0:N], pb, pbk, ['rcp'])
        if not shift:
            self.tt(ap, osb[0:64, 0:N], self.rcp[0:64, 0:N], ALU.mult, ['osb', 'rcp'], [ykey])
        else:
            self.tt(self.ytmp[0:64, 0:N], osb[0:64, 0:N], self.rcp[0:64, 0:N], ALU.mult, ['osb', 'rcp'], ['ytmp'])
            self.acopy(ap, self.ytmp[0:64, 0:N], ['ytmp'], [ykey])

    def attn_tiles(self):
        self.pT = [self.T(f"pT{i}", [128, 512], BF16) for i in range(3)]
        self.rcp = self.T("rcp", [64, 512], F32)
        self.osb = self.T("osb", [65, 512], F32)
        self.rec = self.osb
        self.ytmp = self.T("ytmp", [64, 512], BF16)
        self.pcnt = 0
        self.ocnt = 0

    def out_tok(self, src_fn, nparts, ncols, path, l, dst, srckey):
        for blk in range(path.NB):
            ps, pk = self.psq(nparts, parts=128)
            self.tr(ps, src_fn(blk), self.cst[0:nparts, CST['ident']:CST['ident'] + nparts], [srckey, 'cst'], pk)
            i = self.stc % 2
            self.stc += 1
            self.acopy(self.stage[i][:, 0:nparts], ps, pk, [('stage', i)])
            seq, t0 = blk // 2, (blk % 2) * 128
            self.dma('sp', dst(seq, t0), self.stage[i][:, 0:nparts], [('stage', i)], [])

    def pass2(self, l, path):
        self.push()
        G = 2
        mT = self.T("mT", [128, 8, G * 512], BF16)
        self.oT = self.T("oT", [128, 8, 512], F32)
        wg = [self.T(f"wgt{i}", [128, 4, 8, 128], BF16) for i in range(2)]
        wb = [self.T(f"wbt{i}", [128, 8, 128], BF16) for i in range(2)]
        wo = [self.T(f"wot{i}", [128, 8, 128], BF16) for i in range(3)]
        acc, sgm, tmp = self.tmpf, self.sq, self.rs
        cnt = 0
        woc = 0
        for g in range(path.NT // G):
            for e in range(8):
                i = e % 2
                self.dma('pool', wg[i][:].rearrange("p a b c -> p (a b c)"), self.gate_t[l, e], [], [('wgt', i)])
                self.dma('pool', wb[i][:].rearrange("p a b -> p (a b)"), self.wbr_t[l, e], [], [('wbt', i)])
                for t in range(G):
                    j = g * G + t
                    sl = slice(j * 512, (j + 1) * 512)
                    ai = t % 2
                    for n in range(4):
                        pa, pak = self.psq(512)
                        for kc in range(8):
                            self.mm(pa, wg[i][:, n, kc, :], self.hT[:, kc, sl], kc == 0, kc == 7, [('wgt', i), ('hT', j)], pak)
                        si = cnt % 2
                        cnt += 1
                        self.act(sgm[si][:], pa, AF.Sigmoid, pak, [('sq', si)])
                        pb, pbk = self.psq(512)
                        for k2 in range(2):
                            self.mm(pb, wb[i][:, 2 * n + k2, :], self.yT[:, 2 * n + k2, sl], k2 == 0, k2 == 1,
                                    [('wbt', i), ('yT', 2 * n + k2)], pbk)
                        if n == 0:
                            self.tt(acc[ai][:], sgm[si][:], pb, ALU.mult, [('sq', si)] + pbk, [('tmpf', ai)])
                        else:
                            self.tt(tmp[si][:], sgm[si][:], pb, ALU.mult, [('sq', si)] + pbk, [('rs', si)])
                            if n < 3:
                                self.tt(acc[ai][:], acc[ai][:], tmp[si][:], ALU.add, [('tmpf', ai), ('rs', si)], [('tmpf', ai)])
                            else:
                                self.tt(mT[:, e, t * 512:(t + 1) * 512], acc[ai][:], tmp[si][:], ALU.add, [('tmpf', ai), ('rs', si)],
                                        [('mT', t)])
            for t in range(G):
                j = g * G + t
                ps_n, pnk = self.psr(6, 512)
                for oc in range(8):
                    wi_ = woc % 3
                    woc += 1
                    self.dma('pool', wo[wi_][:].rearrange("p a b -> p (a b)"), self.wout_t[l, oc], [], [('wot', wi_)])
                    ps, pk = self.psq(512)
                    for kc in range(8):
                        self.mm(ps, wo[wi_][:, kc, :], mT[:, kc, t * 512:(t + 1) * 512], kc == 0, kc == 7, [('wot', wi_), ('mT', t)], pk)
                    self.evac_oT(oc, ps, pk, ps_n, pnk)
                self.post_residual(l, path, j, 2, ps_n, pnk)
        self.pop()

    def branch_mla(self, l, path):
        T_, SC, lat = path.T, path.SC, path.lat
        TK = T_ + SC
        NS = TK // 128
        self.push()
        self.more_wp(5)
        self.attn_tiles()
        cqn = self.T("cqn", [128, 2, T_], BF16)
        ckb = self.T("ckb", [128, TK], BF16)
        Qh = self.T("Qh", [96, T_], BF16)
        Kh = self.T("Kh", [96, TK], BF16)
        va = self.T("va", [128, NS, 65], BF16)
        wuq = self.T("wuq", [128, 2, 384], BF16)
        wukv = self.T("wukv", [128, 512], BF16)
        cqf = [self.T(f"cqf{i}", [128, 512], F32) for i in range(2)]
        ckf = self.T("ckf", [128, 512], F32)
        ktb = self.T("ktb", [32, 512], BF16)
        self.dma('pool', wuq[:], self.w_uq[l].rearrange("(kc p) n -> p kc n", p=128), [], ['wuq'])
        self.dma('pool', wukv[:], self.w_ukv[l], [], ['wukv'])
        if lat:
            wuqs = self.T("wuqs", [128, 2, 128], BF16)
            self.dma('pool', wuqs[:], self.w_uq_sw[l].rearrange("(kc p) n -> p kc n", p=128), [], ['wuqs'])
            rC = self.T("rC", [32, 2048], BF16)
            rS = self.T("rS", [32, 2048], BF16)
            self.dma('pool', rC[:], self.rope_d[0:32, 2, :], [], ['rC'])
            self.dma('pool', rS[:], self.rope_d[0:32, 3, :], [], ['rS'])
        else:
            ckn_f = self.T("ckn_f", [128, T_], F32)
            kpe_f = self.T("kpe_f", [32, T_], F32)
            self.stage = [self.T(f"stage{i}", [128, 128], F32) for i in range(2)]
            self.stc = 0
        self.memset(va[:, :, 64:65], 1.0, ['va'])
        st8 = {}

        def evac(tag, j, ps, pk):
            sl = slice(j * 512, (j + 1) * 512)
            if tag in ('cq0', 'cq1'):
                c = int(tag[2])
                self.acopy(cqf[c][:], ps, pk, [('cqf', c)])
                i = self.cnt % 2
                self.cnt += 1
                self.act(self.sq[i][:], ps, AF.Square, pk, [('sq', i)])
                if c == 0:
                    st8['pn'] = self.psr(7, 512)
                self.mm(st8['pn'][0], self.cc('ones'), self.sq[i][:], c == 0, c == 1, [('sq', i), 'cst'], st8['pn'][1])
                if c == 1:
                    ri = j % 2
                    self.rstd(self.rs[ri][:], st8['pn'][0], st8['pn'][1], ('rs', ri), 256, 'eps_rms')
                    for c2 in range(2):
                        self.stt(cqn[:, c2, sl], cqf[c2][:], self.vv(l, 'qn', c2), self.rs[ri][:], ALU.mult, ALU.mult,
                                 [('cqf', c2), ('rs', ri), 'vecs'], [('cqn', j)])
            elif tag == 'ckv':
                self.acopy(ckf[:], ps, pk, ['ckf'])
                i = self.cnt % 2
                self.cnt += 1
                self.act(self.sq[i][:], ps, AF.Square, pk, [('sq', i)])
                pn, pnk = self.psq(512)
                self.mm(pn, self.cc('ones'), self.sq[i][:], True, True, [('sq', i), 'cst'], pnk)
                i2 = self.cnt % 2
                self.cnt += 1
                self.rstd(self.tmpf[i2][:], pn, pnk, ('tmpf', i2), 128, 'eps_rms')
                if lat:
                    self.stt(ckb[:, sl], ckf[:], self.vv(l, 'kvn'), self.tmpf[i2][:], ALU.mult, ALU.mult,
                             ['ckf', ('tmpf', i2), 'vecs'], [('ckb', j)])
                else:
                    self.stt(ckn_f[:, sl], ckf[:], self.vv(l, 'kvn'), self.tmpf[i2][:], ALU.mult, ALU.mult,
                             ['ckf', ('tmpf', i2), 'vecs'], ['ckn_f'])
                    self.acopy(ckb[:, sl], ckn_f[:, sl], ['ckn_f'], [('ckb', j)])
            elif tag == 'kpe':
                if lat:
                    self.tt(cqf[0][0:32, :], ps, rC[:, sl], ALU.mult, pk + ['rC'], [('cqf', 0)])
                else:
                    self.acopy(kpe_f[:, sl], ps, pk, ['kpe_f'])
                    self.acopy(Kh[64:96, sl], ps, pk, [('Khr', j)])
            elif tag == 'kpes':
                self.tt(cqf[1][0:32, :], ps, rS[:, sl], ALU.mult, pk + ['rS'], [('cqf', 1)])
                self.tt(ktb[:], cqf[0][0:32, :], cqf[1][0:32, :], ALU.add, [('cqf', 0), ('cqf', 1)], ['ktb'])
                self.acopy(Kh[64:96, sl], ktb[:], ['ktb'], [('Khr', j)])

        specs = [('cq0', [self.win(l, 'm_cq', 0, 128)]), ('cq1', [self.win(l, 'm_cq', 128, 128)]),
                 ('ckv', [self.win(l, 'm_ckv', 0, 128)]), ('kpe', [self.win(l, 'm_kpe', 0, 32)])]
        if lat:
            specs.append(('kpes', [(self.w_in_sw[l][:, 384:416], 32)]))
        self.project_group(path, specs, evac)

        if lat:
            cst_ = [self.T(f"cst_{i}", [128, 160], F32) for i in range(2)]
            for st in range(4):
                i = st % 2
                self.dma('sp', cst_[i][:, 0:128], self.c_ckv[l][st * 128:(st + 1) * 128, :], [], [('cst_', i)])
                self.dma('sp', cst_[i][:, 128:160], self.c_kpe[l][st * 128:(st + 1) * 128, :], [], [('cst_', i)])
                ps, pk = self.psq(128)
                self.tr(ps, cst_[i][:, 0:128], self.cc('ident'), [('cst_', i), 'cst'], pk)
                self.acopy(ckb[:, T_ + st * 128:T_ + (st + 1) * 128], ps, pk, [('ckb', 'c')])
                ps, pk = self.psq(128, parts=32)
                self.tr(ps, cst_[i][:, 128:160], self.cc('ident'), [('cst_', i), 'cst'], pk)
                self.acopy(Kh[64:96, T_ + st * 128:T_ + (st + 1) * 128], ps, pk, [('Khr', 'c')])
        else:
            self.out_tok(lambda blk: ckn_f[:, blk * 128:(blk + 1) * 128], 128, 128, path, l,
                         lambda seq, t0: self.o_ckv[seq, l, t0:t0 + 128, :], 'ckn_f')
            self.out_tok(lambda blk: kpe_f[:, blk * 128:(blk + 1) * 128], 32, 32, path, l,
                         lambda seq, t0: self.o_kpe[seq, l, t0:t0 + 128, :], 'kpe_f')

        ckb_keys = [('ckb', j) for j in range(path.NT)] + ([('ckb', 'c')] if lat else [])
        khr_keys = [('Khr', j) for j in range(path.NT)] + ([('Khr', 'c')] if lat else [])
        sc = 96 ** -0.5
        for h in range(4):
            for j in range(path.NT):
                sl = slice(j * 512, (j + 1) * 512)
                ps, pk = self.psq(512, parts=64)
                for c2 in range(2):
                    self.mm(ps, wuq[:, c2, h * 96:h * 96 + 64], cqn[:, c2, sl], c2 == 0, c2 == 1, ['wuq', ('cqn', j)], pk)
                self.acopy(Qh[0:64, sl], ps, pk, ['Qh'])
                ps, pk = self.psq(512, parts=32)
                for c2 in range(2):
                    self.mm(ps, wuq[:, c2, h * 96 + 64:h * 96 + 96], cqn[:, c2, sl], c2 == 0, c2 == 1, ['wuq', ('cqn', j)], pk)
                if lat:
                    self.tt(cqf[0][0:32, :], ps, rC[:, sl], ALU.mult, pk + ['rC'], [('cqf', 0)])
                    ps, pk = self.psq(512, parts=32)
                    for c2 in range(2):
                        self.mm(ps, wuqs[:, c2, h * 32:(h + 1) * 32], cqn[:, c2, sl], c2 == 0, c2 == 1, ['wuqs', ('cqn', j)], pk)
                    self.tt(cqf[1][0:32, :], ps, rS[:, sl], ALU.mult, pk + ['rS'], [('cqf', 1)])
                    self.tt(ktb[:], cqf[0][0:32, :], cqf[1][0:32, :], ALU.add, [('cqf', 0), ('cqf', 1)], ['ktb'])
                    self.acopy(Qh[64:96, sl], ktb[:], ['ktb'], ['Qh'])
                else:
                    self.acopy(Qh[64:96, sl], ps, pk, ['Qh'])
            for jt in range(TK // 512):
                sl = slice(jt * 512, (jt + 1) * 512)
                ps, pk = self.psq(512, parts=64)
                self.mm(ps, wukv[:, h * 128:h * 128 + 64], ckb[:, sl], True, True, ['wukv'] + ckb_keys, pk)
                self.acopy(Kh[0:64, sl], ps, pk, ['Kh'])
            for s8 in range(0, NS, 8):
                ns = min(8, NS - s8)
                ps, pk = self.psq(512)
                for s in range(ns):
                    st = s8 + s
                    self.mm(ps[:, s * 64:(s + 1) * 64], ckb[:, st * 128:(st + 1) * 128], wukv[:, h * 128 + 64:h * 128 + 128],
                            True, True, ['wukv'] + ckb_keys, pk)
                self.acopy(va[:, s8:s8 + ns, 0:64], ps[:, 0:ns * 64].rearrange("p (a b) -> p a b", b=64), pk, ['va'])
            pieces = [(lambda q0, n: Qh[0:96, q0:q0 + n], lambda st: Kh[0:96, st * 128:(st + 1) * 128],
                       ['Qh', 'Kh'] + khr_keys)]
            rows = slice((h % 2) * 64, (h % 2) * 64 + 64)
            self.attn_head(path, pieces, lambda st: va[:, st, :], 'va',
                           lambda q0, n, h=h, rows=rows: (self.yT[rows, 2 + h // 2, q0:q0 + n], h % 2 == 1),
                           ('yT', 2 + h // 2), sc)
        self.pop()

    def branch_gqa(self, l, path):
        T_, SC, lat = path.T, path.SC, path.lat
        TK = T_ + SC
        NS = TK // 128
        self.push()
        self.more_wp(3)
        self.attn_tiles()
        qT = self.T("gq", [128, 2, T_], BF16)
        kT = self.T("gk", [128, TK], BF16)
        va = self.T("gva", [128, NS, 2, 65], BF16)
        f32 = [self.T(f"gf{i}", [128, 512], F32) for i in range(4)]
        self.stage = [self.T(f"stage{i}", [128, 128], F32) for i in range(2)]
        self.stc = 0
        if lat:
            rC = self.T("rC", [128, 2048], BF16)
            rS = self.T("rS", [128, 2048], BF16)
            self.dma('pool', rC[:], self.rope_d[:, 0, :], [], ['rC'])
            self.dma('pool', rS[:], self.rope_d[:, 1, :], [], ['rS'])
        else:
            kn_f = self.T("kn_f", [128, T_], F32)
        self.memset(va[:, :, :, 64:65], 1.0, ['va'])
        oq = IN_OFF['g_q']

        def normed(tag, j, ps, pk, gname, gsw, dst, dkey, fdst=None):
            sl = slice(j * 512, (j + 1) * 512)
            self.acopy(f32[0][:], ps, pk, [('gf', 0)])
            i = self.cnt % 2
            self.cnt += 1
            self.act(self.sq[i][:], ps, AF.Square, pk, [('sq', i)])
            pn, pnk = self.psq(512)
            self.mm(pn, self.cc('bones'), self.sq[i][:], True, True, [('sq', i), 'cst'], pnk)
            ri = self.cnt % 2
            self.cnt += 1
            self.rstd(self.rs[ri][:], pn, pnk, ('rs', ri), 64, 'eps_rms')
            self.cur_rs = ri
            if not lat:
                if fdst is not None:
                    self.stt(fdst[:, sl], f32[0][:], self.vv(l, gname), self.rs[ri][:], ALU.mult, ALU.mult,
                             [('gf', 0), ('rs', ri), 'vecs'], ['kn_f'])
                    self.acopy(dst[:, sl], fdst[:, sl], ['kn_f'], [dkey])
                else:
                    self.stt(dst[:, sl], f32[0][:], self.vv(l, gname), self.rs[ri][:], ALU.mult, ALU.mult,
                             [('gf', 0), ('rs', ri), 'vecs'], [dkey])
            else:
                self.stt(f32[1][:], f32[0][:], self.vv(l, gname), self.rs[ri][:], ALU.mult, ALU.mult,
                         [('gf', 0), ('rs', ri), 'vecs'], [('gf', 1)])
                self.tt(f32[1][:], f32[1][:], rC[:, sl], ALU.mult, [('gf', 1), 'rC'], [('gf', 1)])

        def swapped(j, ps, pk, gsw, dst, dkey):
            sl = slice(j * 512, (j + 1) * 512)
            ri = self.cur_rs
            self.stt(f32[2][:], ps, self.vv(l, gsw), self.rs[ri][:], ALU.mult, ALU.mult, pk + [('rs', ri), 'vecs'], [('gf', 2)])
            self.tt(f32[2][:], f32[2][:], rS[:, sl], ALU.mult, [('gf', 2), 'rS'], [('gf', 2)])
            self.tt(dst[:, sl], f32[1][:], f32[2][:], ALU.add, [('gf', 1), ('gf', 2)], [dkey])

        def evac(tag, j, ps, pk):
            sl = slice(j * 512, (j + 1) * 512)
            if tag in ('q0', 'q1'):
                X = int(tag[1])
                normed(tag, j, ps, pk, 'gq', 'gq_sw', qT[:, X, :], ('gq', X))
            elif tag in ('q0s', 'q1s'):
                X = int(tag[1])
                swapped(j, ps, pk, 'gq_sw', qT[:, X, :], ('gq', X))
            elif tag == 'k':
                normed(tag, j, ps, pk, 'gk', 'gk_sw', kT, 'gk', None if lat else kn_f)
            elif tag == 'ks':
                swapped(j, ps, pk, 'gk_sw', kT, 'gk')
            elif tag == 'v':
                self.acopy(f32[3][:], ps, pk, [('gf', 3)])
                for b4 in range(4):
                    blk = j * 4 + b4
                    pt, ptk = self.psq(128)
                    self.tr(pt, f32[3][:, b4 * 128:(b4 + 1) * 128], self.cc('ident'), [('gf', 3), 'cst'], ptk)
                    self.acopy(va[:, blk, :, 0:64], pt.rearrange("p (a b) -> p a b", b=64), ptk, ['va'])
                    if not lat:
                        i = self.stc % 2
                        self.stc += 1
                        self.vcopy(self.stage[i][:], pt, ptk, [('stage', i)])
                        seq, t0 = blk // 2, (blk % 2) * 128
                        self.dma('sp', self.o_gv[seq, l, t0:t0 + 128, :], self.stage[i][:], [('stage', i)], [])

        def qspec(X, src, off):
            return [(src[l][:, off + X * 64: off + X * 64 + 64], 64), (src[l][:, off + (X + 2) * 64: off + (X + 2) * 64 + 64], 64)]

        for X in range(2):
            specs = [(f'q{X}', qspec(X, self.w_in, oq))]
            if lat:
                specs.append((f'q{X}s', qspec(X, self.w_in_sw, 0)))
            self.project_group(path, specs, evac)
        specs = [('k', [self.win(l, 'g_k', 0, 128)])]
        if lat:
            specs.append(('ks', [(self.w_in_sw[l][:, 256:384], 128)]))
        specs.append(('v', [self.win(l, 'g_v', 0, 128)]))
        self.project_group(path, specs, evac)

        if lat:
            cst_ = [self.T(f"cst_{i}", [128, 128], F32) for i in range(2)]
            for st in range(4):
                i = st % 2
                self.dma('sp', cst_[i][:], self.c_gk[l][st * 128:(st + 1) * 128, :], [], [('cst_', i)])
                ps, pk = self.psq(128)
                self.tr(ps, cst_[i][:], self.cc('ident'), [('cst_', i), 'cst'], pk)
                self.acopy(kT[:, T_ + st * 128:T_ + (st + 1) * 128], ps, pk, ['gk'])
                self.dma('pool', va[:, T_ // 128 + st, :, 0:64],
                         self.c_gv[l][st * 128:(st + 1) * 128, :].rearrange("p (a b) -> p a b", b=64), [], ['va'])
        else:
            self.out_tok(lambda blk: kn_f[:, blk * 128:(blk + 1) * 128], 128, 128, path, l,
                         lambda seq, t0: self.o_gk[seq, l, t0:t0 + 128, :], 'kn_f')

        for h in range(4):
            X, kvh = h % 2, h // 2
            rows = slice(kvh * 64, kvh * 64 + 64)
            pieces = [(lambda q0, n, X=X, rows=rows: qT[rows, X, q0:q0 + n],
                       lambda st, rows=rows: kT[rows, st * 128:(st + 1) * 128], [('gq', X), 'gk'])]
            orow = slice((h % 2) * 64, (h % 2) * 64 + 64)
            self.attn_head(path, pieces, lambda st, kvh=kvh: va[:, st, kvh, :], 'va',
                           lambda q0, n, h=h, orow=orow: (self.yT[orow, 4 + h // 2, q0:q0 + n], h % 2 == 1),
                           ('yT', 4 + h // 2), 0.125)
        self.pop()

    def branch_na(self, l, path):
        T_, SC, lat, NB = path.T, path.SC, path.lat, path.NB
        TK = T_ + SC
        self.push()
        self.attn_tiles()
        va_e = self.T("nva_e", [128, NB, 4, 65], BF16)
        self.memset(va_e[:, :, :, 64:65], 1.0, ['va_e'])
        if lat:
            va_o = self.T("nva_o", [128, NB - 1, 4, 65], BF16)
            va_c = self.T("nva_c", [128, 4, 4, 65], BF16)
            self.memset(va_o[:, :, :, 64:65], 1.0, ['va_o'])
            self.memset(va_c[:, :, :, 64:65], 1.0, ['va_c'])
        self.push()
        self.more_wp(2)
        vfull = self.T("vfull", [128, 2, T_], F32)
        self.stage = [self.T(f"stage{i}", [128, 128], F32) for i in range(2)]
        self.stc = 0

        def evac_v(tag, j, ps, pk):
            c = int(tag[1])
            self.acopy(vfull[:, c, j * 512:(j + 1) * 512], ps, pk, [('vfull', c)])

        self.project_group(path, [('v0', [self.win(l, 'n_v', 0, 128)]), ('v1', [self.win(l, 'n_v', 128, 128)])], evac_v)
        for blk in range(NB):
            for c in range(2):
                pt, ptk = self.psq(128)
                self.tr(pt, vfull[:, c, blk * 128:(blk + 1) * 128], self.cc('ident'), [('vfull', c), 'cst'], ptk)
                self.acopy(va_e[:, blk, 2 * c:2 * c + 2, 0:64], pt.rearrange("p (a b) -> p a b", b=64), ptk, ['va_e'])
                if not lat:
                    i = self.stc % 2
                    self.stc += 1
                    self.vcopy(self.stage[i][:], pt, ptk, [('stage', i)])
                    seq, t0 = blk // 2, (blk % 2) * 128
                    self.dma('sp', self.o_nv[seq, l, t0:t0 + 128, c * 128:(c + 1) * 128], self.stage[i][:], [('stage', i)], [])
        if lat:
            for s in range(NB - 1):
                for c in range(2):
                    pt, ptk = self.psq(128)
                    self.tr(pt, vfull[:, c, 64 + s * 128:192 + s * 128], self.cc('ident'), [('vfull', c), 'cst'], ptk)
                    self.acopy(va_o[:, s, 2 * c:2 * c + 2, 0:64], pt.rearrange("p (a b) -> p a b", b=64), ptk, ['va_o'])
            for st in range(4):
                self.dma('pool', va_c[:, st, :, 0:64],
                         self.c_nv[l][st * 128:(st + 1) * 128, :].rearrange("p (a b) -> p a b", b=64), [], ['va_c'])
        self.pop()
        self.more_wp(4)
        qT = self.T("nq", [128, 2, T_], BF16)
        kT = self.T("nk", [128, 2, TK], BF16)
        if not lat:
            kf = self.T("nkf", [128, 2, T_], F32)
            self.stage = [self.T(f"stage{i}", [128, 128], F32) for i in range(2)]

        def evac(tag, j, ps, pk):
            sl = slice(j * 512, (j + 1) * 512)
            c = int(tag[1])
            if tag[0] == 'q':
                self.act(qT[:, c, sl], ps, AF.Identity, pk, [('nq', c)], scale=0.125)
            else:
                if lat:
                    self.acopy(kT[:, c, sl], ps, pk, [('nk', c)])
                else:
                    self.acopy(kf[:, c, sl], ps, pk, [('nkf', c)])
                    self.vcopy(kT[:, c, sl], kf[:, c, sl], [('nkf', c)], [('nk', c)])

        self.project_group(path, [('q0', [self.win(l, 'n_q', 0, 128)]), ('q1', [self.win(l, 'n_q', 128, 128)]),
                                  ('k0', [self.win(l, 'n_k', 0, 128)]), ('k1', [self.win(l, 'n_k', 128, 128)])], evac)
        if lat:
            cst_ = [self.T(f"cst_{i}", [128, 256], F32) for i in range(2)]
            for st in range(4):
                i = st % 2
                self.dma('sp', cst_[i][:], self.c_nk[l][st * 128:(st + 1) * 128, :], [], [('cst_', i)])
                for c in range(2):
                    ps, pk = self.psq(128)
                    self.tr(ps, cst_[i][:, c * 128:(c + 1) * 128], self.cc('ident'), [('cst_', i), 'cst'], pk)
                    self.acopy(kT[:, c, T_ + st * 128:T_ + (st + 1) * 128], ps, pk, [('nk', c)])
        else:
            for c in range(2):
                self.out_tok(lambda blk, c=c: kf[:, c, blk * 128:(blk + 1) * 128], 128, 128, path, l,
                             lambda seq, t0, c=c: self.o_nk[seq, l, t0:t0 + 128, c * 128:(c + 1) * 128], ('nkf', c))

        if not lat:
            for h in range(4):
                c = h // 2
                rows = slice((h % 2) * 64, (h % 2) * 64 + 64)
                pieces = [(lambda q0, n, c=c, rows=rows: qT[rows, c, q0:q0 + n],
                           lambda st, c=c, rows=rows: kT[rows, c, st * 128:(st + 1) * 128], [('nq', c), ('nk', c)])]
                self.attn_head(path, pieces, lambda st, h=h: va_e[:, st, h, :], 'va_e',
                               lambda q0, n, h=h, rows=rows: (self.yT[rows, 6 + h // 2, q0:q0 + n], h % 2 == 1),
                               ('yT', 6 + h // 2), 1.0)
        else:
            BT = self.T("nBT", [128, 8, 4, 64], BF16)
            for h in range(4):
                c = h // 2
                rows = slice((h % 2) * 64, (h % 2) * 64 + 64)
                self.dma('pool', BT[:].rearrange("p d s q -> p (d s q)"), self.nab_d[l][h], [], ['nBT'])
                inflight = {}

                def emit_scores(r, h=h, c=c, rows=rows):
                    r0 = min(max(r - 4, 0), 24)
                    dr = r - r0
                    k0 = r0 * 64
                    qs = qT[rows, c, r * 64:(r + 1) * 64]
                    ps, pk = self.psq(512)
                    for i in range(4):
                        self.mm(ps[:, i * 64:(i + 1) * 64], kT[rows, c, k0 + i * 128:k0 + (i + 1) * 128], qs, True, False,
                                [('nq', c), ('nk', c)], pk)
                        self.mm(ps[:, i * 64:(i + 1) * 64], self.identb[:], BT[:, dr, i, :], False, True, ['identb', 'nBT'], pk)
                    for i in range(4):
                        self.mm(ps[:, (4 + i) * 64:(5 + i) * 64], kT[rows, c, T_ + i * 128:T_ + (i + 1) * 128], qs, True, True,
                                [('nq', c), ('nk', c)], pk)
                    inflight[r] = (ps, pk)

                NLA = 2
                for r_ in range(NLA):
                    emit_scores(r_)
                for r in range(32):
                    if r + NLA < 32:
                        emit_scores(r + NLA)
                    r0 = min(max(r - 4, 0), 24)
                    ps, pk = inflight.pop(r)
                    pi = self.pcnt % 3
                    self.pcnt += 1
                    self.act(self.pT[pi][:], ps, AF.Exp, pk, [('pT', pi)])
                    po, pok = self.psr(6 + self.ocnt % 2, 64, parts=65)
                    self.ocnt += 1
                    for i in range(8):
                        if i < 4:
                            if r0 % 2 == 0:
                                v, vk = va_e[:, r0 // 2 + i, h, :], 'va_e'
                            else:
                                v, vk = va_o[:, (r0 - 1) // 2 + i, h, :], 'va_o'
                        else:
                            v, vk = va_c[:, i - 4, h, :], 'va_c'
                        self.mm(po, v, self.pT[pi][:, i * 64:(i + 1) * 64], i == 0, i == 7, [('pT', pi), vk], pok)
                    self.attn_norm(po, pok, 64, (self.yT[rows, 6 + h // 2, r * 64:(r + 1) * 64], h % 2 == 1), ('yT', 6 + h // 2))
        self.pop()

    def branch_rwkv(self, l, path):
        T_, lat, NB, nseq = path.T, path.lat, path.NB, path.nseq
        NBS = NB // nseq
        self.S.barrier()
        self.push()
        self.more_wp(3)
        bigs = [self.sq[0], self.sq[1], self.rs[0], self.rs[1], self.tmpf[0], self.tmpf[1]]
        rw = [t[:, q * 128:(q + 1) * 128] for t in bigs for q in range(4)]
        RK = lambda n: ('rw', n)
        csr = self.T("csr", [128, 512], F32)
        csrb = self.T("csrb", [128, NCSR - 512], BF16)
        self.dma('sp', csr[:], self.csr_d[:, 0:512], [], ['csr'])
        self.dma('pool', csrb[:], self.csr_d[:, 512:NCSR], [], ['csr'])

        def cr(name, n=128):
            o = CSR[name]
            return csr[:, o:o + n] if o < 512 else csrb[:, o - 512:o - 512 + n]

        w2t = self.T("w2t", [128, 256], BF16)
        a2t = self.T("a2t", [128, 256], BF16)
        g2t = self.T("g2t", [128, 256], BF16)
        self.dma('pool', w2t[:], self.w2_d[l], [], ['w2t'])
        self.dma('pool', a2t[:], self.a2_d[l], [], ['a2t'])
        self.dma('pool', g2t[:], self.g2_d[l], [], ['g2t'])
        tanhT = self.T("tanhT", [128, T_], BF16)
        aloT = self.T("aloT", [128, T_], BF16)
        sgT = self.T("sgT", [128, T_], BF16)
        psbf = self.psbf

        def evac0(tag, j, ps, pk):
            sl = slice(j * 512, (j + 1) * 512)
            if tag == 'wlo':
                self.act(tanhT[:, sl], ps, AF.Tanh, pk, ['tanhT'])
            elif tag == 'alo':
                self.acopy(aloT[:, sl], ps, pk, ['aloT'])
            else:
                self.act(sgT[:, sl], ps, AF.Sigmoid, pk, ['sgT'])

        self.project_group(path, [('wlo', [self.win(l, 'a_wlo', 0, 128)]), ('alo', [self.win(l, 'a_alo', 0, 128)]),
                                  ('glo', [self.win(l, 'a_glo', 0, 128)])], evac0)
        rT = self.T("rT", [128, T_], BF16)
        kT = self.T("kT", [128, T_], F32)
        vT = self.T("vT", [128, T_], BF16)
        yacc = self.T("yacc", [128, NB, 128], F32)
        bonv = self.T("bonv", [128, T_], BF16)
        nch = nseq * 2
        ST32 = self.T("ST32", [128, nch, 64], F32)
        STb = self.T("STb", [128, nch, 64], BF16)
        rows = self.T("rows", [1, 256], F32)
        ones1 = self.cst[0:1, CST['ones']:CST['ones'] + 128]
        carve = {'c': 2, 'o': 0}

        def balloc(name, cols):
            if carve['c'] < 8 and carve['o'] + cols > T_:
                carve['c'] += 1
                carve['o'] = 0
            if carve['c'] < 8:
                ap = self.yT[:, carve['c'], carve['o']:carve['o'] + cols]
                carve['o'] += cols
                return ap
            assert not lat
            return self.T(name, [128, cols], BF16)[:]

        NSLOT = 2 if lat else 4
        rwx = [self.T(f"rwx{i}", [128, 128], F32)[:] for i in range(22)] if not lat else []
        slots = []
        for s in range(NSLOT):
            sl_ = dict(idx=s)
            for nm, cols in (('AR', 256), ('Bh', 128), ('Kh', 128), ('BKt', 256), ('R01', 256), ('TM', 512), ('App', 128),
                             ('UT', 128), ('S0k', 128)):
                sl_[nm] = balloc(f"{nm}{s}", cols)
            for hh in range(2):
                for nm, cols in (('NP', 256), ('MP', 256), ('NpT0', 128), ('NpT1', 128), ('Npw0', 128), ('Npw1', 128),
                                 ('Tac0', 128), ('Tac1', 128), ('Xt', 64)):
                    sl_[(nm, hh)] = balloc(f"{nm}{s}{hh}", cols)
            sl_['tmpS'] = self.T(f"tmpS{s}", [128, 64], F32)
            sl_['lwc'] = self.T(f"lwc{s}", [128, 2], F32)
            if s < 2:
                sl_['rw'] = rw[s * 11:(s + 1) * 11]
            else:
                sl_['rw'] = rwx[(s - 2) * 11:(s - 1) * 11]
            sl_['rwk'] = [RK(s * 11 + i if s < 2 else 100 + s * 11 + i) for i in range(11)]
            slots.append(sl_)

        def visit_gen(S_, c2, seq, d, blk):
            s = S_['idx']
            K_ = lambda nm: (nm, s)
            tok = slice(blk * 128, (blk + 1) * 128)
            ch = seq * 2 + d
            cN = slice(c2 * 128, (c2 + 1) * 128)
            R, RKs = S_['rw'], S_['rwk']
            kkraw, sqk, rn, an, tmp, b_ = R[0], R[1], R[2], R[3], R[4], R[5]
            kkraw_k, sqk_k, rn_k, an_k, tmp_k, b_k = RKs[0], RKs[1], RKs[2], RKs[3], RKs[4], RKs[5]
            alr, alr_k = [R[6], R[7]], [RKs[6], RKs[7]]
            kd, kd_k = [R[8], R[9]], [RKs[8], RKs[9]]
            eWp, eWp_k = R[10], RKs[10]
            sg, sg_k = R[0], RKs[0]
            rk, rk_k = R[1], RKs[1]
            ksum, ksum_k = R[2], RKs[2]
            eWt, eWt_k = R[4], RKs[4]
            eW, eW_k = R[6], RKs[6]
            eWi, eWi_k = R[7], RKs[7]
            AR, Bh, Kh, BKt, R01, TM, App, UT, S0k = (S_[n] for n in ('AR', 'Bh', 'Kh', 'BKt', 'R01', 'TM', 'App', 'UT', 'S0k'))
            tmpS, lwc = S_['tmpS'], S_['lwc']
            self.ts(kkraw, kT[:, tok], self.vv(l, 'k_k', c2), None, ALU.mult, None, ['kT', 'vecs'], [kkraw_k])
            self.act(sqk, kkraw, AF.Square, [kkraw_k], [sqk_k])
            ps, pk = self.psq(128)
            self.mm(ps, self.cc('bones'), sqk, True, True, [sqk_k, 'cst'], pk)
            self.act(rn, ps, AF.Sqrt, pk + ['cst'], [rn_k], bias=self.cc('eps_kk', 1), scale=1.0)
            self.recip(rn, rn, [rn_k], [rn_k])
            self.stt(an, kkraw, -1.0, rn, ALU.mult, ALU.mult, [kkraw_k, rn_k], [an_k])
            yield
            for dd in ([0, 1] if d == 0 else [1]):
                ps, pk = self.psq(128)
                self.mm(ps, a2t[64 * dd:64 * dd + 64, cN], aloT[64 * dd:64 * dd + 64, tok], True, True, ['a2t', 'aloT'], pk)
                self.act(alr[dd], ps, AF.Sigmoid, pk + ['vecs'], [alr_k[dd]], bias=self.vv(l, 'a0', dd * 2 + c2))
                self.ts(tmp, alr[dd], -1.0, self.vv(l, 'k_a', c2), ALU.add, ALU.mult, [alr_k[dd], 'vecs'], [tmp_k])
                self.stt(kd[dd], tmp, 1.0, kT[:, tok], ALU.add, ALU.mult, [tmp_k, 'kT'], [kd_k[dd]])
            self.stt(b_, an, -1.0, alr[d], ALU.mult, ALU.mult, [an_k, alr_k[d]], [b_k])
            yield
            if d == 0:
                self.ts(rk, rT[:, tok], self.vv(l, 'r_k', c2), None, ALU.mult, None, ['rT', 'vecs'], [rk_k])
                self.tt(ksum, kd[0], kd[1], ALU.add, [kd_k[0], kd_k[1]], [ksum_k])
                self.tt(rk, rk, ksum, ALU.mult, [rk_k, ksum_k], [rk_k])
                ps, pk = self.psq(128)
                self.mm(ps, self.cc('bones'), rk, True, True, [rk_k, 'cst'], pk)
                self.stt(bonv[:, tok], ps, 0.5, vT[:, tok], ALU.mult, ALU.mult, pk + ['vT'], ['bonv'])
                yield
            ps, pk = self.psq(128)
            self.mm(ps, tanhT[64 * d:64 * d + 64, tok], w2t[64 * d:64 * d + 64, cN], True, False, ['tanhT', 'w2t'], pk)
            self.mm(ps, ones1, rows[0:1, d * 128:(d + 1) * 128], False, True, ['rows', 'cst'], pk)
            self.act(sg, ps, AF.Sigmoid, pk, [sg_k])
            plw, plwk = self.psq(128)
            self.mm(plw, sg, cr(f'TI{d}'), True, True, [sg_k, 'csr'], plwk)
            plp, plpk = self.psq(128)
            self.mm(plp, sg, cr(f'TS{d}'), True, True, [sg_k, 'csr'], plpk)
            lcs = [63, 127] if d == 0 else [0, 64]
            self.act(eW, plw, AF.Exp, plwk, [eW_k])
            self.act(eWi, plw, AF.Exp, plwk, [eWi_k], scale=-1.0)
            for cc in range(2):
                self.vcopy(lwc[:, cc:cc + 1], plw[:, lcs[cc]:lcs[cc] + 1], plwk, [K_('lwc')])
            for cc in range(2):
                self.act(eWt[:, cc * 64:(cc + 1) * 64], plw[:, cc * 64:(cc + 1) * 64], AF.Exp, plwk + [K_('lwc')], [eWt_k],
                         bias=lwc[:, cc:cc + 1], scale=-1.0)
            self.act(eWp, plp, AF.Exp, plpk, [eWp_k])
            yield
            self.tt(AR[:, 0:128], an, eWp, ALU.mult, [an_k, eWp_k], [K_('AR')])
            self.tt(AR[:, 128:256], rT[:, tok], eW, ALU.mult, ['rT', eW_k], [K_('AR')])
            self.tt(Bh, b_, eWi, ALU.mult, [b_k, eWi_k], [K_('Bh')])
            self.tt(Kh, kd[d], eWi, ALU.mult, [kd_k[d], eWi_k], [K_('Kh')])
            self.tt(BKt[:, 0:128], b_, eWt, ALU.mult, [b_k, eWt_k], [K_('BKt')])
            self.tt(BKt[:, 128:256], kd[d], eWt, ALU.mult, [kd_k[d], eWt_k], [K_('BKt')])
            self.tt(R01[:, 0:128], AR[:, 128:256], cr('cm0'), ALU.mult, [K_('AR'), 'csr'], [K_('R01')])
            self.tt(R01[:, 128:256], AR[:, 128:256], cr('cm1'), ALU.mult, [K_('AR'), 'csr'], [K_('R01')])
            yield
            pbk = [('ps', 5)]
            for i, (src, k) in enumerate(((AR[:, 0:128], K_('AR')), (BKt[:, 0:128], K_('BKt')), (BKt[:, 128:256], K_('BKt')),
                                          (vT[:, tok], 'vT'))):
                self.tr(psbf[:, i * 128:(i + 1) * 128], src, self.identb[:], [k, 'identb'], pbk)
            self.acopy(TM, psbf[:, 0:512], pbk, [K_('TM')])
            yield
            WCs = [eW[:, lcs[cc]:lcs[cc] + 1] for cc in range(2)]
            msi = cr(f'MS{d}', 256)
            mst = cr(f'MS{1 - d}')
            py, pyk = self.psr(6 + s // 2, 128, off=(s % 2) * 128)

            def head_gen(hh):
                fr = slice(64 * hh, 64 * hh + 64)
                hc = slice(64 * hh, 64 * hh + 64)
                sk = ('ST', ch, hh)
                H_ = lambda nm: (nm, s, hh)
                NP, MP, Xt = S_[('NP', hh)], S_[('MP', hh)], S_[('Xt', hh)]
                NpT = [S_[('NpT0', hh)], S_[('NpT1', hh)]]
                Npw = [S_[('Npw0', hh)], S_[('Npw1', hh)]]
                Tac = [S_[('Tac0', hh)], S_[('Tac1', hh)]]
                pa, pak = self.psq(256)
                self.mm(pa, Bh[fr, :], AR[fr, :], True, True, [K_('Bh'), K_('AR')], pak)
                self.tt(NP, pa, msi, ALU.mult, pak + ['csr'], [H_('NP')])
                yield
                pb2, pb2k = self.psq(256)
                self.mm(pb2, Kh[fr, :], AR[fr, :], True, True, [K_('Kh'), K_('AR')], pb2k)
                self.tt(MP, pb2, msi, ALU.mult, pb2k + ['csr'], [H_('MP')])
                yield
                pc, pck = self.psq(128)
                self.mm(pc, AR[fr, 0:128], Bh[fr, :], True, True, [K_('AR'), K_('Bh')], pck)
                self.tt(NpT[0], pc, mst, ALU.mult, pck + ['csr'], [H_('NpT0')])
                self.tt(Tac[0], NP[:, 0:128], self.identb[:], ALU.add, [H_('NP'), 'identb'], [H_('Tac0')])
                yield
                Ncur, Nk = NP[:, 0:128], H_('NP')
                ti = 0
                ni = 0
                for p in (2, 4, 8, 16, 32):
                    pT_, pTk = self.psq(128)
                    self.mm(pT_, Ncur, NpT[ni], True, True, [Nk, H_(f'NpT{ni}')], pTk)
                    self.acopy(NpT[1 - ni], pT_, pTk, [H_(f'NpT{1 - ni}')])
                    if p < 32:
                        pN, pNk = self.psq(128)
                        self.mm(pN, NpT[ni], Ncur, True, True, [Nk, H_(f'NpT{ni}')], pNk)
                        wi_ = (p.bit_length()) % 2
                        self.vcopy(Npw[wi_], pN, pNk, [H_(f'Npw{wi_}')])
                    yield
                    pU, pUk = self.psq(128)
                    self.mm(pU, NpT[1 - ni], Tac[ti], True, True, [H_(f'NpT{1 - ni}'), H_(f'Tac{ti}')], pUk)
                    self.tt(Tac[1 - ti], pU, Tac[ti], ALU.add, pUk + [H_(f'Tac{ti}')], [H_(f'Tac{1 - ti}')])
                    ti = 1 - ti
                    ni = 1 - ni
                    if p < 32:
                        Ncur, Nk = Npw[wi_], H_(f'Npw{wi_}')
                    yield
                Tinv, Tk = Tac[ti], H_(f'Tac{ti}')
                px, pxk = self.psq(64)
                self.mm(px, MP[:, 0:128], TM[:, 384 + 64 * hh:384 + 64 * hh + 64], True, True, [H_('MP'), K_('TM')], pxk)
                self.vcopy(Xt, px, pxk, [H_('Xt')])
                pp, ppk = self.psq(128, parts=64)
                self.mm(pp, TM[:, 64 * hh:64 * hh + 64], Tinv, True, True, [K_('TM'), Tk], ppk)
                self.acopy(App[fr, :], pp, ppk, [H_('App')])
                yield
                for ci, cc in enumerate((0, 1) if d == 0 else (1, 0)):
                    tr_ = slice(64 * cc, 64 * cc + 64)
                    self.vcopy(S0k[fr, cc * 64:(cc + 1) * 64], STb[fr, ch, :], [sk], [H_('S0k')])
                    pu, puk = self.psq(64, parts=64)
                    self.mm(pu, Tinv[:, tr_], Xt, True, False, [Tk, H_('Xt')], puk)
                    self.mm(pu, App[fr, tr_], STb[fr, ch, :], False, True, [H_('App'), sk], puk)
                    self.acopy(UT[tr_, hc], pu, puk, [H_('UT')])
                    yield
                    pS, pSk = self.psq(64, parts=64)
                    self.mm(pS, TM[tr_, 128 + 64 * hh:128 + 64 * hh + 64], UT[tr_, hc], True, False, [K_('TM'), H_('UT')], pSk)
                    self.mm(pS, TM[tr_, 256 + 64 * hh:256 + 64 * hh + 64], TM[tr_, 384 + 64 * hh:384 + 64 * hh + 64], False, True,
                            [K_('TM')], pSk)
                    if hh == 0:
                        self.stt(ST32[fr, ch, :], ST32[fr, ch, :], WCs[cc][fr, :], pS, ALU.mult, ALU.add, [sk, eW_k] + pSk, [sk])
                    else:
                        self.acopy(tmpS[fr, :], pS, pSk, [K_('tmpS')])
                        self.stt(ST32[fr, ch, :], ST32[fr, ch, :], WCs[cc][fr, :], tmpS[fr, :], ALU.mult, ALU.add,
                                 [sk, eW_k, K_('tmpS')], [sk])
                    self.acopy(STb[fr, ch, :], ST32[fr, ch, :], [sk], [sk])
                    yield
                self.mm(py[:, hc], R01[fr, 0:128], S0k[fr, 0:64], True, False, [K_('R01'), H_('S0k')], pyk)
                self.mm(py[:, hc], R01[fr, 128:256], S0k[fr, 64:128], False, False, [K_('R01'), H_('S0k')], pyk)
                self.mm(py[:, hc], NP[:, 128:256], UT[:, hc], False, False, [H_('NP'), H_('UT')], pyk)
                self.mm(py[:, hc], MP[:, 128:256], TM[:, 384 + 64 * hh:384 + 64 * hh + 64], False, True, [H_('MP'), K_('TM')], pyk)

            hg = [head_gen(0), head_gen(1)]
            while hg:
                for g in list(hg):
                    try:
                        next(g)
                    except StopIteration:
                        hg.remove(g)
                yield
            self.tt(yacc[:, blk, :], yacc[:, blk, :], py, ALU.add, [('yacc', blk)] + pyk, [('yacc', blk)])

        def stream_gen(S_, c2, seq, d):
            for i in range(NBS):
                blk = seq * NBS + (i if d == 0 else NBS - 1 - i)
                yield from visit_gen(S_, c2, seq, d, blk)

        for c2 in range(2):
            cN = slice(c2 * 128, (c2 + 1) * 128)

            def evac1(tag, j, ps, pk):
                sl = slice(j * 512, (j + 1) * 512)
                dst, key = {'r': (rT, 'rT'), 'k': (kT, 'kT'), 'v': (vT, 'vT')}[tag]
                self.acopy(dst[:, sl], ps, pk, [key])

            self.project_group(path, [('r', [self.win(l, 'a_r', c2 * 128, 128)]), ('k', [self.win(l, 'a_k', c2 * 128, 128)]),
                                      ('v', [self.win(l, 'a_v', c2 * 128, 128)])], evac1)
            for d in range(2):
                self.dma('sp', rows[0:1, d * 128:(d + 1) * 128], self.rows_d[0:1, l, d * 256 + c2 * 128:d * 256 + (c2 + 1) * 128], [], ['rows'])
            for blk in range(NB):
                self.memset(yacc[:, blk, :], 0.0, [('yacc', blk)])
            stin = rw[22][0:64, :]
            for seq in range(nseq):
                for d in range(2):
                    ch = seq * 2 + d
                    keys = [('ST', ch, 0), ('ST', ch, 1)]
                    if lat:
                        self.dma('sp', stin.rearrange("v (h k) -> v h k", h=2),
                                 self.st_in[l, d, 2 * c2:2 * c2 + 2].rearrange("h v k -> v h k"), [], [RK(22)])
                        ps, pk = self.psq(64)
                        self.tr(ps, stin, self.cst[0:64, CST['ident']:CST['ident'] + 64], [RK(22), 'cst'], pk)
                        self.acopy(ST32[:, ch, :], ps, pk, keys)
                        self.vcopy(STb[:, ch, :], ST32[:, ch, :], keys, keys)
                    else:
                        self.memset(ST32[:, ch, :], 0.0, keys)
                        self.memset(STb[:, ch, :], 0.0, keys)
                if NSLOT == 4 and seq % 2 == 0:
                    pending = [stream_gen(slots[0], c2, seq, 0), stream_gen(slots[1], c2, seq, 1)]
                    continue
                if NSLOT == 4:
                    gens = pending + [stream_gen(slots[2], c2, seq, 0), stream_gen(slots[3], c2, seq, 1)]
                else:
                    gens = [stream_gen(slots[0], c2, seq, 0), stream_gen(slots[1], c2, seq, 1)]
                while gens:
                    for g in list(gens):
                        try:
                            next(g)
                        except StopIteration:
                            gens.remove(g)
                if not lat:
                    for sq_ in ([seq - 1, seq] if NSLOT == 4 else [seq]):
                        for d in range(2):
                            ch = sq_ * 2 + d
                            keys = [('ST', ch, 0), ('ST', ch, 1)]
                            ps, pk = self.psq(128, parts=64)
                            self.tr(ps, ST32[:, ch, :], self.cc('ident'), keys + ['cst'], pk)
                            self.acopy(stin, ps, pk, [RK(22)])
                            self.dma('sp', self.st_out[sq_, l, d, 2 * c2:2 * c2 + 2].rearrange("h v k -> v h k"),
                                     stin.rearrange("v (h k) -> v h k", h=2), [RK(22)], [])
            for blk in range(NB):
                tok = slice(blk * 128, (blk + 1) * 128)
                y = yacc[:, blk, :]
                i4 = (blk % 2) * 4
                ysq, yn, r19, r23 = rw[i4 + 0], rw[i4 + 1], rw[i4 + 2], rw[i4 + 3]
                k17, k18, k19, k23 = RK(i4 + 0), RK(i4 + 1), RK(i4 + 2), RK(i4 + 3)
                pt, ptk = self.psq(128)
                self.tr(pt, y, self.cc('ident'), [('yacc', blk), 'cst'], ptk)
                self.acopy(ysq, pt, ptk, [k17])
                pm, pmk = self.psq(128)
                self.mm(pm, self.cc('bones'), ysq, True, True, [k17, 'cst'], pmk)
                self.stt(yn, pm, -1.0 / 64, ysq, ALU.mult, ALU.add, pmk + [k17], [k18])
                self.act(ysq, yn, AF.Square, [k18], [k17])
                pv, pvk = self.psq(128)
                self.mm(pv, self.cc('bones'), ysq, True, True, [k17, 'cst'], pvk)
                self.act(r23, pv, AF.Sqrt, pvk + ['cst'], [k23], bias=self.cc('eps_gn', 1), scale=1.0 / 64)
                self.recip(r23, r23, [k23], [k23])
                self.tt(yn, yn, r23, ALU.mult, [k18, k23], [k18])
                self.act(r19, yn, AF.Identity, [k18, 'vecs'], [k19], bias=self.vv(l, 'ln_b', c2), scale=self.vv(l, 'ln_g', c2))
                pg, pgk = self.psq(128)
                self.mm(pg, g2t[:, cN], sgT[:, tok], True, True, ['g2t', 'sgT'], pgk)
                self.tt(r19, r19, bonv[:, tok], ALU.add, [k19, 'bonv'], [k19])
                self.tt(self.yT[:, c2, tok], r19, pg, ALU.mult, [k19] + pgk, [('yT', c2)])
        self.pop()


_CACHE = {}


def _get_nc(cfg_key, cfg):
    if cfg_key not in _CACHE:
        nc = bass.Bass("TRN2", target_bir_lowering=False)
        p = Prog(nc, cfg)
        p.build()
        _CACHE[cfg_key] = (nc, p)
    return _CACHE[cfg_key]


def _host_inputs(inp):
    f = lambda a: np.ascontiguousarray(np.asarray(a, dtype=np.float32))
    g = {k: f(v) for k, v in inp.items()}
    shared = {}
    vec = np.zeros((L, 128, NVEC), np.float32)

    def put(name, arr):
        vec[:, :, VEC[name]:VEC[name] + arr.shape[2]] = arr
    put('b_mod', _fm(g['b_mod']))
    put('g_mpre', _fm(g['norm_mix_pre']))
    put('g_mpost', _fm(g['norm_mix_post']))
    put('g_fpre', _fm(g['norm_ffn_pre']))
    put('g_fpost', _fm(g['norm_ffn_post']))
    put('a0', _fm(g['rwkv_a0'].reshape(L, 512)))
    put('k_k', _fm(g['rwkv_k_k']))
    put('k_a', _fm(g['rwkv_k_a']))
    put('r_k', _fm(g['rwkv_r_k'].reshape(L, 256)))
    put('qn', _fm(g['mla_q_norm']))
    put('kvn', _fm(g['mla_kv_norm']))
    put('gq', _fm(np.tile(g['gqa_q_norm'], (1, 2))))
    put('gq_sw', _fm(np.tile(_swap_halves(g['gqa_q_norm'], 64), (1, 2))))
    put('gk', _fm(np.tile(g['gqa_k_norm'], (1, 2))))
    put('gk_sw', _fm(np.tile(_swap_halves(g['gqa_k_norm'], 64), (1, 2))))
    put('ln_g', _fm(g['rwkv_ln_g']))
    put('ln_b', _fm(g['rwkv_ln_b']))
    shared['vecs'] = np.ascontiguousarray(vec.transpose(1, 0, 2))
    shared['cst'], shared['csr'] = _consts()
    for k in ('w_in', 'mla_w_uq', 'mla_w_ukv'):
        shared[k] = g[k]
    ca = np.ascontiguousarray
    shared['w_mod_t'] = ca(g['w_mod'].reshape(L, 8, 128, 12, 512).transpose(0, 3, 2, 1, 4)).reshape(L, 12, 128, 8 * 512)
    gu = np.concatenate([g['ffn_w_gate'].reshape(L, 8, 128, NF // 2, 256), g['ffn_w_up'].reshape(L, 8, 128, NF // 2, 256)], axis=-1)
    shared['ffn_gu_t'] = ca(gu.transpose(0, 3, 2, 1, 4)).reshape(L, NF // 2, 128, 8 * 512)
    shared['ffn_d_t'] = ca(g['ffn_w_down'].reshape(L, NF, 128, 4, 256).transpose(0, 3, 2, 1, 4)).reshape(L, 4, 128, NF * 256)
    gt = g['w_in'][:, :, IN_OFF['gate']:].reshape(L, 8, 128, 4, 8, 128)
    shared['gate_t'] = ca(gt.transpose(0, 4, 2, 3, 1, 5)).reshape(L, 8, 128, 4 * 8 * 128)
    wb = g['w_branch'].reshape(L, 4, 2, 128, 8, 128)
    shared['wbr_t'] = ca(wb.transpose(0, 4, 3, 1, 2, 5)).reshape(L, 8, 128, 8 * 128)
    wo = g['w_out'].reshape(L, 8, 128, 8, 128)
    shared['wout_t'] = ca(wo.transpose(0, 3, 2, 1, 4)).reshape(L, 8, 128, 8 * 128)
    wi = g['w_in']
    o = IN_OFF
    shared['w_in_sw'] = np.concatenate([
        _swap_halves(wi[:, :, o['g_q']:o['g_q'] + 256], 64),
        _swap_halves(wi[:, :, o['g_k']:o['g_k'] + 128], 64),
        _swap_halves(wi[:, :, o['m_kpe']:o['m_kpe'] + 32], 32)], axis=2)
    uq = g['mla_w_uq'].reshape(L, 256, 4, 96)
    shared['mla_w_uq_sw'] = np.ascontiguousarray(_swap_halves(uq[:, :, :, 64:96], 32).reshape(L, 256, 128))
    rows = np.concatenate([g['rwkv_w0'].reshape(L, 512), g['rwkv_ln_g'], g['rwkv_ln_b']], axis=1)
    shared['rows'] = np.ascontiguousarray(np.broadcast_to(rows[None], (128, L, 1024)))
    shared['rwkv_w2'] = g['rwkv_w2'].reshape(L, 128, 256)
    shared['rwkv_a2'] = g['rwkv_a2'].reshape(L, 128, 256)
    shared['rwkv_g2'] = g['rwkv_g2']
    rope = np.zeros((128, 4, 2048), np.float32)
    C, S_ = _rope_tables(2048, 64)
    rope[:, 0] = np.concatenate([C, C], axis=0)
    rope[:, 1] = np.concatenate([S_, S_], axis=0)
    C, S_ = _rope_tables(2048, 32)
    rope[0:32, 2] = C
    rope[0:32, 3] = S_
    shared['rope'] = rope
    shared['nab'] = _na_bias_table(g['na_rel_bias']).reshape(L, 4, 128, -1)
    in_maps = []
    for i in range(8):
        b = i % 2
        m = dict(shared)
        m['xc'] = np.ascontiguousarray(g['x_prompt'][4 * i:4 * i + 4].reshape(1024, D))
        m['xl'] = g['x_sample'][b]
        cv = np.stack([g['c_ctx'], g['c'][b]], axis=-1)
        m['cvec'] = np.ascontiguousarray(cv.reshape(8, 128, 2).transpose(1, 0, 2))
        m['st_in'] = g['state_rwkv'][b]
        m['c_ckv'] = g['cache_mla_ckv'][b]
        m['c_kpe'] = g['cache_mla_kpe'][b]
        m['c_gk'] = g['cache_gqa_k'][b].reshape(L, 512, 128)
        m['c_gv'] = g['cache_gqa_v'][b].reshape(L, 512, 128)
        m['c_nk'] = g['cache_na_k'][b].reshape(L, 512, 256)
        m['c_nv'] = g['cache_na_v'][b].reshape(L, 512, 256)
        in_maps.append(m)
    return in_maps


def run(inputs, cfg=None, cores=8):
    cfg = dict(cfg or {})
    key = tuple(sorted(cfg.items()))
    nc, prog = _get_nc(key, cfg)
    in_maps = _host_inputs(inputs)[:cores]
    in_maps = [{k: v for k, v in m.items() if k in prog.io} for m in in_maps]
    res = run_bass_kernel_spmd(nc, in_maps, core_ids=list(range(cores)))
    return res.results


def kernel(**inputs):
    r = run(inputs)
    y_prompt = np.concatenate([r[i]['yc'].reshape(4, 256, D) for i in range(8)], axis=0)
    y_sample = np.stack([r[0]['yl'], r[1]['yl']], axis=0)
    cat = lambda k, shp: np.concatenate([r[i][k].reshape((4,) + shp) for i in range(8)], axis=0)
    return (y_prompt, y_sample,
            cat('st_out', (L, 2, 4, 64, 64)),
            cat('o_ckv', (L, 256, 128)),
            cat('o_kpe', (L, 256, 32)),
            cat('o_gk', (L, 256, 2, 64)),
            cat('o_gv', (L, 256, 2, 64)),
            cat('o_nk', (L, 256, 4, 64)),
            cat('o_nv', (L, 256, 4, 64)))
```
